# Optimizing a Trainium2 kernel written in Bass

```python
import math
import jax, jax.numpy as jnp
from jax import lax
import numpy as np

D_MODEL = 1024
BATCH = 16
SEQ = 2048
DEPTH = 1
DEC_BATCH = 8
DEC_SEQ = 4096
PAST_LEN = 128

MIX_WIDTH = D_MODEL
ATT_WIDTH = MIX_WIDTH // 2
LRU_WIDTH = MIX_WIDTH - ATT_WIDTH
N_ATT_HEADS = 4
HEAD_DV = ATT_WIDTH // N_ATT_HEADS
HEAD_DK = HEAD_DV // 2
QK_WIDTH = N_ATT_HEADS * 2 * HEAD_DK
N_LRU_BLOCKS = 8
LRU_BLOCK = LRU_WIDTH // N_LRU_BLOCKS
CONV_W = 4
CONV_PAD_L = 2
RG_C = 8.0
D_FF = ((8 * D_MODEL // 3 + 255) // 256) * 256
IN_WIDTH = 2 * QK_WIDTH + ATT_WIDTH + 2 * LRU_WIDTH
Q_BLOCK = 128
NORM_EPS = 1e-6

kernel_name = "hymba_diffattn_rglru_encoder"


def rmsnorm(x, g):
    xf = x.astype(jnp.float32)
    y = xf * lax.rsqrt(jnp.mean(xf * xf, axis=-1, keepdims=True) + NORM_EPS)
    return (y * g.astype(jnp.float32)).astype(x.dtype)


def alibi_slopes(n):
    return jnp.asarray([2.0 ** (-8.0 * (h + 1) / n) for h in range(n)], dtype=jnp.float32)


def diff_attention(q, k, v, lam):
    B, S = q.shape[0], q.shape[1]
    nb = S // Q_BLOCK
    scale = 1.0 / math.sqrt(HEAD_DK)
    slopes = alibi_slopes(N_ATT_HEADS)
    qb = q.reshape(B, nb, Q_BLOCK, N_ATT_HEADS, 2, HEAD_DK).transpose(1, 0, 2, 3, 4, 5)
    starts = jnp.arange(nb, dtype=jnp.int32) * Q_BLOCK
    kpos = jnp.arange(S, dtype=jnp.int32)

    def block(args):
        qblk, start = args
        s = jnp.einsum('bqhmd,bkhmd->bhmqk', qblk, k,
                       preferred_element_type=jnp.float32) * scale
        qpos = start + jnp.arange(Q_BLOCK, dtype=jnp.int32)
        dist = jnp.abs(qpos[:, None] - kpos[None, :]).astype(jnp.float32)
        s = s - slopes[None, :, None, None, None] * dist[None, None, None]
        p = jax.nn.softmax(s, axis=-1)
        w = p[:, :, 0] - lam * p[:, :, 1]
        return jnp.einsum('bhqk,bkhd->bqhd', w.astype(v.dtype), v)

    o = lax.map(block, (qb, starts))
    return o.transpose(1, 0, 2, 3, 4).reshape(B, S, N_ATT_HEADS, HEAD_DV)


def centred_dwconv(x, w, b):
    S = x.shape[1]
    xp = jnp.pad(x, ((0, 0), (CONV_PAD_L, CONV_W - 1 - CONV_PAD_L), (0, 0)))
    out = xp[:, 0:S] * w[0]
    for j in range(1, CONV_W):
        out = out + xp[:, j:j + S] * w[j]
    return out + b


def _lin_combine(left, right):
    a1, b1 = left
    a2, b2 = right
    return a1 * a2, a2 * b1 + b2


def rg_lru(x, w_r, b_r, w_i, b_i, lam, reverse):
    B, S, W = x.shape
    xf = x.astype(jnp.float32)
    xb = xf.reshape(B, S, N_LRU_BLOCKS, LRU_BLOCK)
    r = jax.nn.sigmoid(jnp.einsum('bsnc,ncd->bsnd', xb, w_r.astype(jnp.float32)).reshape(B, S, W)
                       + b_r.astype(jnp.float32))
    i = jax.nn.sigmoid(jnp.einsum('bsnc,ncd->bsnd', xb, w_i.astype(jnp.float32)).reshape(B, S, W)
                       + b_i.astype(jnp.float32))
    log_a = -RG_C * r * jax.nn.softplus(-lam.astype(jnp.float32))
    a = jnp.exp(log_a)
    mult = jnp.sqrt(jnp.maximum(-jnp.expm1(2.0 * log_a), 1e-12))
    bterm = mult * (i * xf)
    _, h = lax.associative_scan(_lin_combine, (a, bterm), axis=1, reverse=reverse)
    return h


def encoder_layer(x, layer, norm_mix, w_in, conv_w, conv_b, w_rg, b_rg, w_ig, b_ig,
                  lru_lambda, lambda_q1, lambda_k1, lambda_q2, lambda_k2, subln_g,
                  w_out, norm_ffn, w_gate, w_up, w_down):
    B, S, _ = x.shape
    lam_init = 0.8 - 0.6 * math.exp(-0.3 * layer)
    h = rmsnorm(x, norm_mix)
    proj = h @ w_in
    q, k, v, xr, gate = jnp.split(
        proj, [QK_WIDTH, 2 * QK_WIDTH, 2 * QK_WIDTH + ATT_WIDTH,
               2 * QK_WIDTH + ATT_WIDTH + LRU_WIDTH], axis=-1)
    q = q.reshape(B, S, N_ATT_HEADS, 2, HEAD_DK)
    k = k.reshape(B, S, N_ATT_HEADS, 2, HEAD_DK)
    v = v.reshape(B, S, N_ATT_HEADS, HEAD_DV)
    f32 = jnp.float32
    lam = (jnp.exp(jnp.sum(lambda_q1.astype(f32) * lambda_k1.astype(f32)))
           - jnp.exp(jnp.sum(lambda_q2.astype(f32) * lambda_k2.astype(f32))) + lam_init)
    o = diff_attention(q, k, v, lam)
    o = rmsnorm(o, subln_g) * (1.0 - lam_init)
    o = o.reshape(B, S, ATT_WIDTH)
    xc = centred_dwconv(xr, conv_w, conv_b)
    hl = (rg_lru(xc, w_rg[0], b_rg[0], w_ig[0], b_ig[0], lru_lambda[0], False)
          + rg_lru(xc, w_rg[1], b_rg[1], w_ig[1], b_ig[1], lru_lambda[1], True))
    y_lru = hl.astype(x.dtype) * jax.nn.gelu(gate)
    x = x + jnp.concatenate([o.astype(x.dtype), y_lru], axis=-1) @ w_out
    h2 = rmsnorm(x, norm_ffn)
    x = x + (jax.nn.silu(h2 @ w_gate) * (h2 @ w_up)) @ w_down
    return x


def run_trunk(x, norm_mix, w_in, conv_w, conv_b, w_rg, b_rg, w_ig, b_ig, lru_lambda,
              lambda_q1, lambda_k1, lambda_q2, lambda_k2, subln_g, w_out, norm_ffn,
              w_gate, w_up, w_down, norm_final):
    for l in range(DEPTH):
        x = encoder_layer(x, l, norm_mix[l], w_in[l], conv_w[l], conv_b[l], w_rg[l], b_rg[l],
                          w_ig[l], b_ig[l], lru_lambda[l], lambda_q1[l], lambda_k1[l],
                          lambda_q2[l], lambda_k2[l], subln_g[l], w_out[l], norm_ffn[l],
                          w_gate[l], w_up[l], w_down[l])
    return rmsnorm(x, norm_final)


def setup_inputs(seed: int = 0) -> dict:
    key = jax.random.key(seed)
    ks = jax.random.split(key, 24)
    f32 = jnp.float32
    nrm = lambda k, shape, s: jax.random.normal(k, shape, f32) * s
    a8 = jax.random.uniform(ks[12], (DEPTH, 2, LRU_WIDTH), f32, 0.9, 0.999)
    a = a8 ** (1.0 / RG_C)
    lru_lambda = jnp.log(a) - jnp.log1p(-a)
    return {
        "x_prompt": nrm(ks[0], (BATCH, SEQ, D_MODEL), 1.0),
        "x_sample": nrm(ks[1], (DEC_BATCH, DEC_SEQ, D_MODEL), 1.0),
        "norm_mix": 1.0 + nrm(ks[2], (DEPTH, D_MODEL), 0.02),
        "w_in": nrm(ks[3], (DEPTH, D_MODEL, IN_WIDTH), D_MODEL ** -0.5),
        "conv_w": nrm(ks[4], (DEPTH, CONV_W, LRU_WIDTH), CONV_W ** -0.5),
        "conv_b": nrm(ks[5], (DEPTH, LRU_WIDTH), 0.02),
        "w_rg": nrm(ks[6], (DEPTH, 2, N_LRU_BLOCKS, LRU_BLOCK, LRU_BLOCK), LRU_BLOCK ** -0.5),
        "b_rg": nrm(ks[7], (DEPTH, 2, LRU_WIDTH), 0.02),
        "w_ig": nrm(ks[8], (DEPTH, 2, N_LRU_BLOCKS, LRU_BLOCK, LRU_BLOCK), LRU_BLOCK ** -0.5),
        "b_ig": nrm(ks[9], (DEPTH, 2, LRU_WIDTH), 0.02),
        "lru_lambda": lru_lambda,
        "lambda_q1": nrm(ks[10], (DEPTH, HEAD_DK), 0.1),
        "lambda_k1": nrm(ks[11], (DEPTH, HEAD_DK), 0.1),
        "lambda_q2": nrm(ks[13], (DEPTH, HEAD_DK), 0.1),
        "lambda_k2": nrm(ks[14], (DEPTH, HEAD_DK), 0.1),
        "subln_g": 1.0 + nrm(ks[15], (DEPTH, HEAD_DV), 0.02),
        "w_out": nrm(ks[16], (DEPTH, MIX_WIDTH, D_MODEL), MIX_WIDTH ** -0.5),
        "norm_ffn": 1.0 + nrm(ks[17], (DEPTH, D_MODEL), 0.02),
        "w_gate": nrm(ks[18], (DEPTH, D_MODEL, D_FF), D_MODEL ** -0.5),
        "w_up": nrm(ks[19], (DEPTH, D_MODEL, D_FF), D_MODEL ** -0.5),
        "w_down": nrm(ks[20], (DEPTH, D_FF, D_MODEL), D_FF ** -0.5),
        "norm_final": 1.0 + nrm(ks[21], (D_MODEL,), 0.02),
    }


def reference(x_prompt, x_sample, norm_mix, w_in, conv_w, conv_b, w_rg, b_rg, w_ig, b_ig,
              lru_lambda, lambda_q1, lambda_k1, lambda_q2, lambda_k2, subln_g, w_out,
              norm_ffn, w_gate, w_up, w_down, norm_final):
    y_prompt = run_trunk(x_prompt, norm_mix, w_in, conv_w, conv_b, w_rg, b_rg, w_ig, b_ig,
                         lru_lambda, lambda_q1, lambda_k1, lambda_q2, lambda_k2, subln_g,
                         w_out, norm_ffn, w_gate, w_up, w_down, norm_final)
    y_sample = run_trunk(x_sample, norm_mix, w_in, conv_w, conv_b, w_rg, b_rg, w_ig, b_ig,
                         lru_lambda, lambda_q1, lambda_k1, lambda_q2, lambda_k2, subln_g,
                         w_out, norm_ffn, w_gate, w_up, w_down, norm_final)
    return (y_prompt, y_sample)
```

```python
import math
from contextlib import ExitStack

import numpy as np
import concourse.bass as bass
import concourse.mybir as mybir
from concourse.bass_utils import run_bass_kernel_spmd

F32 = mybir.dt.float32
BF16 = mybir.dt.bfloat16
U8 = mybir.dt.uint8
I32 = mybir.dt.int32
ALU = mybir.AluOpType
AF = mybir.ActivationFunctionType

D = 1024
DFF = 2816
NH = 4
INW = 2560
EPS = 1e-6
LAM_INIT = 0.8 - 0.6 * math.exp(0.0)
SLOPES = [2.0 ** (-8.0 * (h + 1) / NH) for h in range(NH)]
NFC = DFF // 128
ENGS = ("pe", "act", "dve", "pool", "sp")


class Res:
    __slots__ = ("name", "w", "r")

    def __init__(self, name):
        self.name = name
        self.w = None
        self.r = []


class Op:
    __slots__ = ("eng", "fn", "deps", "sig", "cnt", "key")


class Prog:
    def __init__(self):
        self.ops = {e: [] for e in ENGS}
        self.keycnt = {}
        self.dma_since_bar = []

    def op(self, eng, fn, r=(), w=(), key=None, ndma=1):
        o = Op()
        o.eng, o.fn, o.sig, o.cnt, o.key = eng, fn, False, 0, key
        deps = {}
        for x in r:
            if x.w is not None:
                deps[id(x.w)] = (x.w, True)
        for x in w:
            if x.w is not None and id(x.w) not in deps:
                deps[id(x.w)] = (x.w, True)
            for q in x.r:
                if id(q) not in deps:
                    deps[id(q)] = (q, False)
        o.deps = list(deps.values())
        for d, raw in o.deps:
            if d.key is None and (d.eng != eng or (raw and eng in ("act", "dve", "pool"))):
                d.sig = True
        for x in r:
            if key is None:
                x.r = [q for q in x.r if not (q.key is None and q.eng == eng)]
            x.r.append(o)
        for x in w:
            x.w = o
            x.r = []
        if key is not None:
            self.keycnt[key] = self.keycnt.get(key, 0) + 16 * ndma
            o.cnt = self.keycnt[key]
            self.dma_since_bar.append(o)
        self.ops[eng].append(o)
        return o

    def barrier(self):
        lasts = []
        for e in ENGS:
            for o in reversed(self.ops[e]):
                if o.key is None and o.fn is not None:
                    lasts.append(o)
                    o.sig = True
                    break
        lasts += self.dma_since_bar
        self.dma_since_bar = []
        for e in ENGS:
            o = Op()
            o.eng, o.fn, o.sig, o.cnt, o.key = e, None, False, 0, None
            o.deps = [(l, True) for l in lasts]
            self.ops[e].append(o)

    def emit(self, nc):
        with ExitStack() as es:
            sem = {e: es.enter_context(nc.semaphore("s_" + e)) for e in ("pe", "act", "dve", "pool")}
            keysem = {k: es.enter_context(nc.semaphore("k_%d" % i)) for i, k in enumerate(self.keycnt)}
            for e in ("pe", "act", "dve", "pool"):
                c = 0
                for o in self.ops[e]:
                    if o.key is None and o.sig and o.fn is not None:
                        c += 1
                        o.cnt = c
            block = es.enter_context(nc.Block())
            prog = self

            def run(engobj, E):
                waited = {}
                for o in prog.ops[E]:
                    for d, raw in o.deps:
                        if d.key is None:
                            if d.eng == E and (E == "pe" or not raw):
                                continue
                            sh, val = sem[d.eng], d.cnt
                        else:
                            sh, val = keysem[d.key], d.cnt
                        if waited.get(id(sh), 0) >= val:
                            continue
                        engobj.wait_ge(sh, val)
                        waited[id(sh)] = val
                    if o.fn is None:
                        continue
                    ins = o.fn(engobj)
                    if o.key is not None:
                        for i in ins:
                            i.then_inc(keysem[o.key], 16)
                    elif o.sig:
                        ins.then_inc(sem[E], 1)

            @block.tensor
            def _(e):
                run(e, "pe")

            @block.scalar
            def _(e):
                run(e, "act")

            @block.vector
            def _(e):
                run(e, "dve")

            @block.gpsimd
            def _(e):
                run(e, "pool")

            @block.sync
            def _(e):
                run(e, "sp")


def _build(seqs):
    SM = 4096
    assert max(seqs) <= SM
    NTM = SM // 128
    nc = bass.Bass("TRN2", target_bir_lowering=False)
    P = Prog()

    def din(name, shape, dt=F32):
        return nc.dram_tensor(name, list(shape), dt, kind="ExternalInput").ap()

    xs = [din("x%d" % i, [S, D]) for i, S in enumerate(seqs)]
    ys = [nc.dram_tensor("y%d" % i, [S, D], F32, kind="ExternalOutput").ap() for i, S in enumerate(seqs)]
    w_in = din("w_in", [D, INW])
    w_out = din("w_out", [D, D])
    w_gate = din("w_gate", [D, DFF])
    w_up = din("w_up", [D, DFF])
    w_down = din("w_down", [DFF, D])
    vec1024 = din("vec1024", [3, D])
    convw_d = din("convw", [128, 16])
    convb_d = din("convb", [128, 4])
    brg_d = din("brg", [128, 8])
    big_d = din("big", [128, 8])
    lru_d = din("lrul", [128, 8])
    wrg_d = din("wrg", [2, 8, 64, 64])
    wig_d = din("wig", [2, 8, 64, 64])
    lamv_d = din("lamv", [4, 64])
    subg_d = din("subg", [1, 128])

    def dint(name, shape):
        return nc.dram_tensor(name, list(shape), BF16, kind="Internal").ap()

    w_in_b = dint("w_in_b", [D, INW])
    w_out_b = dint("w_out_b", [D, D])
    w_gate_b = dint("w_gate_b", [NFC, 128, 8, 128])
    w_up_b = dint("w_up_b", [NFC, 128, 8, 128])
    w_down_b = dint("w_down_b", [DFF, D])

    es = ExitStack()
    sb = es.enter_context(nc.sbuf_tensor("sb", [128, 212000], U8))
    ps = es.enter_context(nc.psum_tensor("ps", [128, 8, 512], F32))
    psB = ps[:, 7, :].bitcast(BF16)

    def V(off, shape, dt):
        n = 1
        for s in shape[1:]:
            n *= s
        esz = 4 if dt in (F32, I32) else 2
        assert off % 4 == 0 and off + n * esz <= 212000, (off, shape)
        v = sb[:, off:off + n * esz].bitcast(dt)
        if len(shape) > 2:
            names = "abcde"[:len(shape) - 1]
            kw = {names[i]: shape[i + 1] for i in range(len(shape) - 2)}
            v = v.rearrange("p (%s) -> p %s" % (" ".join(names), " ".join(names)), **kw)
        return v, off + n * esz

    o = 0
    ident, o = V(o, [128, 128], BF16)
    gmix, o = V(o, [128, D], F32)
    gffn, o = V(o, [128, D], F32)
    gfin, o = V(o, [128, D], F32)
    gsub, o = V(o, [128, 128], F32)
    Dt, o = V(o, [128, 896], F32)
    sm, o = V(o, [128, 256], F32)
    biasL, o = V(o, [128, NH, 36], F32)
    biasR, o = V(o, [128, NH, 36], F32)
    dvals, o = V(o, [128, 36], F32)
    bd, o = V(o, [128, 16, 128], BF16)
    lamt, o = V(o, [128, 4, 64], F32)
    identf, o = V(o, [128, 128], F32)
    CONST_END = (o + 63) // 64 * 64
    CW, CB, BR, BI, CH, KP = 0, 16, 20, 28, 36, 44
    NHALF, PHALF, LAM, NLAM = 45, 46, 47, 48
    FL, FR = 52, 68
    KPS, KPR = 84, 88
    EPSC = 92
    F8L, F8R = 128, 160
    CF = 120
    TMP = 96

    def col(c, n=1):
        return sm[:, c:c + n]

    R = {}

    def res(name):
        if name not in R:
            R[name] = Res(name)
        return R[name]

    rc = res("const")
    bank = [res("bank%d" % i) for i in range(8)]

    def castw(src, dst, rows, name):
        def fn(e):
            out = []
            for r0 in range(0, rows, 128):
                out.append(e.dma_start(out=dst[r0:r0 + 128, :], in_=src[r0:r0 + 128, :]))
            return out
        P.op("pool", fn, w=[res(name)], key=name, ndma=rows // 128)

    castw(w_in, w_in_b, D, "w_in_b")
    castw(w_out, w_out_b, D, "w_out_b")
    def castgu(src, dst, name):
        def fn(e):
            out = []
            for kc in range(8):
                out.append(e.dma_start(out=dst[:, :, kc, :], in_=src[kc * 128:(kc + 1) * 128, :].rearrange("p (fc n) -> fc p n", n=128)))
            return out
        P.op("pool", fn, w=[res(name)], key=name, ndma=8)

    castgu(w_gate, w_gate_b, "w_gate_b")
    castgu(w_up, w_up_b, "w_up_b")
    castw(w_down, w_down_b, DFF, "w_down_b")

    def dma(eng, out, in_, r, w, key, n=1):
        P.op(eng, lambda e: [e.dma_start(out=out, in_=in_)], r=r, w=w, key=key)

    dma("sp", gmix, vec1024[0:1, :].partition_broadcast(128), [], [rc], "c0")
    dma("sp", gffn, vec1024[1:2, :].partition_broadcast(128), [], [rc], "c1")
    dma("sp", gfin, vec1024[2:3, :].partition_broadcast(128), [], [rc], "c2")
    dma("sp", gsub, subg_d[0:1, :].partition_broadcast(128), [], [rc], "c3")
    dma("sp", col(CW, 16), convw_d[:, :], [], [rc], "c4")
    dma("sp", col(CB, 4), convb_d[:, :], [], [rc], "c5")
    dma("sp", col(BR, 8), brg_d[:, :], [], [rc], "c6")
    dma("sp", col(BI, 8), big_d[:, :], [], [rc], "c7")
    dma("sp", col(CH, 8), lru_d[:, :], [], [rc], "c8")
    for i in range(4):
        dma("sp", lamt[:, i, :], lamv_d[i:i + 1, :].partition_broadcast(128), [], [rc], "c9_%d" % i)

    def C(eng, fn, r=None, w=None):
        P.op(eng, fn, r=[rc] if r is None else r, w=[rc] if w is None else w)

    C("pool", lambda e: e.iota(identf.bitcast(I32), pattern=[[1, 128]], base=0, channel_multiplier=-1))
    C("dve", lambda e: e.tensor_copy(out=Dt[:, 0:128], in_=identf.bitcast(I32)))
    C("dve", lambda e: e.tensor_single_scalar(out=identf, in_=Dt[:, 0:128], scalar=0.0, op=ALU.is_equal))
    C("dve", lambda e: e.tensor_copy(out=ident, in_=identf))
    C("pool", lambda e: e.iota(Dt.bitcast(I32), pattern=[[1, 896]], base=-384, channel_multiplier=-1))
    C("dve", lambda e: e.tensor_copy(out=Dt, in_=Dt.bitcast(I32)))
    C("act", lambda e: e.activation(out=Dt, in_=Dt, func=AF.Abs))
    C("pool", lambda e: e.iota(col(TMP).bitcast(I32), pattern=[[1, 1]], base=0, channel_multiplier=1))
    C("dve", lambda e: e.tensor_copy(out=col(KP), in_=col(TMP).bitcast(I32)))
    C("pool", lambda e: e.iota(dvals.bitcast(I32), pattern=[[1, 36]], base=0, channel_multiplier=0))
    C("dve", lambda e: e.tensor_copy(out=dvals, in_=dvals.bitcast(I32)))
    C("dve", lambda e: e.memset(col(NHALF), -0.5))
    C("dve", lambda e: e.memset(col(PHALF), 0.5))
    C("dve", lambda e: e.memset(col(EPSC), EPS))
    for h in range(NH):
        sl = SLOPES[h]
        C("dve", lambda e, h=h, sl=sl: e.tensor_scalar(out=col(KPS + h), in0=col(KP), scalar1=sl, scalar2=None, op0=ALU.mult))
        C("dve", lambda e, h=h, sl=sl: e.tensor_scalar(out=col(KPR + h), in0=col(KP), scalar1=-sl, scalar2=-sl, op0=ALU.mult, op1=ALU.add))
        C("dve", lambda e, h=h, sl=sl: e.tensor_scalar(out=biasL[:, h, :], in0=dvals, scalar1=-128.0 * sl, scalar2=col(KPS + h), op0=ALU.mult, op1=ALU.add))
        C("dve", lambda e, h=h, sl=sl: e.tensor_scalar(out=biasR[:, h, :], in0=dvals, scalar1=-128.0 * sl, scalar2=col(KPR + h), op0=ALU.mult, op1=ALU.add))
        for qs in range(4):
            C("act", lambda e, h=h, sl=sl, qs=qs: e.activation(out=col(FL + h * 4 + qs), in_=col(KP), func=AF.Exp, scale=-sl, bias=-sl * 128.0 * qs))
            C("act", lambda e, h=h, sl=sl, qs=qs: e.activation(out=col(FR + h * 4 + qs), in_=col(KP), func=AF.Exp, scale=sl, bias=-sl * (511.0 - 128.0 * qs)))
    for src_, dst_ in ((FL, F8L), (FR, F8R)):
        C("dve", lambda e, src_=src_, dst_=dst_: e.tensor_copy(
            out=col(dst_, 32).rearrange("p (h m q) -> p h m q", h=NH, m=2),
            in_=col(src_, 16).rearrange("p (h q) -> p h q", h=NH).unsqueeze(2).to_broadcast([128, NH, 2, 4])))
    C("dve", lambda e: e.tensor_scalar(out=gffn, in0=gffn, scalar1=32.0, scalar2=None, op0=ALU.mult))
    C("dve", lambda e: e.tensor_scalar(out=gfin, in0=gfin, scalar1=32.0, scalar2=None, op0=ALU.mult))
    C("dve", lambda e: e.tensor_scalar(out=gsub, in0=gsub, scalar1=(1.0 - LAM_INIT) * math.sqrt(128.0), scalar2=None, op0=ALU.mult))
    C("dve", lambda e: e.tensor_tensor(out=lamt[:, 0, :], in0=lamt[:, 0, :], in1=lamt[:, 1, :], op=ALU.mult))
    C("dve", lambda e: e.tensor_tensor(out=lamt[:, 2, :], in0=lamt[:, 2, :], in1=lamt[:, 3, :], op=ALU.mult))
    C("dve", lambda e: e.reduce_sum(out=col(TMP + 1), in_=lamt[:, 0, :], axis=mybir.AxisListType.X))
    C("dve", lambda e: e.reduce_sum(out=col(TMP + 2), in_=lamt[:, 2, :], axis=mybir.AxisListType.X))
    C("act", lambda e: e.activation(out=col(TMP + 1, 2), in_=col(TMP + 1, 2), func=AF.Exp))
    C("dve", lambda e: e.tensor_tensor(out=col(LAM), in0=col(TMP + 1), in1=col(TMP + 2), op=ALU.subtract))
    C("dve", lambda e: e.tensor_scalar(out=col(LAM), in0=col(LAM), scalar1=LAM_INIT, scalar2=None, op0=ALU.add))
    C("dve", lambda e: e.tensor_scalar(out=col(NLAM), in0=col(LAM), scalar1=-1.0, scalar2=None, op0=ALU.mult))
    C("dve", lambda e: e.tensor_scalar(out=col(BR, 16), in0=col(BR, 16), scalar1=0.5, scalar2=None, op0=ALU.mult))
    C("act", lambda e: e.activation(out=col(CH, 8), in_=col(CH, 8), func=AF.Exp, scale=-1.0))
    C("act", lambda e: e.activation(out=col(CH, 8), in_=col(CH, 8), func=AF.Ln, bias=1.0))
    C("dve", lambda e: e.tensor_scalar(out=col(CF, 8), in0=col(CH, 8), scalar1=-8.0, scalar2=None, op0=ALU.mult))
    C("dve", lambda e: e.tensor_scalar(out=col(CH, 8), in0=col(CH, 8), scalar1=-4.0, scalar2=None, op0=ALU.mult))
    C("pool", lambda e: e.memset(bd, 0.0))

    def bdload(e):
        out = []
        for d in range(2):
            for g, src in enumerate((wrg_d, wig_d)):
                for c in range(4):
                    idx = (d * 2 + g) * 4 + c
                    out.append(e.dma_start(out=bd[0:64, idx, 0:64], in_=src[d, 2 * c, :, :]))
                    out.append(e.dma_start(out=bd[64:128, idx, 64:128], in_=src[d, 2 * c + 1, :, :]))
        return out
    P.op("pool", bdload, r=[rc], w=[rc], key="bd", ndma=32)

    P.barrier()

    def rstd_ops(ssq_ap, n, epsn, rr, rw):
        P.op("dve", lambda e: e.tensor_scalar(out=ssq_ap, in0=ssq_ap, scalar1=epsn, scalar2=None, op0=ALU.add), r=rr, w=rw)
        P.op("pool", lambda e: e.tensor_tensor(out=ssq_ap, in0=ssq_ap, in1=col(NHALF).to_broadcast([128, n]), op=ALU.pow), r=rr + [rc], w=rw)

    psB6 = ps[:, 6, :].bitcast(BF16)

    def prep_tile(x_d, t, xt_s, xn_s, junk, ssq_c, dst_ap, rs, gain):
        r_xt, r_xn, r_junk, r_ssq, r_dst = rs
        pB, bB = (psB, bank[7]) if t % 2 == 0 else (psB6, bank[6])
        P.op("sp", lambda e: [e.dma_start(out=xt_s, in_=x_d[t * 128:(t + 1) * 128, :])], w=[r_xt], key=r_xt.name)
        P.op("act", lambda e: e.activation(out=xn_s, in_=xt_s, func=AF.Square, accum_out=ssq_c), r=[r_xt], w=[r_xn, r_ssq])
        P.op("act", lambda e: e.activation(out=ssq_c, in_=ssq_c, func=AF.Sqrt, scale=1.0 / D, bias=col(EPSC)), r=[r_ssq, rc], w=[r_ssq])
        P.op("dve", lambda e: e.reciprocal(out=ssq_c, in_=ssq_c), r=[r_ssq], w=[r_ssq])
        P.op("dve", lambda e: e.scalar_tensor_tensor(out=xn_s, in0=xt_s, scalar=ssq_c, in1=gain, op0=ALU.mult, op1=ALU.mult),
             r=[r_xt, r_ssq, rc], w=[r_xn])

        def tr(e):
            i = None
            for kc in range(8):
                i = e.transpose(out=pB[:, kc * 128:(kc + 1) * 128], in_=xn_s[:, kc * 128:(kc + 1) * 128], identity=ident)
            return i

        def tpart():
            P.op("pe", tr, r=[r_xn, rc], w=[bB])

        def back():
            if t % 2 == 0:
                P.op("act", lambda e: e.activation(out=dst_ap, in_=pB.rearrange("p (a b) -> p a b", a=8), func=AF.Copy), r=[bB], w=[r_dst])
            else:
                P.op("dve", lambda e: e.tensor_copy(out=dst_ap, in_=pB.rearrange("p (a b) -> p a b", a=8)), r=[bB], w=[r_dst])
        return tpart, back

    def mm_group(out_ap, pairs, r, w):
        def fn(e):
            i = None
            n = len(pairs)
            for k, (l, rh) in enumerate(pairs):
                i = e.matmul(out_ap, lhsT=l, rhs=rh, start=(k == 0), stop=(k == n - 1))
            return i
        P.op("pe", fn, r=r, w=w)

    Y0 = CONST_END
    yT, Y1 = V(Y0, [128, 4, SM], BF16)

    def do_seq(si, S):
        x_d, y_d = xs[si], ys[si]
        NT = S // 128
        NB = S // 512

        o = Y1
        xnT, o = V(o, [128, 8, SM], BF16)
        wA, o = V(o, [128, 2, 8, 256], BF16)
        PREP_OFF = o
        xt, o = V(o, [128, 3, D], F32)
        xn, o = V(o, [128, 3, D], BF16)
        junk = None
        bufX, o = V(o, [128, SM + 4], F32)
        xc, o = V(o, [128, SM], F32)
        xcb, o = V(o, [128, SM], BF16)
        tmp, o = V(o, [128, 10, 512], F32)
        tag = "s%dA" % si
        r_xnT = [Res(tag + "xnT%d" % b) for b in range(NB)]
        r_xt = [Res(tag + "xt%d" % i) for i in range(3)]
        r_xn = [Res(tag + "xn%d" % i) for i in range(3)]
        r_junk = Res(tag + "junk")
        r_ssq = [Res(tag + "ssq%d" % i) for i in range(3)]
        r_wA = [Res(tag + "wA%d" % i) for i in range(2)]
        r_bufX, r_xc, r_xcb = Res(tag + "bufX"), Res(tag + "xc"), Res(tag + "xcb")
        r_tmp = [Res(tag + "tmp%d" % i) for i in range(10)]
        r_yT = [res("yT%d" % c) for c in range(4)]
        r_hcar = Res(tag + "hcar")

        pipe = []
        for t in range(NT):
            s = t % 3
            pipe.append(prep_tile(x_d, t, xt[:, s, :], xn[:, s, :], junk, col(TMP + 4 + s),
                                  xnT[:, :, t * 128:(t + 1) * 128], (r_xt[s], r_xn[s], r_junk, r_ssq[s], r_xnT[t // 4]), gmix))
            if t >= 1:
                pipe[t - 1][0]()
            if t >= 2:
                pipe[t - 2][1]()
        pipe[NT - 1][0]()
        if NT >= 2:
            pipe[NT - 2][1]()
        pipe[NT - 1][1]()
        P.op("dve", lambda e: e.memset(bufX[:, 0:2], 0.0), w=[r_bufX])
        set1 = tuple(V(PREP_OFF + k * 4096, [128, 1024], F32)[0] for k in range(3))
        fence = []
        for x_ in r_xt + r_xn + [r_junk]:
            fence += ([x_.w] if x_.w is not None else []) + list(x_.r)
        r_sets = [(Res(tag + "a0"), Res(tag + "m0"), Res(tag + "u0")), (Res(tag + "a1"), Res(tag + "m1"), Res(tag + "u1"))]
        for x_ in r_sets[1]:
            x_.r = list(fence)
        bcount = [0]
        for c in range(4):
            ws = c % 2
            def ldw(e, c=c, ws=ws):
                src = w_in_b.rearrange("(kc p) n -> p kc n", p=128)
                return [e.dma_start(out=wA[:, ws, :, 0:128], in_=src[:, :, 1536 + c * 128:1536 + (c + 1) * 128]),
                        e.dma_start(out=wA[:, ws, :, 128:256], in_=src[:, :, 2048 + c * 128:2048 + (c + 1) * 128])]
            P.op("sp", ldw, r=[res("w_in_b")], w=[r_wA[ws]], key=r_wA[ws].name, ndma=2)
            if c > 0:
                P.op("dve", lambda e: e.memset(bufX[:, 0:2], 0.0), w=[r_bufX])
            P.op("dve", lambda e, S=S: e.memset(bufX[:, S + 2:S + 4], 0.0), w=[r_bufX])
            for b in range(NB):
                pb = bank[b % 2]
                mm_group(ps[:, b % 2, :], [(wA[:, ws, kc, 0:128], xnT[:, kc, b * 512:(b + 1) * 512]) for kc in range(8)],
                         [r_wA[ws], r_xnT[b]], [pb])
                P.op("act", lambda e, b=b: e.activation(out=bufX[:, 2 + b * 512:2 + (b + 1) * 512], in_=ps[:, b % 2, :], func=AF.Copy),
                     r=[pb], w=[r_bufX])
            cw = lambda j, c=c: col(CW + c * 4 + j)
            P.op("dve", lambda e, c=c, S=S, cw=cw: e.tensor_scalar(out=xc[:, 0:S], in0=bufX[:, 0:S], scalar1=cw(0), scalar2=col(CB + c),
                                                               op0=ALU.mult, op1=ALU.add), r=[r_bufX, rc], w=[r_xc])
            for j in range(1, 4):
                P.op("dve", lambda e, j=j, S=S, cw=cw: e.scalar_tensor_tensor(out=xc[:, 0:S], in0=bufX[:, j:j + S], scalar=cw(j), in1=xc[:, 0:S],
                                                                            op0=ALU.mult, op1=ALU.add), r=[r_bufX, r_xc, rc], w=[r_xc])
            P.op("act", lambda e, S=S: e.activation(out=xcb[:, 0:S], in_=xc[:, 0:S], func=AF.Copy), r=[r_xc], w=[r_xcb])
            def gelu_front(b, c=c, ws=ws):
                pb = bank[2 + b % 2]
                pg = ps[:, 2 + b % 2, :]
                blk = slice(b * 512, (b + 1) * 512)
                mm_group(pg, [(wA[:, ws, kc, 128:256], xnT[:, kc, blk]) for kc in range(8)], [r_wA[ws], r_xnT[b]], [pb])
                if b % 2 == 0:
                    t0, t1, rt0, rt1 = tmp[:, 8, :], tmp[:, 9, :], r_tmp[2], r_tmp[3]
                else:
                    t0, t1, rt0, rt1 = tmp[:, 0, :], tmp[:, 2, :], r_sets[0][0], r_sets[0][1]
                P.op("act", lambda e: e.activation(out=t0, in_=pg, func=AF.Square), r=[pb], w=[rt0])
                P.op("dve", lambda e: e.tensor_scalar(out=t0, in0=t0, scalar1=0.044715, scalar2=1.0, op0=ALU.mult, op1=ALU.add),
                     r=[rt0], w=[rt0])
                P.op("dve", lambda e: e.tensor_tensor(out=t0, in0=t0, in1=pg, op=ALU.mult), r=[rt0, pb], w=[rt0])

                def back():
                    P.op("act", lambda e: e.activation(out=t1, in_=t0, func=AF.Tanh, scale=math.sqrt(2.0 / math.pi)),
                         r=[rt0], w=[rt1])
                    P.op("dve", lambda e: e.scalar_tensor_tensor(out=yT[:, c, blk], in0=t1, scalar=1.0, in1=pg,
                                                                op0=ALU.add, op1=ALU.mult), r=[rt1, pb], w=[r_yT[c]])
                return back
            pend = None
            for b in range(NB):
                bk_ = gelu_front(b)
                if pend is not None:
                    pend()
                pend = bk_
            pend()
            sets = [tuple(tmp[:, 2 * k:2 * k + 2, :].rearrange("p a b -> p (a b)") for k in range(3)), set1]
            hb_b = tmp[:, 6:8, :].rearrange("p a b -> p (a b)")
            tr_, ti_ = tmp[:, 8, :], tmp[:, 9, :]
            batches = [list(range(g, min(g + 2, NB))) for g in range(0, NB, 2)]
            items = [(0, n, blks) for n, blks in enumerate(batches)] + [(1, n, blks) for n, blks in enumerate(batches[::-1])]
            hcar = col(TMP + 8)

            def phase12(item, sx, c=c):
                d, n, blks = item
                a_b, m_b, u_b = sets[sx]
                r_a, r_m, r_u = r_sets[sx]
                ir, ii = (d * 2 + 0) * 4 + c, (d * 2 + 1) * 4 + c
                kb = d * 4 + c
                nb = len(blks)
                t0_, t1_ = blks[0] * 512, (blks[-1] + 1) * 512
                L = t1_ - t0_
                for bb, b in enumerate(blks):
                    blk = slice(b * 512, (b + 1) * 512)
                    mm_group(ps[:, 4 + bb, :], [(bd[:, ir, :], xcb[:, blk])], [rc, r_xcb], [bank[4 + bb]])
                    mm_group(ps[:, 2 + bb, :], [(bd[:, ii, :], xcb[:, blk])], [rc, r_xcb], [bank[2 + bb]])
                m3 = m_b[:, 0:L].rearrange("p (a b) -> p a b", a=nb)
                u3 = u_b[:, 0:L].rearrange("p (a b) -> p a b", a=nb)
                P.op("act", lambda e: e.activation(out=m3, in_=ps[:, 4:4 + nb, :], func=AF.Tanh, scale=0.5, bias=col(BR + kb)),
                     r=[bank[4 + bb] for bb in range(nb)] + [rc], w=[r_m])
                P.op("act", lambda e: e.activation(out=a_b[:, 0:L], in_=m_b[:, 0:L], func=AF.Exp, scale=col(CH + kb), bias=col(CH + kb)),
                     r=[r_m, rc], w=[r_a])
                P.op("act", lambda e: e.activation(out=m_b[:, 0:L], in_=m_b[:, 0:L], func=AF.Exp, scale=col(CF + kb), bias=col(CF + kb)),
                     r=[r_m, rc], w=[r_m])
                P.op("act", lambda e: e.activation(out=u3, in_=ps[:, 2:2 + nb, :], func=AF.Tanh, scale=0.5, bias=col(BI + kb)),
                     r=[bank[2 + bb] for bb in range(nb)] + [rc], w=[r_u])
                P.op("dve", lambda e: e.tensor_scalar(out=m_b[:, 0:L], in0=m_b[:, 0:L], scalar1=1.0, scalar2=-1e-12, op0=ALU.subtract, op1=ALU.min),
                     r=[r_m], w=[r_m])
                P.op("dve", lambda e: e.scalar_tensor_tensor(out=u_b[:, 0:L], in0=u_b[:, 0:L], scalar=1.0, in1=xc[:, t0_:t1_], op0=ALU.add, op1=ALU.mult),
                     r=[r_u, r_xc], w=[r_u])
                P.op("act", lambda e: e.activation(out=m_b[:, 0:L], in_=m_b[:, 0:L], func=AF.Sqrt, scale=-1.0), r=[r_m], w=[r_m])

            def phase3(item, sx, c=c):
                d, n, blks = item
                a_b, m_b, u_b = sets[sx]
                r_a, r_m, r_u = r_sets[sx]
                t0_, t1_ = blks[0] * 512, (blks[-1] + 1) * 512
                L = t1_ - t0_
                P.op("dve", lambda e: e.scalar_tensor_tensor(out=u_b[:, 0:L], in0=u_b[:, 0:L], scalar=0.5, in1=m_b[:, 0:L], op0=ALU.mult, op1=ALU.mult),
                     r=[r_u, r_m], w=[r_u])
                if d == 0:
                    init = 0.0 if n == 0 else bufX[:, t0_ - 1:t0_]
                    P.op("dve", lambda e: e.tensor_tensor_scan(out=bufX[:, t0_:t1_], data0=a_b[:, 0:L], data1=u_b[:, 0:L], initial=init,
                                                               op0=ALU.mult, op1=ALU.add), r=[r_a, r_u, r_bufX], w=[r_bufX])
                else:
                    init = 0.0 if n == 0 else hcar
                    hx = n % 2
                    hb_b = hbs[hx]
                    r_hb = r_hbs[hx]
                    P.op("dve", lambda e: e.tensor_tensor_scan(out=hb_b[:, 0:L][:, ::-1], data0=a_b[:, 0:L][:, ::-1], data1=u_b[:, 0:L][:, ::-1],
                                                               initial=init, op0=ALU.mult, op1=ALU.add), r=[r_a, r_u, r_hcar], w=[r_hb])
                    P.op("dve", lambda e: e.tensor_copy(out=hcar, in_=hb_b[:, 0:1]), r=[r_hb], w=[r_hcar])
                    P.op("pool", lambda e: e.tensor_tensor(out=hb_b[:, 0:L], in0=hb_b[:, 0:L], in1=bufX[:, t0_:t1_], op=ALU.add),
                         r=[r_hb, r_bufX, r_hcar], w=[r_hb])
                    for fn_ in pend_y:
                        fn_()
                    pend_y[:] = [lambda: P.op("dve", lambda e: e.scalar_tensor_tensor(out=yT[:, c, t0_:t1_], in0=hb_b[:, 0:L], scalar=0.5, in1=yT[:, c, t0_:t1_],
                                                                                     op0=ALU.mult, op1=ALU.mult), r=[r_hb, r_yT[c]], w=[r_yT[c]])]

            hbs = [tmp[:, 6:8, :].rearrange("p a b -> p (a b)"), tmp[:, 8:10, :].rearrange("p a b -> p (a b)")]
            r_hbs = [r_tmp[8], r_tmp[2]]
            pend_y = []
            sx0 = bcount[0]
            bcount[0] += len(items)
            phase12(items[0], sx0 % 2)
            for i_, item in enumerate(items):
                if i_ + 1 < len(items):
                    phase12(items[i_ + 1], (sx0 + i_ + 1) % 2)
                phase3(item, (sx0 + i_) % 2)
            for fn_ in pend_y:
                fn_()
        P.barrier()

        o = Y1
        qT, o = V(o, [128, NH, SM], BF16)
        kT, o = V(o, [128, NH, SM], BF16)
        Va, o = V(o, [128, NTM, NH, 130], BF16)
        XR = o
        wB, o = V(o, [128, 8, 1536], BF16)
        xt, o = V(o, [128, 2, D], F32)
        xn, o = V(o, [128, 2, D], BF16)
        junk, o = V(o, [128, D], BF16)
        xnb, o = V(o, [128, 2, 8, 512], BF16)
        tag = "s%dB" % si
        r_wB = Res(tag + "wB")
        r_xt = [Res(tag + "xt%d" % i) for i in range(2)]
        r_xn = [Res(tag + "xn%d" % i) for i in range(2)]
        r_junk = Res(tag + "junk")
        r_ssq = [Res(tag + "ssq%d" % i) for i in range(2)]
        r_xnb = [Res(tag + "xnb%d" % i) for i in range(2)]
        r_q, r_k, r_v = Res(tag + "q"), Res(tag + "k"), Res(tag + "v")

        def ldwB(e):
            src = w_in_b.rearrange("(kc p) n -> p kc n", p=128)
            return [e.dma_start(out=wB[:, :, i * 512:(i + 1) * 512], in_=src[:, :, i * 512:(i + 1) * 512]) for i in range(3)]
        P.op("sp", ldwB, r=[res("w_in_b")], w=[r_wB], key=r_wB.name, ndma=3)
        P.op("pool", lambda e, NT=NT: e.memset(Va[:, 0:NT, :, 128:129], 1.0), w=[r_v])
        def b1_A(b, tts):
            bs = b % 2
            out = []
            for tt in tts:
                t = b * 4 + tt
                s = t % 2
                out.append(prep_tile(x_d, t, xt[:, s, :], xn[:, s, :], junk, col(TMP + 4 + s),
                                     xnb[:, bs, :, tt * 128:(tt + 1) * 128], (r_xt[s], r_xn[s], r_junk, r_ssq[s], r_xnb[bs]), gmix))
            return out

        def b1_TC(parts):
            for tp_, _ in parts:
                tp_()
            for _, bk_ in parts:
                bk_()

        def b1_groups(b):
            bs = b % 2
            blk = slice(b * 512, (b + 1) * 512)
            gl = []
            k = 0
            for h in range(NH):
                for which, dst, rr, scale in ((0, qT, r_q, 0.125), (1, kT, r_k, 1.0)):
                    pbi = k % 4
                    k += 1

                    def g(pbi=pbi, which=which, dst=dst, rr=rr, scale=scale, h=h):
                        c0 = which * 512 + h * 128
                        mm_group(ps[:, pbi, :], [(wB[:, kc, c0:c0 + 128], xnb[:, bs, kc, :]) for kc in range(8)], [r_wB, r_xnb[bs]], [bank[pbi]])
                        if which == 0:
                            P.op("act", lambda e: e.activation(out=dst[:, h, blk], in_=ps[:, pbi, :], func=AF.Copy, scale=scale), r=[bank[pbi]], w=[rr])
                        else:
                            P.op("dve", lambda e: e.tensor_copy(out=dst[:, h, blk], in_=ps[:, pbi, :]), r=[bank[pbi]], w=[rr])
                    gl.append(g)
            for tt in range(4):
                def g(tt=tt):
                    t = b * 4 + tt
                    pbi = 4 + tt % 2
                    mm_group(ps[:, pbi, :], [(xnb[:, bs, kc, tt * 128:(tt + 1) * 128], wB[:, kc, 1024:1536]) for kc in range(8)], [r_wB, r_xnb[bs]], [bank[pbi]])
                    P.op("dve", lambda e: e.tensor_copy(out=Va[:, t, :, 0:128], in_=ps[:, pbi, :].rearrange("p (h d) -> p h d", h=NH)),
                         r=[bank[pbi]], w=[r_v])
                gl.append(g)
            return gl

        p01 = b1_A(0, [0, 1])
        b1_TC(p01)
        p23 = b1_A(0, [2, 3])
        b1_TC(p23)
        for b in range(NB):
            gl = b1_groups(b)
            nxt = b + 1 < NB
            if nxt:
                p01 = b1_A(b + 1, [0, 1])
            for g in gl[:6]:
                g()
            if nxt:
                b1_TC(p01)
                p23 = b1_A(b + 1, [2, 3])
            for g in gl[6:]:
                g()
            if nxt:
                b1_TC(p23)
        P.barrier()

        o = XR
        oT, o = V(o, [128, NH, SM], BF16)
        OT_END = o
        Pb, o = V(o, [128, 3, 2, 512], BF16)
        Ssb, o = V(o, [128, 2, 512], F32)
        acc, o = V(o, [128, 8, 130], F32)
        ot, o = V(o, [128, 2, 4, 128], F32)
        onb, o = V(o, [128, 2, 4, 128], BF16)
        rl, o = V(o, [128, 32], F32)
        junkf, o = V(o, [128, 128], F32)
        tag = "s%dC" % si
        r_P = [Res(tag + "P%d" % i) for i in range(3)]
        r_Ssb = [Res(tag + "S%d" % i) for i in range(2)]
        r_accs = [Res(tag + "acc%d" % i) for i in range(8)]
        r_junkb = Res(tag + "jb")
        r_ot = [Res(tag + "ot%d" % i) for i in range(2)]
        r_on = [Res(tag + "on%d" % i) for i in range(2)]
        r_rl = [Res(tag + "rl%d" % i) for i in range(2)]
        r_oT = [res("oT%d" % h) for h in range(NH)]
        psO = ps[:, 4:7, :].rearrange("p a b -> p (a b)")

        def slot_ap(sl):
            bk, j = divmod(sl, 3)
            return ps[:, 4 + bk, j * 130:j * 130 + 129]
        iters = []
        for qb in range(NB):
            jd0 = 4 * qb
            for h in range(NH):
                phases = [(pn, ch) for pn, ch in (("L", list(range(0, jd0))), ("D", list(range(jd0, jd0 + 4))), ("R", list(range(jd0 + 4, NT)))) if ch]
                for pi_, (pn, ch) in enumerate(phases):
                    for idx, j in enumerate(ch):
                        iters.append(dict(qb=qb, h=h, pn=pn, idx=idx, n=len(ch), j=j, first_phase=(pi_ == 0),
                                          last_phase=(pi_ == len(phases) - 1)))
        NI = len(iters)

        def emit_qk_exp(i):
            it = iters[i]
            qb, h, j, pn = it["qb"], it["h"], it["j"], it["pn"]
            sbi, pbi = i % 2, i % 3
            jd0 = 4 * qb
            qblk = slice(qb * 512, (qb + 1) * 512)
            kblk = slice(j * 128, (j + 1) * 128)

            def qk(e):
                e.matmul(ps[:, 2 * sbi, :], lhsT=kT[0:64, h, kblk], rhs=qT[0:64, h, qblk], start=True, stop=True)
                return e.matmul(ps[:, 2 * sbi + 1, :], lhsT=kT[64:128, h, kblk], rhs=qT[64:128, h, qblk], start=True, stop=True)
            P.op("pe", qk, r=[r_q, r_k], w=[bank[2 * sbi], bank[2 * sbi + 1]])
            src2 = ps[:, 2 * sbi:2 * sbi + 2, :]
            if pn == "D":
                off = 384 - 128 * (j - jd0)
                P.op("dve", lambda e, off=off: e.scalar_tensor_tensor(out=Ssb, in0=Dt[:, off:off + 512].unsqueeze(1).to_broadcast([128, 2, 512]), scalar=-SLOPES[h],
                                                                    in1=src2, op0=ALU.mult, op1=ALU.add),
                     r=[bank[2 * sbi], bank[2 * sbi + 1], rc], w=[r_Ssb[0], r_Ssb[1]])
                P.op("act", lambda e: e.activation(out=Pb[:, pbi, :, :], in_=Ssb, func=AF.Exp), r=[r_Ssb[0], r_Ssb[1]], w=[r_P[pbi]])
            else:
                bias = biasL[:, h, jd0 - j:jd0 - j + 1] if pn == "L" else biasR[:, h, j - jd0 - 4:j - jd0 - 3]
                P.op("act", lambda e, bias=bias: e.activation(out=Pb[:, pbi, :, :], in_=src2, func=AF.Exp, bias=bias),
                     r=[bank[2 * sbi], bank[2 * sbi + 1], rc], w=[r_P[pbi]])

        deferred = []

        def flush(parity=None, tick=False):
            keep = []
            for ent in deferred:
                if tick:
                    ent[0] -= 1
                if ent[0] <= 0 or (parity is not None and ent[1] == parity) or (parity == -1):
                    ent[2]()
                else:
                    keep.append(ent)
            deferred[:] = keep

        fin_count = [0]

        def emit_pv(i):
            it = iters[i]
            qb, h, j, pn, idx, n = it["qb"], it["h"], it["j"], it["pn"], it["idx"], it["n"]
            pbi = i % 3
            qblk = slice(qb * 512, (qb + 1) * 512)
            for bk in range(3):
                sls = [sl for sl in range(8) if sl // 3 == bk]

                def pv(e, sls=sls):
                    ins = None
                    for sl in sls:
                        m, qs = divmod(sl, 4)
                        ins = e.matmul(slot_ap(sl), lhsT=Pb[:, pbi, m, qs * 128:(qs + 1) * 128], rhs=Va[:, j, h, 0:129],
                                       start=(idx == 0 and sl % 3 == 0), stop=(idx == n - 1), skip_group_check=True)
                    return ins
                P.op("pe", pv, r=[r_P[pbi], r_v], w=[bank[4 + bk]])
                if idx == n - 1:
                    fp = it["first_phase"]
                    nsl = len(sls)
                    pview = ps[:, 4 + bk, 0:nsl * 130].rearrange("p (s c) -> p s c", c=130)[:, :, 0:129]
                    aview = acc[:, sls[0]:sls[0] + nsl, 0:129]
                    racc = [r_accs[sl] for sl in sls]
                    if pn == "D":
                        if fp:
                            P.op("dve", lambda e, pview=pview, aview=aview: e.tensor_copy(out=aview, in_=pview), r=[bank[4 + bk]], w=racc)
                        else:
                            P.op("dve", lambda e, pview=pview, aview=aview: e.tensor_tensor(out=aview, in0=aview, in1=pview, op=ALU.add),
                                 r=[bank[4 + bk]] + racc, w=racc)
                    elif fp:
                        ftab = col((F8L if pn == "L" else F8R) + h * 8 + sls[0], nsl).unsqueeze(2).to_broadcast([128, nsl, 129])
                        P.op("dve", lambda e, pview=pview, aview=aview, ftab=ftab: e.tensor_tensor(out=aview, in0=pview, in1=ftab, op=ALU.mult),
                             r=[bank[4 + bk], rc], w=racc)
                    else:
                        for sl in sls:
                            f = col((FL if pn == "L" else FR) + h * 4 + sl % 4)
                            P.op("dve", lambda e, sl=sl, f=f: e.scalar_tensor_tensor(out=acc[:, sl, 0:129], in0=slot_ap(sl), scalar=f, in1=acc[:, sl, 0:129],
                                                                                    op0=ALU.mult, op1=ALU.add), r=[bank[4 + bk], r_accs[sl], rc], w=[r_accs[sl]])
            if idx == n - 1 and it["last_phase"]:
                par = fin_count[0] % 2
                fin_count[0] += 1
                flush(parity=par)
                otp, onp, rlp = ot[:, par], onb[:, par], rl[:, par * 16:(par + 1) * 16]
                P.op("dve", lambda e: e.reciprocal(out=rlp[:, 0:8], in_=acc[:, :, 128]), r=r_accs, w=[r_rl[par]])
                P.op("dve", lambda e: e.tensor_scalar(out=rlp[:, 4:8], in0=rlp[:, 4:8], scalar1=col(NLAM), scalar2=None, op0=ALU.mult), r=[r_rl[par], rc], w=[r_rl[par]])
                for qs in range(4):
                    P.op("dve", lambda e, qs=qs: e.tensor_scalar(out=otp[:, qs, :], in0=acc[:, 4 + qs, 0:128], scalar1=rlp[:, 4 + qs:5 + qs], scalar2=None, op0=ALU.mult),
                         r=[r_accs[4 + qs], r_rl[par]], w=[r_ot[par]])
                    P.op("dve", lambda e, qs=qs: e.scalar_tensor_tensor(out=otp[:, qs, :], in0=acc[:, qs, 0:128], scalar=rlp[:, qs:qs + 1], in1=otp[:, qs, :],
                                                                       op0=ALU.mult, op1=ALU.add), r=[r_accs[qs], r_rl[par], r_ot[par]], w=[r_ot[par]])
                for qs in range(4):
                    P.op("dve", lambda e, qs=qs: e.scalar_tensor_tensor(out=junkf, in0=otp[:, qs, :], scalar=1.0, in1=otp[:, qs, :], op0=ALU.mult, op1=ALU.mult,
                                                                       accum_out=rlp[:, 8 + qs:9 + qs]), r=[r_ot[par]], w=[r_junkb, r_rl[par]])
                rstd_ops(rlp[:, 8:12], 4, 128.0 * EPS, [r_rl[par]], [r_rl[par]])
                for qs in range(4):
                    P.op("dve", lambda e, qs=qs: e.scalar_tensor_tensor(out=onp[:, qs, :], in0=otp[:, qs, :], scalar=rlp[:, 8 + qs:9 + qs], in1=gsub,
                                                                       op0=ALU.mult, op1=ALU.mult), r=[r_ot[par], r_rl[par], rc], w=[r_on[par]])

                def fin(par=par, h=h, qblk=qblk, onp=onp):
                    def tro(e):
                        ins = None
                        for qs in range(4):
                            ins = e.transpose(out=psB[:, qs * 128:(qs + 1) * 128], in_=onp[:, qs, :], identity=ident)
                        return ins
                    P.op("pe", tro, r=[r_on[par], rc], w=[bank[7]])
                    P.op("act", lambda e: e.activation(out=oT[:, h, qblk], in_=psB[:, 0:512], func=AF.Copy), r=[bank[7]], w=[r_oT[h]])
                deferred.append([10, par, fin])

        emit_qk_exp(0)
        for i in range(NI):
            if i + 1 < NI:
                emit_qk_exp(i + 1)
            emit_pv(i)
            flush(tick=True)
        flush(parity=-1)
        P.barrier()

        o = Y1
        wo, o = V(o, [128, 8, D], BF16)
        wd, o = V(o, [128, 11, D], BF16)
        hT, o = V(o, [128, 2, 11, 512], BF16)
        x1, o = V(o, [128, 4, D], F32)
        h2T, o = V(o, [128, 8, 512], BF16)
        wgu, o = V(o, [128, 2, 2, 8, 128], BF16)
        assert o <= XR, (o, XR)
        o = OT_END
        xt, o = V(o, [128, 2, D], F32)
        h2n, o = V(o, [128, 2, D], BF16)
        tw, o = V(o, [128, 4, 512], F32)
        tag = "s%dD" % si
        r_wo, r_wd = Res(tag + "wo"), Res(tag + "wd")
        r_hT = [Res(tag + "hT%d" % i) for i in range(2)]
        r_x1 = [Res(tag + "x1_%d" % i) for i in range(4)]
        r_h2T = Res(tag + "h2T")
        r_wgu = [Res(tag + "wgu%d" % i) for i in range(2)]
        r_xt = [Res(tag + "xt%d" % i) for i in range(2)]
        r_h2n, r_junk = Res(tag + "h2n"), Res(tag + "junk")
        r_ssq = [Res(tag + "ssq%d" % i) for i in range(2)]
        r_tw = [Res(tag + "tw%d" % i) for i in range(4)]

        P.op("sp", lambda e: [e.dma_start(out=wo, in_=w_out_b.rearrange("(kc p) n -> p kc n", p=128))], r=[res("w_out_b")], w=[r_wo], key=r_wo.name)
        gcount = 0
        h2n2 = h2n
        sqo = tw[:, 0:2, :].rearrange("p a b -> p (a b)")
        r_h2ns = [Res(tag + "h2n%d" % i) for i in range(2)]
        r_ssq4 = [Res(tag + "ssq4_%d" % i) for i in range(4)]

        def c_front(b, tt):
            t = b * 4 + tt
            s = t % 2
            tok = slice(t * 128, (t + 1) * 128)
            P.op("sp", lambda e: [e.dma_start(out=xt[:, s, :], in_=x_d[tok, :])], w=[r_xt[s]], key=r_xt[s].name)
            for hf in range(2):
                cs = slice(hf * 512, (hf + 1) * 512)
                pairs = [((oT[:, kc, tok] if kc < 4 else yT[:, kc - 4, tok]), wo[:, kc, cs]) for kc in range(8)]
                mm_group(ps[:, hf, :], pairs, r_oT + r_yT + [r_wo], [bank[hf]])
                P.op("dve", lambda e, hf=hf, cs=cs: e.tensor_tensor(out=x1[:, tt, cs], in0=ps[:, hf, :], in1=xt[:, s, cs], op=ALU.add),
                     r=[bank[hf], r_xt[s]], w=[r_x1[tt]])
            ssq_c = col(TMP + 10 + tt)
            P.op("act", lambda e: e.activation(out=sqo, in_=x1[:, tt, :], func=AF.Square, accum_out=ssq_c), r=[r_x1[tt]], w=[r_tw[0], r_tw[1], r_ssq4[tt]])
            rstd_ops(ssq_c, 1, D * EPS, [r_ssq4[tt]], [r_ssq4[tt]])
            P.op("dve", lambda e: e.scalar_tensor_tensor(out=h2n2[:, tt % 2, :], in0=x1[:, tt, :], scalar=ssq_c, in1=gffn, op0=ALU.mult, op1=ALU.mult),
                 r=[r_x1[tt], r_ssq4[tt], rc], w=[r_h2ns[tt % 2]])

        def c_back(b, tt):
            def tr2(e):
                i = None
                for kc in range(8):
                    i = e.transpose(out=psB[:, kc * 128:(kc + 1) * 128], in_=h2n2[:, tt % 2, kc * 128:(kc + 1) * 128], identity=ident)
                return i
            P.op("pe", tr2, r=[r_h2ns[tt % 2], rc], w=[bank[7]])
            P.op("act", lambda e: e.activation(out=h2T[:, :, tt * 128:(tt + 1) * 128], in_=psB.rearrange("p (a b) -> p a b", a=8), func=AF.Copy),
                 r=[bank[7]], w=[r_h2T])

        def c_gu(fc):
            nonlocal_g = gstate
            gs = nonlocal_g[0] % 2
            nonlocal_g[0] += 1
            part, fi = divmod(fc, 11)

            def ldgu(e):
                return [e.dma_start(out=wgu[:, gs, 0, :, :], in_=w_gate_b[fc]),
                        e.dma_start(out=wgu[:, gs, 1, :, :], in_=w_up_b[fc])]
            P.op("sp", ldgu, r=[res("w_gate_b"), res("w_up_b")], w=[r_wgu[gs]], key=r_wgu[gs].name, ndma=2)
            bg, bu = 2 + 2 * gs, 3 + 2 * gs
            mm_group(ps[:, bg, :], [(wgu[:, gs, 0, kc, :], h2T[:, kc, :]) for kc in range(8)], [r_wgu[gs], r_h2T], [bank[bg]])
            mm_group(ps[:, bu, :], [(wgu[:, gs, 1, kc, :], h2T[:, kc, :]) for kc in range(8)], [r_wgu[gs], r_h2T], [bank[bu]])
            th, ww = tw[:, 2 * gs, :], tw[:, 2 * gs + 1, :]
            P.op("act", lambda e: e.activation(out=th, in_=ps[:, bg, :], func=AF.Tanh, scale=0.5), r=[bank[bg]], w=[r_tw[2 * gs]])
            P.op("dve", lambda e: e.scalar_tensor_tensor(out=ww, in0=th, scalar=1.0, in1=ps[:, bg, :], op0=ALU.add, op1=ALU.mult),
                 r=[r_tw[2 * gs], bank[bg]], w=[r_tw[2 * gs + 1]])
            P.op("dve", lambda e: e.scalar_tensor_tensor(out=hT[:, part, fi, :], in0=ww, scalar=0.5, in1=ps[:, bu, :],
                                                        op0=ALU.mult, op1=ALU.mult), r=[r_tw[2 * gs + 1], bank[bu]], w=[r_hT[part]])

        def c_wd(part):
            P.op("pool", lambda e: [e.dma_start(out=wd, in_=w_down_b[part * 1408:(part + 1) * 1408, :].rearrange("(fc p) n -> p fc n", p=128))],
                 r=[res("w_down_b")], w=[r_wd], key=r_wd.name)

        def c_down(part):
            for tt in range(4):
                for hf in range(2):
                    cs = slice(hf * 512, (hf + 1) * 512)
                    mm_group(ps[:, hf, :], [(hT[:, part, fi, tt * 128:(tt + 1) * 128], wd[:, fi, cs]) for fi in range(11)], [r_hT[part], r_wd], [bank[hf]])
                    P.op("dve", lambda e, hf=hf, tt=tt, cs=cs: e.tensor_tensor(out=x1[:, tt, cs], in0=x1[:, tt, cs], in1=ps[:, hf, :], op=ALU.add),
                         r=[bank[hf], r_x1[tt]], w=[r_x1[tt]])

        def c_final(b, tt):
            t = b * 4 + tt
            tok = slice(t * 128, (t + 1) * 128)
            ssq_c = col(TMP + 14 + tt)
            P.op("act", lambda e: e.activation(out=sqo, in_=x1[:, tt, :], func=AF.Square, accum_out=ssq_c), r=[r_x1[tt]], w=[r_tw[0], r_tw[1], r_ssq4[tt]])
            rstd_ops(ssq_c, 1, D * EPS, [r_ssq4[tt]], [r_ssq4[tt]])
            P.op("dve", lambda e: e.scalar_tensor_tensor(out=x1[:, tt, :], in0=x1[:, tt, :], scalar=ssq_c, in1=gfin, op0=ALU.mult, op1=ALU.mult),
                 r=[r_x1[tt], r_ssq4[tt], rc], w=[r_x1[tt]])
            P.op("pool", lambda e: [e.dma_start(out=y_d[tok, :], in_=x1[:, tt, :])], r=[r_x1[tt]], key="st" + r_x1[tt].name)

        gstate = [0]
        for b in range(NB):
            for tt in range(4):
                c_front(b, tt)
                if tt >= 1:
                    c_back(b, tt - 1)
            c_back(b, 3)
            c_wd(0)
            for fc in range(12):
                c_gu(fc)
            c_down(0)
            c_wd(1)
            for fc in range(12, 22):
                c_gu(fc)
            c_down(1)
            for tt in range(4):
                c_final(b, tt)
        P.barrier()

    for si_, S_ in enumerate(seqs):
        do_seq(si_, S_)
    P.emit(nc)
    es.close()
    return nc


_NC_CACHE = {}


def _common_inputs(inp):
    f = lambda a: np.ascontiguousarray(np.asarray(a, dtype=np.float32))
    pc = lambda a: f(np.asarray(a).reshape(4, 128).T)
    pdc = lambda a: f(np.asarray(a).reshape(2, 4, 128).transpose(2, 0, 1).reshape(128, 8))
    return {
        "w_in": f(inp["w_in"][0]), "w_out": f(inp["w_out"][0]), "w_gate": f(inp["w_gate"][0]),
        "w_up": f(inp["w_up"][0]), "w_down": f(inp["w_down"][0]),
        "vec1024": f(np.stack([inp["norm_mix"][0], inp["norm_ffn"][0], inp["norm_final"]])),
        "convw": f(np.asarray(inp["conv_w"][0]).T.reshape(4, 128, 4).transpose(1, 0, 2).reshape(128, 16)),
        "convb": pc(inp["conv_b"][0]),
        "brg": pdc(inp["b_rg"][0]), "big": pdc(inp["b_ig"][0]), "lrul": pdc(inp["lru_lambda"][0]),
        "wrg": f(inp["w_rg"][0]), "wig": f(inp["w_ig"][0]),
        "lamv": f(np.stack([inp["lambda_q1"][0], inp["lambda_k1"][0], inp["lambda_q2"][0], inp["lambda_k2"][0]])),
        "subg": f(np.asarray(inp["subln_g"][0]).reshape(1, 128)),
    }


def kernel(**inp):
    xp = np.asarray(inp["x_prompt"], dtype=np.float32)
    xsm = np.asarray(inp["x_sample"], dtype=np.float32)
    seqs = (xp.shape[1], xp.shape[1], xsm.shape[1])
    if seqs not in _NC_CACHE:
        _NC_CACHE[seqs] = _build(list(seqs))
    nc = _NC_CACHE[seqs]
    common = _common_inputs(inp)
    in_maps = []
    for c in range(8):
        m = dict(common)
        m["x0"] = np.ascontiguousarray(xp[2 * c])
        m["x1"] = np.ascontiguousarray(xp[2 * c + 1])
        m["x2"] = np.ascontiguousarray(xsm[c])
        in_maps.append(m)
    res = run_bass_kernel_spmd(nc, in_maps, core_ids=list(range(8)))
    yp = np.empty_like(xp)
    ysm = np.empty_like(xsm)
    for c in range(8):
        r = res.results[c]
        yp[2 * c] = r["y0"]
        yp[2 * c + 1] = r["y1"]
        ysm[c] = r["y2"]
    return yp, ysm
```

```python
import math
from contextlib import ExitStack

import numpy as np
import concourse.bass as bass
import concourse.mybir as mybir
from concourse.bass_utils import run_bass_kernel_spmd

F32 = mybir.dt.float32
BF16 = mybir.dt.bfloat16
U8 = mybir.dt.uint8
I32 = mybir.dt.int32
ALU = mybir.AluOpType
AF = mybir.ActivationFunctionType

D = 1024
DFF = 2816
NH = 4
INW = 2560
EPS = 1e-6
LAM_INIT = 0.8 - 0.6 * math.exp(0.0)
SLOPES = [2.0 ** (-8.0 * (h + 1) / NH) for h in range(NH)]
NFC = DFF // 128
ENGS = ("pe", "act", "dve", "pool", "sp")


class Res:
    __slots__ = ("name", "w", "r")

    def __init__(self, name):
        self.name = name
        self.w = None
        self.r = []


class Op:
    __slots__ = ("eng", "fn", "deps", "sig", "cnt", "key")


class Prog:
    def __init__(self):
        self.ops = {e: [] for e in ENGS}
        self.keycnt = {}
        self.dma_since_bar = []

    def op(self, eng, fn, r=(), w=(), key=None, ndma=1):
        o = Op()
        o.eng, o.fn, o.sig, o.cnt, o.key = eng, fn, False, 0, key
        deps = {}
        for x in r:
            if x.w is not None:
                deps[id(x.w)] = (x.w, True)
        for x in w:
            if x.w is not None and id(x.w) not in deps:
                deps[id(x.w)] = (x.w, True)
            for q in x.r:
                if id(q) not in deps:
                    deps[id(q)] = (q, False)
        o.deps = list(deps.values())
        for d, raw in o.deps:
            if d.key is None and (d.eng != eng or (raw and eng in ("act", "dve", "pool"))):
                d.sig = True
        for x in r:
            if key is None:
                x.r = [q for q in x.r if not (q.key is None and q.eng == eng)]
            x.r.append(o)
        for x in w:
            x.w = o
            x.r = []
        if key is not None:
            self.keycnt[key] = self.keycnt.get(key, 0) + 16 * ndma
            o.cnt = self.keycnt[key]
            self.dma_since_bar.append(o)
        self.ops[eng].append(o)
        return o

    def barrier(self):
        lasts = []
        for e in ENGS:
            for o in reversed(self.ops[e]):
                if o.key is None and o.fn is not None:
                    lasts.append(o)
                    o.sig = True
                    break
        lasts += self.dma_since_bar
        self.dma_since_bar = []
        for e in ENGS:
            o = Op()
            o.eng, o.fn, o.sig, o.cnt, o.key = e, None, False, 0, None
            o.deps = [(l, True) for l in lasts]
            self.ops[e].append(o)

    def emit(self, nc):
        with ExitStack() as es:
            sem = {e: es.enter_context(nc.semaphore("s_" + e)) for e in ("pe", "act", "dve", "pool")}
            keysem = {k: es.enter_context(nc.semaphore("k_%d" % i)) for i, k in enumerate(self.keycnt)}
            for e in ("pe", "act", "dve", "pool"):
                c = 0
                for o in self.ops[e]:
                    if o.key is None and o.sig and o.fn is not None:
                        c += 1
                        o.cnt = c
            block = es.enter_context(nc.Block())
            prog = self

            def run(engobj, E):
                waited = {}
                for o in prog.ops[E]:
                    for d, raw in o.deps:
                        if d.key is None:
                            if d.eng == E and (E == "pe" or not raw):
                                continue
                            sh, val = sem[d.eng], d.cnt
                        else:
                            sh, val = keysem[d.key], d.cnt
                        if waited.get(id(sh), 0) >= val:
                            continue
                        engobj.wait_ge(sh, val)
                        waited[id(sh)] = val
                    if o.fn is None:
                        continue
                    ins = o.fn(engobj)
                    if o.key is not None:
                        for i in ins:
                            i.then_inc(keysem[o.key], 16)
                    elif o.sig:
                        ins.then_inc(sem[E], 1)

            @block.tensor
            def _(e):
                run(e, "pe")

            @block.scalar
            def _(e):
                run(e, "act")

            @block.vector
            def _(e):
                run(e, "dve")

            @block.gpsimd
            def _(e):
                run(e, "pool")

            @block.sync
            def _(e):
                run(e, "sp")


def _build(seqs):
    SM = 4096
    assert max(seqs) <= SM
    NTM = SM // 128
    nc = bass.Bass("TRN2", target_bir_lowering=False)
    P = Prog()

    def din(name, shape, dt=F32):
        return nc.dram_tensor(name, list(shape), dt, kind="ExternalInput").ap()

    xs = [din("x%d" % i, [S, D]) for i, S in enumerate(seqs)]
    ys = [nc.dram_tensor("y%d" % i, [S, D], F32, kind="ExternalOutput").ap() for i, S in enumerate(seqs)]
    w_in = din("w_in", [D, INW])
    w_out = din("w_out", [D, D])
    w_gate = din("w_gate", [D, DFF])
    w_up = din("w_up", [D, DFF])
    w_down = din("w_down", [DFF, D])
    vec1024 = din("vec1024", [3, D])
    convw_d = din("convw", [128, 16])
    convb_d = din("convb", [128, 4])
    brg_d = din("brg", [128, 8])
    big_d = din("big", [128, 8])
    lru_d = din("lrul", [128, 8])
    wrg_d = din("wrg", [2, 8, 64, 64])
    wig_d = din("wig", [2, 8, 64, 64])
    lamv_d = din("lamv", [4, 64])
    subg_d = din("subg", [1, 128])

    def dint(name, shape):
        return nc.dram_tensor(name, list(shape), BF16, kind="Internal").ap()

    w_in_b = dint("w_in_b", [D, INW])
    w_out_b = dint("w_out_b", [D, D])
    w_gate_b = dint("w_gate_b", [NFC, 128, 8, 128])
    w_up_b = dint("w_up_b", [NFC, 128, 8, 128])
    w_down_b = dint("w_down_b", [DFF, D])

    es = ExitStack()
    sb = es.enter_context(nc.sbuf_tensor("sb", [128, 212000], U8))
    ps = es.enter_context(nc.psum_tensor("ps", [128, 8, 512], F32))
    psB = ps[:, 7, :].bitcast(BF16)

    def V(off, shape, dt):
        n = 1
        for s in shape[1:]:
            n *= s
        esz = 4 if dt in (F32, I32) else 2
        assert off % 4 == 0 and off + n * esz <= 212000, (off, shape)
        v = sb[:, off:off + n * esz].bitcast(dt)
        if len(shape) > 2:
            names = "abcde"[:len(shape) - 1]
            kw = {names[i]: shape[i + 1] for i in range(len(shape) - 2)}
            v = v.rearrange("p (%s) -> p %s" % (" ".join(names), " ".join(names)), **kw)
        return v, off + n * esz

    o = 0
    ident, o = V(o, [128, 128], BF16)
    gmix, o = V(o, [128, D], F32)
    gffn, o = V(o, [128, D], F32)
    gfin, o = V(o, [128, D], F32)
    gsub, o = V(o, [128, 128], F32)
    Dt, o = V(o, [128, 896], F32)
    sm, o = V(o, [128, 256], F32)
    biasL, o = V(o, [128, NH, 36], F32)
    biasR, o = V(o, [128, NH, 36], F32)
    dvals, o = V(o, [128, 36], F32)
    bd, o = V(o, [128, 16, 128], BF16)
    lamt, o = V(o, [128, 4, 64], F32)
    identf, o = V(o, [128, 128], F32)
    CONST_END = (o + 63) // 64 * 64
    CW, CB, BR, BI, CH, KP = 0, 16, 20, 28, 36, 44
    NHALF, PHALF, LAM, NLAM = 45, 46, 47, 48
    FL, FR = 52, 68
    KPS, KPR = 84, 88
    EPSC = 92
    F8L, F8R = 128, 160
    CF = 120
    TMP = 96

    def col(c, n=1):
        return sm[:, c:c + n]

    R = {}

    def res(name):
        if name not in R:
            R[name] = Res(name)
        return R[name]

    rc = res("const")
    bank = [res("bank%d" % i) for i in range(8)]

    def castw(src, dst, rows, name):
        def fn(e):
            out = []
            for r0 in range(0, rows, 128):
                out.append(e.dma_start(out=dst[r0:r0 + 128, :], in_=src[r0:r0 + 128, :]))
            return out
        P.op("pool", fn, w=[res(name)], key=name, ndma=rows // 128)

    castw(w_in, w_in_b, D, "w_in_b")
    def castgu(src, dst, name):
        def fn(e):
            out = []
            for kc in range(8):
                out.append(e.dma_start(out=dst[:, :, kc, :], in_=src[kc * 128:(kc + 1) * 128, :].rearrange("p (fc n) -> fc p n", n=128)))
            return out
        P.op("pool", fn, w=[res(name)], key=name, ndma=8)


    def dma(eng, out, in_, r, w, key, n=1):
        P.op(eng, lambda e: [e.dma_start(out=out, in_=in_)], r=r, w=w, key=key)

    dma("sp", gmix, vec1024[0:1, :].partition_broadcast(128), [], [rc], "c0")
    dma("sp", gffn, vec1024[1:2, :].partition_broadcast(128), [], [rc], "c1")
    dma("sp", gfin, vec1024[2:3, :].partition_broadcast(128), [], [rc], "c2")
    dma("sp", gsub, subg_d[0:1, :].partition_broadcast(128), [], [rc], "c3")
    dma("sp", col(CW, 16), convw_d[:, :], [], [rc], "c4")
    dma("sp", col(CB, 4), convb_d[:, :], [], [rc], "c5")
    dma("sp", col(BR, 8), brg_d[:, :], [], [rc], "c6")
    dma("sp", col(BI, 8), big_d[:, :], [], [rc], "c7")
    dma("sp", col(CH, 8), lru_d[:, :], [], [rc], "c8")
    for i in range(4):
        dma("sp", lamt[:, i, :], lamv_d[i:i + 1, :].partition_broadcast(128), [], [rc], "c9_%d" % i)

    def C(eng, fn, r=None, w=None):
        P.op(eng, fn, r=[rc] if r is None else r, w=[rc] if w is None else w)

    C("pool", lambda e: e.iota(identf.bitcast(I32), pattern=[[1, 128]], base=0, channel_multiplier=-1))
    C("dve", lambda e: e.tensor_copy(out=Dt[:, 0:128], in_=identf.bitcast(I32)))
    C("dve", lambda e: e.tensor_single_scalar(out=identf, in_=Dt[:, 0:128], scalar=0.0, op=ALU.is_equal))
    C("dve", lambda e: e.tensor_copy(out=ident, in_=identf))
    C("pool", lambda e: e.iota(Dt.bitcast(I32), pattern=[[1, 896]], base=-384, channel_multiplier=-1))
    C("dve", lambda e: e.tensor_copy(out=Dt, in_=Dt.bitcast(I32)))
    C("act", lambda e: e.activation(out=Dt, in_=Dt, func=AF.Abs))
    C("pool", lambda e: e.iota(col(TMP).bitcast(I32), pattern=[[1, 1]], base=0, channel_multiplier=1))
    C("dve", lambda e: e.tensor_copy(out=col(KP), in_=col(TMP).bitcast(I32)))
    C("pool", lambda e: e.iota(dvals.bitcast(I32), pattern=[[1, 36]], base=0, channel_multiplier=0))
    C("dve", lambda e: e.tensor_copy(out=dvals, in_=dvals.bitcast(I32)))
    C("dve", lambda e: e.memset(col(NHALF), -0.5))
    C("dve", lambda e: e.memset(col(PHALF), 0.5))
    C("dve", lambda e: e.memset(col(EPSC), EPS))
    for h in range(NH):
        sl = SLOPES[h]
        C("dve", lambda e, h=h, sl=sl: e.tensor_scalar(out=col(KPS + h), in0=col(KP), scalar1=sl, scalar2=None, op0=ALU.mult))
        C("dve", lambda e, h=h, sl=sl: e.tensor_scalar(out=col(KPR + h), in0=col(KP), scalar1=-sl, scalar2=-sl, op0=ALU.mult, op1=ALU.add))
        C("dve", lambda e, h=h, sl=sl: e.tensor_scalar(out=biasL[:, h, :], in0=dvals, scalar1=-128.0 * sl, scalar2=col(KPS + h), op0=ALU.mult, op1=ALU.add))
        C("dve", lambda e, h=h, sl=sl: e.tensor_scalar(out=biasR[:, h, :], in0=dvals, scalar1=-128.0 * sl, scalar2=col(KPR + h), op0=ALU.mult, op1=ALU.add))
        for qs in range(4):
            C("act", lambda e, h=h, sl=sl, qs=qs: e.activation(out=col(FL + h * 4 + qs), in_=col(KP), func=AF.Exp, scale=-sl, bias=-sl * 128.0 * qs))
            C("act", lambda e, h=h, sl=sl, qs=qs: e.activation(out=col(FR + h * 4 + qs), in_=col(KP), func=AF.Exp, scale=sl, bias=-sl * (511.0 - 128.0 * qs)))
    for src_, dst_ in ((FL, F8L), (FR, F8R)):
        C("dve", lambda e, src_=src_, dst_=dst_: e.tensor_copy(
            out=col(dst_, 32).rearrange("p (h m q) -> p h m q", h=NH, m=2),
            in_=col(src_, 16).rearrange("p (h q) -> p h q", h=NH).unsqueeze(2).to_broadcast([128, NH, 2, 4])))
    C("dve", lambda e: e.tensor_scalar(out=gffn, in0=gffn, scalar1=32.0, scalar2=None, op0=ALU.mult))
    C("dve", lambda e: e.tensor_scalar(out=gfin, in0=gfin, scalar1=32.0, scalar2=None, op0=ALU.mult))
    C("dve", lambda e: e.tensor_scalar(out=gsub, in0=gsub, scalar1=(1.0 - LAM_INIT) * math.sqrt(128.0), scalar2=None, op0=ALU.mult))
    C("dve", lambda e: e.tensor_tensor(out=lamt[:, 0, :], in0=lamt[:, 0, :], in1=lamt[:, 1, :], op=ALU.mult))
    C("dve", lambda e: e.tensor_tensor(out=lamt[:, 2, :], in0=lamt[:, 2, :], in1=lamt[:, 3, :], op=ALU.mult))
    C("dve", lambda e: e.reduce_sum(out=col(TMP + 1), in_=lamt[:, 0, :], axis=mybir.AxisListType.X))
    C("dve", lambda e: e.reduce_sum(out=col(TMP + 2), in_=lamt[:, 2, :], axis=mybir.AxisListType.X))
    C("act", lambda e: e.activation(out=col(TMP + 1, 2), in_=col(TMP + 1, 2), func=AF.Exp))
    C("dve", lambda e: e.tensor_tensor(out=col(LAM), in0=col(TMP + 1), in1=col(TMP + 2), op=ALU.subtract))
    C("dve", lambda e: e.tensor_scalar(out=col(LAM), in0=col(LAM), scalar1=LAM_INIT, scalar2=None, op0=ALU.add))
    C("dve", lambda e: e.tensor_scalar(out=col(NLAM), in0=col(LAM), scalar1=-1.0, scalar2=None, op0=ALU.mult))
    C("dve", lambda e: e.tensor_scalar(out=col(BR, 16), in0=col(BR, 16), scalar1=0.5, scalar2=None, op0=ALU.mult))
    C("act", lambda e: e.activation(out=col(CH, 8), in_=col(CH, 8), func=AF.Exp, scale=-1.0))
    C("act", lambda e: e.activation(out=col(CH, 8), in_=col(CH, 8), func=AF.Ln, bias=1.0))
    C("dve", lambda e: e.tensor_scalar(out=col(CF, 8), in0=col(CH, 8), scalar1=-8.0, scalar2=None, op0=ALU.mult))
    C("dve", lambda e: e.tensor_scalar(out=col(CH, 8), in0=col(CH, 8), scalar1=-4.0, scalar2=None, op0=ALU.mult))
    C("pool", lambda e: e.memset(bd, 0.0))

    def bdload(e):
        out = []
        for d in range(2):
            for g, src in enumerate((wrg_d, wig_d)):
                for c in range(4):
                    idx = (d * 2 + g) * 4 + c
                    out.append(e.dma_start(out=bd[0:64, idx, 0:64], in_=src[d, 2 * c, :, :]))
                    out.append(e.dma_start(out=bd[64:128, idx, 64:128], in_=src[d, 2 * c + 1, :, :]))
        return out
    P.op("pool", bdload, r=[rc], w=[rc], key="bd", ndma=32)

    P.barrier()
    castw(w_out, w_out_b, D, "w_out_b")
    castgu(w_gate, w_gate_b, "w_gate_b")
    castgu(w_up, w_up_b, "w_up_b")
    castw(w_down, w_down_b, DFF, "w_down_b")

    def rstd_ops(ssq_ap, n, epsn, rr, rw):
        P.op("dve", lambda e: e.tensor_scalar(out=ssq_ap, in0=ssq_ap, scalar1=epsn, scalar2=None, op0=ALU.add), r=rr, w=rw)
        P.op("pool", lambda e: e.tensor_tensor(out=ssq_ap, in0=ssq_ap, in1=col(NHALF).to_broadcast([128, n]), op=ALU.pow), r=rr + [rc], w=rw)

    psB6 = ps[:, 6, :].bitcast(BF16)

    def prep_tile(x_d, t, xt_s, xn_s, junk, ssq_c, dst_ap, rs, gain):
        r_xt, r_xn, r_junk, r_ssq, r_dst = rs
        pB, bB = (psB, bank[7]) if t % 2 == 0 else (psB6, bank[6])
        P.op("sp", lambda e: [e.dma_start(out=xt_s, in_=x_d[t * 128:(t + 1) * 128, :])], w=[r_xt], key=r_xt.name)
        P.op("act", lambda e: e.activation(out=xn_s, in_=xt_s, func=AF.Square, accum_out=ssq_c), r=[r_xt], w=[r_xn, r_ssq])
        P.op("act", lambda e: e.activation(out=ssq_c, in_=ssq_c, func=AF.Sqrt, scale=1.0 / D, bias=col(EPSC)), r=[r_ssq, rc], w=[r_ssq])
        P.op("dve", lambda e: e.reciprocal(out=ssq_c, in_=ssq_c), r=[r_ssq], w=[r_ssq])
        P.op("dve", lambda e: e.scalar_tensor_tensor(out=xn_s, in0=xt_s, scalar=ssq_c, in1=gain, op0=ALU.mult, op1=ALU.mult),
             r=[r_xt, r_ssq, rc], w=[r_xn])

        def tr(e):
            i = None
            for kc in range(8):
                i = e.transpose(out=pB[:, kc * 128:(kc + 1) * 128], in_=xn_s[:, kc * 128:(kc + 1) * 128], identity=ident)
            return i

        def tpart():
            P.op("pe", tr, r=[r_xn, rc], w=[bB])

        def back():
            if t % 2 == 0:
                P.op("act", lambda e: e.activation(out=dst_ap, in_=pB.rearrange("p (a b) -> p a b", a=8), func=AF.Copy), r=[bB], w=[r_dst])
            else:
                P.op("dve", lambda e: e.tensor_copy(out=dst_ap, in_=pB.rearrange("p (a b) -> p a b", a=8)), r=[bB], w=[r_dst])
        return tpart, back

    def mm_group(out_ap, pairs, r, w):
        def fn(e):
            i = None
            n = len(pairs)
            for k, (l, rh) in enumerate(pairs):
                i = e.matmul(out_ap, lhsT=l, rhs=rh, start=(k == 0), stop=(k == n - 1))
            return i
        P.op("pe", fn, r=r, w=w)

    Y0 = CONST_END
    yT, Y1 = V(Y0, [128, 4, SM], BF16)

    def do_seq(si, S):
        x_d, y_d = xs[si], ys[si]
        NT = S // 128
        NB = S // 512

        o = Y1
        xnT, o = V(o, [128, 8, SM], BF16)
        wA, o = V(o, [128, 2, 8, 256], BF16)
        PREP_OFF = o
        xt, o = V(o, [128, 3, D], F32)
        xn, o = V(o, [128, 3, D], BF16)
        junk = None
        bufX, o = V(o, [128, SM + 4], F32)
        xc, o = V(o, [128, SM], F32)
        xcb, o = V(o, [128, SM], BF16)
        tmp, o = V(o, [128, 10, 512], F32)
        tag = "s%dA" % si
        r_xnT = [Res(tag + "xnT%d" % b) for b in range(NB)]
        r_xt = [Res(tag + "xt%d" % i) for i in range(3)]
        r_xn = [Res(tag + "xn%d" % i) for i in range(3)]
        r_junk = Res(tag + "junk")
        r_ssq = [Res(tag + "ssq%d" % i) for i in range(3)]
        r_wA = [Res(tag + "wA%d" % i) for i in range(2)]
        r_bufX, r_xc, r_xcb = Res(tag + "bufX"), Res(tag + "xc"), Res(tag + "xcb")
        r_tmp = [Res(tag + "tmp%d" % i) for i in range(10)]
        r_yT = [res("yT%d" % c) for c in range(4)]
        r_hcar = Res(tag + "hcar")

        pipe = []
        for t in range(NT):
            s = t % 3
            pipe.append(prep_tile(x_d, t, xt[:, s, :], xn[:, s, :], junk, col(TMP + 4 + s),
                                  xnT[:, :, t * 128:(t + 1) * 128], (r_xt[s], r_xn[s], r_junk, r_ssq[s], r_xnT[t // 4]), gmix))
            if t >= 1:
                pipe[t - 1][0]()
            if t >= 2:
                pipe[t - 2][1]()
        pipe[NT - 1][0]()
        if NT >= 2:
            pipe[NT - 2][1]()
        pipe[NT - 1][1]()
        P.op("dve", lambda e: e.memset(bufX[:, 0:2], 0.0), w=[r_bufX])
        set1 = tuple(V(PREP_OFF + k * 4096, [128, 1024], F32)[0] for k in range(3))
        fence = []
        for x_ in r_xt + r_xn + [r_junk]:
            fence += ([x_.w] if x_.w is not None else []) + list(x_.r)
        r_sets = [(Res(tag + "a0"), Res(tag + "m0"), Res(tag + "u0")), (Res(tag + "a1"), Res(tag + "m1"), Res(tag + "u1"))]
        for x_ in r_sets[1]:
            x_.r = list(fence)
        bcount = [0]
        for c in range(4):
            ws = c % 2
            def ldw(e, c=c, ws=ws):
                src = w_in_b.rearrange("(kc p) n -> p kc n", p=128)
                return [e.dma_start(out=wA[:, ws, :, 0:128], in_=src[:, :, 1536 + c * 128:1536 + (c + 1) * 128]),
                        e.dma_start(out=wA[:, ws, :, 128:256], in_=src[:, :, 2048 + c * 128:2048 + (c + 1) * 128])]
            P.op("sp", ldw, r=[res("w_in_b")], w=[r_wA[ws]], key=r_wA[ws].name, ndma=2)
            if c > 0:
                P.op("dve", lambda e: e.memset(bufX[:, 0:2], 0.0), w=[r_bufX])
            P.op("dve", lambda e, S=S: e.memset(bufX[:, S + 2:S + 4], 0.0), w=[r_bufX])
            for b in range(NB):
                pb = bank[b % 2]
                mm_group(ps[:, b % 2, :], [(wA[:, ws, kc, 0:128], xnT[:, kc, b * 512:(b + 1) * 512]) for kc in range(8)],
                         [r_wA[ws], r_xnT[b]], [pb])
                P.op("act", lambda e, b=b: e.activation(out=bufX[:, 2 + b * 512:2 + (b + 1) * 512], in_=ps[:, b % 2, :], func=AF.Copy),
                     r=[pb], w=[r_bufX])
            cw = lambda j, c=c: col(CW + c * 4 + j)
            P.op("dve", lambda e, c=c, S=S, cw=cw: e.tensor_scalar(out=xc[:, 0:S], in0=bufX[:, 0:S], scalar1=cw(0), scalar2=col(CB + c),
                                                               op0=ALU.mult, op1=ALU.add), r=[r_bufX, rc], w=[r_xc])
            for j in range(1, 4):
                P.op("dve", lambda e, j=j, S=S, cw=cw: e.scalar_tensor_tensor(out=xc[:, 0:S], in0=bufX[:, j:j + S], scalar=cw(j), in1=xc[:, 0:S],
                                                                            op0=ALU.mult, op1=ALU.add), r=[r_bufX, r_xc, rc], w=[r_xc])
            P.op("act", lambda e, S=S: e.activation(out=xcb[:, 0:S], in_=xc[:, 0:S], func=AF.Copy), r=[r_xc], w=[r_xcb])
            def gelu_front(b, c=c, ws=ws):
                pb = bank[2 + b % 2]
                pg = ps[:, 2 + b % 2, :]
                blk = slice(b * 512, (b + 1) * 512)
                mm_group(pg, [(wA[:, ws, kc, 128:256], xnT[:, kc, blk]) for kc in range(8)], [r_wA[ws], r_xnT[b]], [pb])
                if b % 2 == 0:
                    t0, t1, rt0, rt1 = tmp[:, 8, :], tmp[:, 9, :], r_tmp[2], r_tmp[3]
                else:
                    t0, t1, rt0, rt1 = tmp[:, 0, :], tmp[:, 2, :], r_sets[0][0], r_sets[0][1]
                P.op("act", lambda e: e.activation(out=t0, in_=pg, func=AF.Square), r=[pb], w=[rt0])
                P.op("dve", lambda e: e.tensor_scalar(out=t0, in0=t0, scalar1=0.044715, scalar2=1.0, op0=ALU.mult, op1=ALU.add),
                     r=[rt0], w=[rt0])
                P.op("dve", lambda e: e.tensor_tensor(out=t0, in0=t0, in1=pg, op=ALU.mult), r=[rt0, pb], w=[rt0])

                def back():
                    P.op("act", lambda e: e.activation(out=t1, in_=t0, func=AF.Tanh, scale=math.sqrt(2.0 / math.pi)),
                         r=[rt0], w=[rt1])
                    P.op("dve", lambda e: e.scalar_tensor_tensor(out=yT[:, c, blk], in0=t1, scalar=1.0, in1=pg,
                                                                op0=ALU.add, op1=ALU.mult), r=[rt1, pb], w=[r_yT[c]])
                return back
            pend = None
            for b in range(NB):
                bk_ = gelu_front(b)
                if pend is not None:
                    pend()
                pend = bk_
            pend()
            sets = [tuple(tmp[:, 2 * k:2 * k + 2, :].rearrange("p a b -> p (a b)") for k in range(3)), set1]
            hb_b = tmp[:, 6:8, :].rearrange("p a b -> p (a b)")
            tr_, ti_ = tmp[:, 8, :], tmp[:, 9, :]
            batches = [list(range(g, min(g + 2, NB))) for g in range(0, NB, 2)]
            items = [(0, n, blks) for n, blks in enumerate(batches)] + [(1, n, blks) for n, blks in enumerate(batches[::-1])]
            hcar = col(TMP + 8)

            def phase12(item, sx, c=c):
                d, n, blks = item
                a_b, m_b, u_b = sets[sx]
                r_a, r_m, r_u = r_sets[sx]
                ir, ii = (d * 2 + 0) * 4 + c, (d * 2 + 1) * 4 + c
                kb = d * 4 + c
                nb = len(blks)
                t0_, t1_ = blks[0] * 512, (blks[-1] + 1) * 512
                L = t1_ - t0_
                for bb, b in enumerate(blks):
                    blk = slice(b * 512, (b + 1) * 512)
                    mm_group(ps[:, 4 + bb, :], [(bd[:, ir, :], xcb[:, blk])], [rc, r_xcb], [bank[4 + bb]])
                    mm_group(ps[:, 2 + bb, :], [(bd[:, ii, :], xcb[:, blk])], [rc, r_xcb], [bank[2 + bb]])
                m3 = m_b[:, 0:L].rearrange("p (a b) -> p a b", a=nb)
                u3 = u_b[:, 0:L].rearrange("p (a b) -> p a b", a=nb)
                P.op("act", lambda e: e.activation(out=m3, in_=ps[:, 4:4 + nb, :], func=AF.Tanh, scale=0.5, bias=col(BR + kb)),
                     r=[bank[4 + bb] for bb in range(nb)] + [rc], w=[r_m])
                P.op("act", lambda e: e.activation(out=a_b[:, 0:L], in_=m_b[:, 0:L], func=AF.Exp, scale=col(CH + kb), bias=col(CH + kb)),
                     r=[r_m, rc], w=[r_a])
                P.op("act", lambda e: e.activation(out=m_b[:, 0:L], in_=m_b[:, 0:L], func=AF.Exp, scale=col(CF + kb), bias=col(CF + kb)),
                     r=[r_m, rc], w=[r_m])
                P.op("act", lambda e: e.activation(out=u3, in_=ps[:, 2:2 + nb, :], func=AF.Tanh, scale=0.5, bias=col(BI + kb)),
                     r=[bank[2 + bb] for bb in range(nb)] + [rc], w=[r_u])
                P.op("dve", lambda e: e.tensor_scalar(out=m_b[:, 0:L], in0=m_b[:, 0:L], scalar1=1.0, scalar2=-1e-12, op0=ALU.subtract, op1=ALU.min),
                     r=[r_m], w=[r_m])
                P.op("dve", lambda e: e.scalar_tensor_tensor(out=u_b[:, 0:L], in0=u_b[:, 0:L], scalar=1.0, in1=xc[:, t0_:t1_], op0=ALU.add, op1=ALU.mult),
                     r=[r_u, r_xc], w=[r_u])
                P.op("act", lambda e: e.activation(out=m_b[:, 0:L], in_=m_b[:, 0:L], func=AF.Sqrt, scale=-1.0), r=[r_m], w=[r_m])

            def phase3(item, sx, c=c):
                d, n, blks = item
                a_b, m_b, u_b = sets[sx]
                r_a, r_m, r_u = r_sets[sx]
                t0_, t1_ = blks[0] * 512, (blks[-1] + 1) * 512
                L = t1_ - t0_
                P.op("dve", lambda e: e.scalar_tensor_tensor(out=u_b[:, 0:L], in0=u_b[:, 0:L], scalar=0.5, in1=m_b[:, 0:L], op0=ALU.mult, op1=ALU.mult),
                     r=[r_u, r_m], w=[r_u])
                if d == 0:
                    init = 0.0 if n == 0 else bufX[:, t0_ - 1:t0_]
                    P.op("dve", lambda e: e.tensor_tensor_scan(out=bufX[:, t0_:t1_], data0=a_b[:, 0:L], data1=u_b[:, 0:L], initial=init,
                                                               op0=ALU.mult, op1=ALU.add), r=[r_a, r_u, r_bufX], w=[r_bufX])
                else:
                    init = 0.0 if n == 0 else hcar
                    hx = n % 2
                    hb_b = hbs[hx]
                    r_hb = r_hbs[hx]
                    P.op("dve", lambda e: e.tensor_tensor_scan(out=hb_b[:, 0:L][:, ::-1], data0=a_b[:, 0:L][:, ::-1], data1=u_b[:, 0:L][:, ::-1],
                                                               initial=init, op0=ALU.mult, op1=ALU.add), r=[r_a, r_u, r_hcar], w=[r_hb])
                    P.op("dve", lambda e: e.tensor_copy(out=hcar, in_=hb_b[:, 0:1]), r=[r_hb], w=[r_hcar])
                    P.op("pool", lambda e: e.tensor_tensor(out=hb_b[:, 0:L], in0=hb_b[:, 0:L], in1=bufX[:, t0_:t1_], op=ALU.add),
                         r=[r_hb, r_bufX, r_hcar], w=[r_hb])
                    for fn_ in pend_y:
                        fn_()
                    pend_y[:] = [lambda: P.op("dve", lambda e: e.scalar_tensor_tensor(out=yT[:, c, t0_:t1_], in0=hb_b[:, 0:L], scalar=0.5, in1=yT[:, c, t0_:t1_],
                                                                                     op0=ALU.mult, op1=ALU.mult), r=[r_hb, r_yT[c]], w=[r_yT[c]])]

            hbs = [tmp[:, 6:8, :].rearrange("p a b -> p (a b)"), tmp[:, 8:10, :].rearrange("p a b -> p (a b)")]
            r_hbs = [r_tmp[8], r_tmp[2]]
            pend_y = []
            sx0 = bcount[0]
            bcount[0] += len(items)
            phase12(items[0], sx0 % 2)
            for i_, item in enumerate(items):
                if i_ + 1 < len(items):
                    phase12(items[i_ + 1], (sx0 + i_ + 1) % 2)
                phase3(item, (sx0 + i_) % 2)
            for fn_ in pend_y:
                fn_()
        P.barrier()

        o = Y1
        qT, o = V(o, [128, NH, SM], BF16)
        kT, o = V(o, [128, NH, SM], BF16)
        Va, o = V(o, [128, NTM, NH, 130], BF16)
        XR = o
        wB, o = V(o, [128, 8, 1536], BF16)
        xt, o = V(o, [128, 2, D], F32)
        xn, o = V(o, [128, 2, D], BF16)
        junk, o = V(o, [128, D], BF16)
        xnb, o = V(o, [128, 2, 8, 512], BF16)
        tag = "s%dB" % si
        r_wB = Res(tag + "wB")
        r_xt = [Res(tag + "xt%d" % i) for i in range(2)]
        r_xn = [Res(tag + "xn%d" % i) for i in range(2)]
        r_junk = Res(tag + "junk")
        r_ssq = [Res(tag + "ssq%d" % i) for i in range(2)]
        r_xnb = [Res(tag + "xnb%d" % i) for i in range(2)]
        r_q, r_k, r_v = Res(tag + "q"), Res(tag + "k"), Res(tag + "v")

        def ldwB(e):
            src = w_in_b.rearrange("(kc p) n -> p kc n", p=128)
            return [e.dma_start(out=wB[:, :, i * 512:(i + 1) * 512], in_=src[:, :, i * 512:(i + 1) * 512]) for i in range(3)]
        P.op("sp", ldwB, r=[res("w_in_b")], w=[r_wB], key=r_wB.name, ndma=3)
        P.op("pool", lambda e, NT=NT: e.memset(Va[:, 0:NT, :, 128:129], 1.0), w=[r_v])
        def b1_A(b, tts):
            bs = b % 2
            out = []
            for tt in tts:
                t = b * 4 + tt
                s = t % 2
                out.append(prep_tile(x_d, t, xt[:, s, :], xn[:, s, :], junk, col(TMP + 4 + s),
                                     xnb[:, bs, :, tt * 128:(tt + 1) * 128], (r_xt[s], r_xn[s], r_junk, r_ssq[s], r_xnb[bs]), gmix))
            return out

        def b1_TC(parts):
            for tp_, _ in parts:
                tp_()
            for _, bk_ in parts:
                bk_()

        def b1_groups(b):
            bs = b % 2
            blk = slice(b * 512, (b + 1) * 512)
            gl = []
            k = 0
            for h in range(NH):
                for which, dst, rr, scale in ((0, qT, r_q, 0.125), (1, kT, r_k, 1.0)):
                    pbi = k % 4
                    k += 1

                    def g(pbi=pbi, which=which, dst=dst, rr=rr, scale=scale, h=h):
                        c0 = which * 512 + h * 128
                        mm_group(ps[:, pbi, :], [(wB[:, kc, c0:c0 + 128], xnb[:, bs, kc, :]) for kc in range(8)], [r_wB, r_xnb[bs]], [bank[pbi]])
                        if which == 0:
                            P.op("act", lambda e: e.activation(out=dst[:, h, blk], in_=ps[:, pbi, :], func=AF.Copy, scale=scale), r=[bank[pbi]], w=[rr])
                        else:
                            P.op("dve", lambda e: e.tensor_copy(out=dst[:, h, blk], in_=ps[:, pbi, :]), r=[bank[pbi]], w=[rr])
                    gl.append(g)
            for tt in range(4):
                def g(tt=tt):
                    t = b * 4 + tt
                    pbi = 4 + tt % 2
                    mm_group(ps[:, pbi, :], [(xnb[:, bs, kc, tt * 128:(tt + 1) * 128], wB[:, kc, 1024:1536]) for kc in range(8)], [r_wB, r_xnb[bs]], [bank[pbi]])
                    P.op("dve", lambda e: e.tensor_copy(out=Va[:, t, :, 0:128], in_=ps[:, pbi, :].rearrange("p (h d) -> p h d", h=NH)),
                         r=[bank[pbi]], w=[r_v])
                gl.append(g)
            return gl

        p01 = b1_A(0, [0, 1])
        b1_TC(p01)
        p23 = b1_A(0, [2, 3])
        b1_TC(p23)
        for b in range(NB):
            gl = b1_groups(b)
            nxt = b + 1 < NB
            if nxt:
                p01 = b1_A(b + 1, [0, 1])
            for g in gl[:6]:
                g()
            if nxt:
                b1_TC(p01)
                p23 = b1_A(b + 1, [2, 3])
            for g in gl[6:]:
                g()
            if nxt:
                b1_TC(p23)
        P.barrier()

        o = XR
        oT, o = V(o, [128, NH, SM], BF16)
        OT_END = o
        Pb, o = V(o, [128, 3, 2, 512], BF16)
        Ssb, o = V(o, [128, 2, 512], F32)
        acc, o = V(o, [128, 8, 130], F32)
        ot, o = V(o, [128, 2, 4, 128], F32)
        onb, o = V(o, [128, 2, 4, 128], BF16)
        rl, o = V(o, [128, 32], F32)
        junkf, o = V(o, [128, 128], F32)
        tag = "s%dC" % si
        r_P = [Res(tag + "P%d" % i) for i in range(3)]
        r_Ssb = [Res(tag + "S%d" % i) for i in range(2)]
        r_accs = [Res(tag + "acc%d" % i) for i in range(8)]
        r_junkb = Res(tag + "jb")
        r_ot = [Res(tag + "ot%d" % i) for i in range(2)]
        r_on = [Res(tag + "on%d" % i) for i in range(2)]
        r_rl = [Res(tag + "rl%d" % i) for i in range(2)]
        r_oT = [res("oT%d" % h) for h in range(NH)]
        psO = ps[:, 4:7, :].rearrange("p a b -> p (a b)")

        def slot_ap(sl):
            bk, j = divmod(sl, 3)
            return ps[:, 4 + bk, j * 130:j * 130 + 129]
        iters = []
        for qb in range(NB):
            jd0 = 4 * qb
            for h in range(NH):
                phases = [(pn, ch) for pn, ch in (("L", list(range(0, jd0))), ("D", list(range(jd0, jd0 + 4))), ("R", list(range(jd0 + 4, NT)))) if ch]
                for pi_, (pn, ch) in enumerate(phases):
                    for idx, j in enumerate(ch):
                        iters.append(dict(qb=qb, h=h, pn=pn, idx=idx, n=len(ch), j=j, first_phase=(pi_ == 0),
                                          last_phase=(pi_ == len(phases) - 1)))
        NI = len(iters)

        def emit_qk_exp(i):
            it = iters[i]
            qb, h, j, pn = it["qb"], it["h"], it["j"], it["pn"]
            sbi, pbi = i % 2, i % 3
            jd0 = 4 * qb
            qblk = slice(qb * 512, (qb + 1) * 512)
            kblk = slice(j * 128, (j + 1) * 128)

            def qk(e):
                e.matmul(ps[:, 2 * sbi, :], lhsT=kT[0:64, h, kblk], rhs=qT[0:64, h, qblk], start=True, stop=True)
                return e.matmul(ps[:, 2 * sbi + 1, :], lhsT=kT[64:128, h, kblk], rhs=qT[64:128, h, qblk], start=True, stop=True)
            P.op("pe", qk, r=[r_q, r_k], w=[bank[2 * sbi], bank[2 * sbi + 1]])
            src2 = ps[:, 2 * sbi:2 * sbi + 2, :]
            if pn == "D":
                off = 384 - 128 * (j - jd0)
                P.op("dve", lambda e, off=off: e.scalar_tensor_tensor(out=src2, in0=Dt[:, off:off + 512].unsqueeze(1).to_broadcast([128, 2, 512]), scalar=-SLOPES[h],
                                                                    in1=src2, op0=ALU.mult, op1=ALU.add),
                     r=[bank[2 * sbi], bank[2 * sbi + 1], rc], w=[bank[2 * sbi], bank[2 * sbi + 1]])
                P.op("act", lambda e: e.activation(out=Pb[:, pbi, :, :], in_=src2, func=AF.Exp), r=[bank[2 * sbi], bank[2 * sbi + 1]], w=[r_P[pbi]])
            else:
                bias = biasL[:, h, jd0 - j:jd0 - j + 1] if pn == "L" else biasR[:, h, j - jd0 - 4:j - jd0 - 3]
                P.op("act", lambda e, bias=bias: e.activation(out=Pb[:, pbi, :, :], in_=src2, func=AF.Exp, bias=bias),
                     r=[bank[2 * sbi], bank[2 * sbi + 1], rc], w=[r_P[pbi]])

        deferred = []

        def flush(parity=None, tick=False):
            keep = []
            for ent in deferred:
                if tick:
                    ent[0] -= 1
                if ent[0] <= 0 or (parity is not None and ent[1] == parity) or (parity == -1):
                    ent[2]()
                else:
                    keep.append(ent)
            deferred[:] = keep

        fin_count = [0]

        def emit_pv(i):
            it = iters[i]
            qb, h, j, pn, idx, n = it["qb"], it["h"], it["j"], it["pn"], it["idx"], it["n"]
            pbi = i % 3
            qblk = slice(qb * 512, (qb + 1) * 512)
            for bk in range(3):
                sls = [sl for sl in range(8) if sl // 3 == bk]

                def pv(e, sls=sls):
                    ins = None
                    for sl in sls:
                        m, qs = divmod(sl, 4)
                        ins = e.matmul(slot_ap(sl), lhsT=Pb[:, pbi, m, qs * 128:(qs + 1) * 128], rhs=Va[:, j, h, 0:129],
                                       start=(idx == 0 and sl % 3 == 0), stop=(idx == n - 1), skip_group_check=True)
                    return ins
                P.op("pe", pv, r=[r_P[pbi], r_v], w=[bank[4 + bk]])
                if idx == n - 1:
                    fp = it["first_phase"]
                    nsl = len(sls)
                    pview = ps[:, 4 + bk, 0:nsl * 130].rearrange("p (s c) -> p s c", c=130)[:, :, 0:129]
                    aview = acc[:, sls[0]:sls[0] + nsl, 0:129]
                    racc = [r_accs[sl] for sl in sls]
                    if pn == "D":
                        if fp:
                            P.op("dve", lambda e, pview=pview, aview=aview: e.tensor_copy(out=aview, in_=pview), r=[bank[4 + bk]], w=racc)
                        else:
                            P.op("dve", lambda e, pview=pview, aview=aview: e.tensor_tensor(out=aview, in0=aview, in1=pview, op=ALU.add),
                                 r=[bank[4 + bk]] + racc, w=racc)
                    elif fp:
                        ftab = col((F8L if pn == "L" else F8R) + h * 8 + sls[0], nsl).unsqueeze(2).to_broadcast([128, nsl, 129])
                        P.op("dve", lambda e, pview=pview, aview=aview, ftab=ftab: e.tensor_tensor(out=aview, in0=pview, in1=ftab, op=ALU.mult),
                             r=[bank[4 + bk], rc], w=racc)
                    else:
                        for sl in sls:
                            f = col((FL if pn == "L" else FR) + h * 4 + sl % 4)
                            P.op("dve", lambda e, sl=sl, f=f: e.scalar_tensor_tensor(out=acc[:, sl, 0:129], in0=slot_ap(sl), scalar=f, in1=acc[:, sl, 0:129],
                                                                                    op0=ALU.mult, op1=ALU.add), r=[bank[4 + bk], r_accs[sl], rc], w=[r_accs[sl]])
            if idx == n - 1 and it["last_phase"]:
                par = fin_count[0] % 2
                fin_count[0] += 1
                flush(parity=par)
                otp, onp, rlp = ot[:, par], onb[:, par], rl[:, par * 16:(par + 1) * 16]
                P.op("dve", lambda e: e.reciprocal(out=rlp[:, 0:8], in_=acc[:, :, 128]), r=r_accs, w=[r_rl[par]])
                P.op("dve", lambda e: e.tensor_scalar(out=rlp[:, 4:8], in0=rlp[:, 4:8], scalar1=col(NLAM), scalar2=None, op0=ALU.mult), r=[r_rl[par], rc], w=[r_rl[par]])
                for qs in range(4):
                    P.op("dve", lambda e, qs=qs: e.tensor_scalar(out=otp[:, qs, :], in0=acc[:, 4 + qs, 0:128], scalar1=rlp[:, 4 + qs:5 + qs], scalar2=None, op0=ALU.mult),
                         r=[r_accs[4 + qs], r_rl[par]], w=[r_ot[par]])
                    P.op("dve", lambda e, qs=qs: e.scalar_tensor_tensor(out=otp[:, qs, :], in0=acc[:, qs, 0:128], scalar=rlp[:, qs:qs + 1], in1=otp[:, qs, :],
                                                                       op0=ALU.mult, op1=ALU.add), r=[r_accs[qs], r_rl[par], r_ot[par]], w=[r_ot[par]])
                for qs in range(4):
                    P.op("dve", lambda e, qs=qs: e.scalar_tensor_tensor(out=junkf, in0=otp[:, qs, :], scalar=1.0, in1=otp[:, qs, :], op0=ALU.mult, op1=ALU.mult,
                                                                       accum_out=rlp[:, 8 + qs:9 + qs]), r=[r_ot[par]], w=[r_junkb, r_rl[par]])
                rstd_ops(rlp[:, 8:12], 4, 128.0 * EPS, [r_rl[par]], [r_rl[par]])
                for qs in range(4):
                    P.op("dve", lambda e, qs=qs: e.scalar_tensor_tensor(out=onp[:, qs, :], in0=otp[:, qs, :], scalar=rlp[:, 8 + qs:9 + qs], in1=gsub,
                                                                       op0=ALU.mult, op1=ALU.mult), r=[r_ot[par], r_rl[par], rc], w=[r_on[par]])

                def fin(par=par, h=h, qblk=qblk, onp=onp):
                    def tro(e):
                        ins = None
                        for qs in range(4):
                            ins = e.transpose(out=psB[:, qs * 128:(qs + 1) * 128], in_=onp[:, qs, :], identity=ident)
                        return ins
                    P.op("pe", tro, r=[r_on[par], rc], w=[bank[7]])
                    P.op("act", lambda e: e.activation(out=oT[:, h, qblk], in_=psB[:, 0:512], func=AF.Copy), r=[bank[7]], w=[r_oT[h]])
                deferred.append([10, par, fin])

        emit_qk_exp(0)
        for i in range(NI):
            if i + 1 < NI:
                emit_qk_exp(i + 1)
            emit_pv(i)
            flush(tick=True)
        flush(parity=-1)
        P.barrier()

        o = Y1
        wo, o = V(o, [128, 8, D], BF16)
        wd, o = V(o, [128, 11, D], BF16)
        hT, o = V(o, [128, 2, 11, 512], BF16)
        x1, o = V(o, [128, 4, D], F32)
        h2T, o = V(o, [128, 8, 512], BF16)
        wgu, o = V(o, [128, 2, 2, 8, 128], BF16)
        assert o <= XR, (o, XR)
        o = OT_END
        xt, o = V(o, [128, 2, D], F32)
        h2n, o = V(o, [128, 2, D], BF16)
        tw, o = V(o, [128, 4, 512], F32)
        tag = "s%dD" % si
        r_wo, r_wd = Res(tag + "wo"), Res(tag + "wd")
        r_hT = [Res(tag + "hT%d" % i) for i in range(2)]
        r_x1 = [Res(tag + "x1_%d" % i) for i in range(4)]
        r_h2T = Res(tag + "h2T")
        r_wgu = [Res(tag + "wgu%d" % i) for i in range(2)]
        r_xt = [Res(tag + "xt%d" % i) for i in range(2)]
        r_h2n, r_junk = Res(tag + "h2n"), Res(tag + "junk")
        r_ssq = [Res(tag + "ssq%d" % i) for i in range(2)]
        r_tw = [Res(tag + "tw%d" % i) for i in range(4)]

        P.op("sp", lambda e: [e.dma_start(out=wo, in_=w_out_b.rearrange("(kc p) n -> p kc n", p=128))], r=[res("w_out_b")], w=[r_wo], key=r_wo.name)
        gcount = 0
        h2n2 = h2n
        sqo = tw[:, 0:2, :].rearrange("p a b -> p (a b)")
        r_h2ns = [Res(tag + "h2n%d" % i) for i in range(2)]
        r_ssq4 = [Res(tag + "ssq4_%d" % i) for i in range(4)]

        def c_front(b, tt):
            t = b * 4 + tt
            s = t % 2
            tok = slice(t * 128, (t + 1) * 128)
            P.op("sp", lambda e: [e.dma_start(out=xt[:, s, :], in_=x_d[tok, :])], w=[r_xt[s]], key=r_xt[s].name)
            for hf in range(2):
                cs = slice(hf * 512, (hf + 1) * 512)
                pairs = [((oT[:, kc, tok] if kc < 4 else yT[:, kc - 4, tok]), wo[:, kc, cs]) for kc in range(8)]
                mm_group(ps[:, hf, :], pairs, r_oT + r_yT + [r_wo], [bank[hf]])
                P.op("dve", lambda e, hf=hf, cs=cs: e.tensor_tensor(out=x1[:, tt, cs], in0=ps[:, hf, :], in1=xt[:, s, cs], op=ALU.add),
                     r=[bank[hf], r_xt[s]], w=[r_x1[tt]])
            ssq_c = col(TMP + 10 + tt)
            P.op("act", lambda e: e.activation(out=sqo, in_=x1[:, tt, :], func=AF.Square, accum_out=ssq_c), r=[r_x1[tt]], w=[r_tw[0], r_tw[1], r_ssq4[tt]])
            rstd_ops(ssq_c, 1, D * EPS, [r_ssq4[tt]], [r_ssq4[tt]])
            P.op("dve", lambda e: e.scalar_tensor_tensor(out=h2n2[:, tt % 2, :], in0=x1[:, tt, :], scalar=ssq_c, in1=gffn, op0=ALU.mult, op1=ALU.mult),
                 r=[r_x1[tt], r_ssq4[tt], rc], w=[r_h2ns[tt % 2]])

        def c_back(b, tt):
            def tr2(e):
                i = None
                for kc in range(8):
                    i = e.transpose(out=psB[:, kc * 128:(kc + 1) * 128], in_=h2n2[:, tt % 2, kc * 128:(kc + 1) * 128], identity=ident)
                return i
            P.op("pe", tr2, r=[r_h2ns[tt % 2], rc], w=[bank[7]])
            P.op("act", lambda e: e.activation(out=h2T[:, :, tt * 128:(tt + 1) * 128], in_=psB.rearrange("p (a b) -> p a b", a=8), func=AF.Copy),
                 r=[bank[7]], w=[r_h2T])

        def c_gu(fc):
            nonlocal_g = gstate
            gs = nonlocal_g[0] % 2
            nonlocal_g[0] += 1
            part, fi = divmod(fc, 11)

            def ldgu(e):
                return [e.dma_start(out=wgu[:, gs, 0, :, :], in_=w_gate_b[fc]),
                        e.dma_start(out=wgu[:, gs, 1, :, :], in_=w_up_b[fc])]
            P.op("sp", ldgu, r=[res("w_gate_b"), res("w_up_b")], w=[r_wgu[gs]], key=r_wgu[gs].name, ndma=2)
            bg, bu = 2 + 2 * gs, 3 + 2 * gs
            mm_group(ps[:, bg, :], [(wgu[:, gs, 0, kc, :], h2T[:, kc, :]) for kc in range(8)], [r_wgu[gs], r_h2T], [bank[bg]])
            mm_group(ps[:, bu, :], [(wgu[:, gs, 1, kc, :], h2T[:, kc, :]) for kc in range(8)], [r_wgu[gs], r_h2T], [bank[bu]])
            th, ww = tw[:, 2 * gs, :], tw[:, 2 * gs + 1, :]
            P.op("act", lambda e: e.activation(out=th, in_=ps[:, bg, :], func=AF.Tanh, scale=0.5), r=[bank[bg]], w=[r_tw[2 * gs]])
            P.op("dve", lambda e: e.scalar_tensor_tensor(out=ww, in0=th, scalar=1.0, in1=ps[:, bg, :], op0=ALU.add, op1=ALU.mult),
                 r=[r_tw[2 * gs], bank[bg]], w=[r_tw[2 * gs + 1]])
            P.op("dve", lambda e: e.scalar_tensor_tensor(out=hT[:, part, fi, :], in0=ww, scalar=0.5, in1=ps[:, bu, :],
                                                        op0=ALU.mult, op1=ALU.mult), r=[r_tw[2 * gs + 1], bank[bu]], w=[r_hT[part]])

        def c_wd(part):
            P.op("pool", lambda e: [e.dma_start(out=wd, in_=w_down_b[part * 1408:(part + 1) * 1408, :].rearrange("(fc p) n -> p fc n", p=128))],
                 r=[res("w_down_b")], w=[r_wd], key=r_wd.name)

        def c_down(part):
            for tt in range(4):
                for hf in range(2):
                    cs = slice(hf * 512, (hf + 1) * 512)
                    mm_group(ps[:, hf, :], [(hT[:, part, fi, tt * 128:(tt + 1) * 128], wd[:, fi, cs]) for fi in range(11)], [r_hT[part], r_wd], [bank[hf]])
                    P.op("dve", lambda e, hf=hf, tt=tt, cs=cs: e.tensor_tensor(out=x1[:, tt, cs], in0=x1[:, tt, cs], in1=ps[:, hf, :], op=ALU.add),
                         r=[bank[hf], r_x1[tt]], w=[r_x1[tt]])

        def c_final(b, tt):
            t = b * 4 + tt
            tok = slice(t * 128, (t + 1) * 128)
            ssq_c = col(TMP + 14 + tt)
            P.op("act", lambda e: e.activation(out=sqo, in_=x1[:, tt, :], func=AF.Square, accum_out=ssq_c), r=[r_x1[tt]], w=[r_tw[0], r_tw[1], r_ssq4[tt]])
            rstd_ops(ssq_c, 1, D * EPS, [r_ssq4[tt]], [r_ssq4[tt]])
            P.op("dve", lambda e: e.scalar_tensor_tensor(out=x1[:, tt, :], in0=x1[:, tt, :], scalar=ssq_c, in1=gfin, op0=ALU.mult, op1=ALU.mult),
                 r=[r_x1[tt], r_ssq4[tt], rc], w=[r_x1[tt]])
            P.op("pool", lambda e: [e.dma_start(out=y_d[tok, :], in_=x1[:, tt, :])], r=[r_x1[tt]], key="st" + r_x1[tt].name)

        gstate = [0]
        for b in range(NB):
            for tt in range(4):
                c_front(b, tt)
                if tt >= 1:
                    c_back(b, tt - 1)
            c_back(b, 3)
            c_wd(0)
            for fc in range(12):
                c_gu(fc)
            c_down(0)
            c_wd(1)
            for fc in range(12, 22):
                c_gu(fc)
            c_down(1)
            for tt in range(4):
                c_final(b, tt)
        P.barrier()

    for si_, S_ in enumerate(seqs):
        do_seq(si_, S_)
    P.emit(nc)
    es.close()
    return nc


_NC_CACHE = {}


def _common_inputs(inp):
    f = lambda a: np.ascontiguousarray(np.asarray(a, dtype=np.float32))
    pc = lambda a: f(np.asarray(a).reshape(4, 128).T)
    pdc = lambda a: f(np.asarray(a).reshape(2, 4, 128).transpose(2, 0, 1).reshape(128, 8))
    return {
        "w_in": f(inp["w_in"][0]), "w_out": f(inp["w_out"][0]), "w_gate": f(inp["w_gate"][0]),
        "w_up": f(inp["w_up"][0]), "w_down": f(inp["w_down"][0]),
        "vec1024": f(np.stack([inp["norm_mix"][0], inp["norm_ffn"][0], inp["norm_final"]])),
        "convw": f(np.asarray(inp["conv_w"][0]).T.reshape(4, 128, 4).transpose(1, 0, 2).reshape(128, 16)),
        "convb": pc(inp["conv_b"][0]),
        "brg": pdc(inp["b_rg"][0]), "big": pdc(inp["b_ig"][0]), "lrul": pdc(inp["lru_lambda"][0]),
        "wrg": f(inp["w_rg"][0]), "wig": f(inp["w_ig"][0]),
        "lamv": f(np.stack([inp["lambda_q1"][0], inp["lambda_k1"][0], inp["lambda_q2"][0], inp["lambda_k2"][0]])),
        "subg": f(np.asarray(inp["subln_g"][0]).reshape(1, 128)),
    }


def kernel(**inp):
    xp = np.asarray(inp["x_prompt"], dtype=np.float32)
    xsm = np.asarray(inp["x_sample"], dtype=np.float32)
    seqs = (xp.shape[1], xp.shape[1], xsm.shape[1])
    if seqs not in _NC_CACHE:
        _NC_CACHE[seqs] = _build(list(seqs))
    nc = _NC_CACHE[seqs]
    common = _common_inputs(inp)
    in_maps = []
    for c in range(8):
        m = dict(common)
        m["x0"] = np.ascontiguousarray(xp[2 * c])
        m["x1"] = np.ascontiguousarray(xp[2 * c + 1])
        m["x2"] = np.ascontiguousarray(xsm[c])
        in_maps.append(m)
    res = run_bass_kernel_spmd(nc, in_maps, core_ids=list(range(8)))
    yp = np.empty_like(xp)
    ysm = np.empty_like(xsm)
    for c in range(8):
        r = res.results[c]
        yp[2 * c] = r["y0"]
        yp[2 * c + 1] = r["y1"]
        ysm[c] = r["y2"]
    return yp, ysm
```

```python
import math
from contextlib import ExitStack

import numpy as np
import concourse.bass as bass
import concourse.mybir as mybir
from concourse.bass_utils import run_bass_kernel_spmd

F32 = mybir.dt.float32
BF16 = mybir.dt.bfloat16
U8 = mybir.dt.uint8
I32 = mybir.dt.int32
ALU = mybir.AluOpType
AF = mybir.ActivationFunctionType

D = 1024
DFF = 2816
NH = 4
INW = 2560
EPS = 1e-6
LAM_INIT = 0.8 - 0.6 * math.exp(0.0)
SLOPES = [2.0 ** (-8.0 * (h + 1) / NH) for h in range(NH)]
NFC = DFF // 128
ENGS = ("pe", "act", "dve", "pool", "sp")


class Res:
    __slots__ = ("name", "w", "r")

    def __init__(self, name):
        self.name = name
        self.w = None
        self.r = []


class Op:
    __slots__ = ("eng", "fn", "deps", "sig", "cnt", "key")


class Prog:
    def __init__(self):
        self.ops = {e: [] for e in ENGS}
        self.keycnt = {}
        self.dma_since_bar = []

    def op(self, eng, fn, r=(), w=(), key=None, ndma=1):
        o = Op()
        o.eng, o.fn, o.sig, o.cnt, o.key = eng, fn, False, 0, key
        deps = {}
        for x in r:
            if x.w is not None:
                deps[id(x.w)] = (x.w, True)
        for x in w:
            if x.w is not None and id(x.w) not in deps:
                deps[id(x.w)] = (x.w, True)
            for q in x.r:
                if id(q) not in deps:
                    deps[id(q)] = (q, False)
        o.deps = list(deps.values())
        for d, raw in o.deps:
            if d.key is None and (d.eng != eng or (raw and eng in ("act", "dve", "pool"))):
                d.sig = True
        for x in r:
            if key is None:
                x.r = [q for q in x.r if not (q.key is None and q.eng == eng)]
            x.r.append(o)
        for x in w:
            x.w = o
            x.r = []
        if key is not None:
            self.keycnt[key] = self.keycnt.get(key, 0) + 16 * ndma
            o.cnt = self.keycnt[key]
            self.dma_since_bar.append(o)
        self.ops[eng].append(o)
        return o

    def barrier(self):
        lasts = []
        for e in ENGS:
            for o in reversed(self.ops[e]):
                if o.key is None and o.fn is not None:
                    lasts.append(o)
                    o.sig = True
                    break
        lasts += self.dma_since_bar
        self.dma_since_bar = []
        for e in ENGS:
            o = Op()
            o.eng, o.fn, o.sig, o.cnt, o.key = e, None, False, 0, None
            o.deps = [(l, True) for l in lasts]
            self.ops[e].append(o)

    def emit(self, nc):
        with ExitStack() as es:
            sem = {e: es.enter_context(nc.semaphore("s_" + e)) for e in ("pe", "act", "dve", "pool")}
            keysem = {k: es.enter_context(nc.semaphore("k_%d" % i)) for i, k in enumerate(self.keycnt)}
            for e in ("pe", "act", "dve", "pool"):
                c = 0
                for o in self.ops[e]:
                    if o.key is None and o.sig and o.fn is not None:
                        c += 1
                        o.cnt = c
            block = es.enter_context(nc.Block())
            prog = self

            def run(engobj, E):
                waited = {}
                for o in prog.ops[E]:
                    for d, raw in o.deps:
                        if d.key is None:
                            if d.eng == E and (E == "pe" or not raw):
                                continue
                            sh, val = sem[d.eng], d.cnt
                        else:
                            sh, val = keysem[d.key], d.cnt
                        if waited.get(id(sh), 0) >= val:
                            continue
                        engobj.wait_ge(sh, val)
                        waited[id(sh)] = val
                    if o.fn is None:
                        continue
                    ins = o.fn(engobj)
                    if o.key is not None:
                        for i in ins:
                            i.then_inc(keysem[o.key], 16)
                    elif o.sig:
                        ins.then_inc(sem[E], 1)

            @block.tensor
            def _(e):
                run(e, "pe")

            @block.scalar
            def _(e):
                run(e, "act")

            @block.vector
            def _(e):
                run(e, "dve")

            @block.gpsimd
            def _(e):
                run(e, "pool")

            @block.sync
            def _(e):
                run(e, "sp")


def _build(seqs):
    SM = 4096
    assert max(seqs) <= SM
    NTM = SM // 128
    nc = bass.Bass("TRN2", target_bir_lowering=False)
    P = Prog()

    def din(name, shape, dt=F32):
        return nc.dram_tensor(name, list(shape), dt, kind="ExternalInput").ap()

    xs = [din("x%d" % i, [S, D]) for i, S in enumerate(seqs)]
    ys = [nc.dram_tensor("y%d" % i, [S, D], F32, kind="ExternalOutput").ap() for i, S in enumerate(seqs)]
    w_in = din("w_in", [D, INW])
    w_out = din("w_out", [D, D])
    w_gate = din("w_gate", [D, DFF])
    w_up = din("w_up", [D, DFF])
    w_down = din("w_down", [DFF, D])
    vec1024 = din("vec1024", [3, D])
    convw_d = din("convw", [128, 16])
    convb_d = din("convb", [128, 4])
    brg_d = din("brg", [128, 8])
    big_d = din("big", [128, 8])
    lru_d = din("lrul", [128, 8])
    wrg_d = din("wrg", [2, 8, 64, 64])
    wig_d = din("wig", [2, 8, 64, 64])
    lamv_d = din("lamv", [4, 64])
    subg_d = din("subg", [1, 128])

    def dint(name, shape):
        return nc.dram_tensor(name, list(shape), BF16, kind="Internal").ap()

    w_in_b = dint("w_in_b", [D, INW])
    w_out_b = dint("w_out_b", [D, D])
    w_gate_b = dint("w_gate_b", [NFC, 128, 8, 128])
    w_up_b = dint("w_up_b", [NFC, 128, 8, 128])
    w_down_b = dint("w_down_b", [DFF, D])

    es = ExitStack()
    sb = es.enter_context(nc.sbuf_tensor("sb", [128, 212000], U8))
    ps = es.enter_context(nc.psum_tensor("ps", [128, 8, 512], F32))
    psB = ps[:, 7, :].bitcast(BF16)

    def V(off, shape, dt):
        n = 1
        for s in shape[1:]:
            n *= s
        esz = 4 if dt in (F32, I32) else 2
        assert off % 4 == 0 and off + n * esz <= 212000, (off, shape)
        v = sb[:, off:off + n * esz].bitcast(dt)
        if len(shape) > 2:
            names = "abcde"[:len(shape) - 1]
            kw = {names[i]: shape[i + 1] for i in range(len(shape) - 2)}
            v = v.rearrange("p (%s) -> p %s" % (" ".join(names), " ".join(names)), **kw)
        return v, off + n * esz

    o = 0
    ident, o = V(o, [128, 128], BF16)
    gmix, o = V(o, [128, D], F32)
    gffn, o = V(o, [128, D], F32)
    gfin, o = V(o, [128, D], F32)
    gsub, o = V(o, [128, 128], F32)
    Dt, o = V(o, [128, 896], F32)
    sm, o = V(o, [128, 256], F32)
    biasL, o = V(o, [128, NH, 36], F32)
    biasR, o = V(o, [128, NH, 36], F32)
    dvals, o = V(o, [128, 36], F32)
    bd, o = V(o, [128, 16, 128], BF16)
    lamt, o = V(o, [128, 4, 64], F32)
    identf, o = V(o, [128, 128], F32)
    CONST_END = (o + 63) // 64 * 64
    CW, CB, BR, BI, CH, KP = 0, 16, 20, 28, 36, 44
    NHALF, PHALF, LAM, NLAM = 45, 46, 47, 48
    FL, FR = 52, 68
    KPS, KPR = 84, 88
    EPSC = 92
    F8L, F8R = 128, 160
    CF = 120
    TMP = 96

    def col(c, n=1):
        return sm[:, c:c + n]

    R = {}

    def res(name):
        if name not in R:
            R[name] = Res(name)
        return R[name]

    rc = res("const")
    bank = [res("bank%d" % i) for i in range(8)]

    def castw(src, dst, rows, name):
        def fn(e):
            out = []
            for r0 in range(0, rows, 128):
                out.append(e.dma_start(out=dst[r0:r0 + 128, :], in_=src[r0:r0 + 128, :]))
            return out
        P.op("pool", fn, w=[res(name)], key=name, ndma=rows // 128)

    castw(w_in, w_in_b, D, "w_in_b")
    def castgu(src, dst, name):
        def fn(e):
            out = []
            for kc in range(8):
                out.append(e.dma_start(out=dst[:, :, kc, :], in_=src[kc * 128:(kc + 1) * 128, :].rearrange("p (fc n) -> fc p n", n=128)))
            return out
        P.op("pool", fn, w=[res(name)], key=name, ndma=8)


    def dma(eng, out, in_, r, w, key, n=1):
        P.op(eng, lambda e: [e.dma_start(out=out, in_=in_)], r=r, w=w, key=key)

    dma("sp", gmix, vec1024[0:1, :].partition_broadcast(128), [], [rc], "c0")
    dma("sp", gffn, vec1024[1:2, :].partition_broadcast(128), [], [rc], "c1")
    dma("sp", gfin, vec1024[2:3, :].partition_broadcast(128), [], [rc], "c2")
    dma("sp", gsub, subg_d[0:1, :].partition_broadcast(128), [], [rc], "c3")
    dma("sp", col(CW, 16), convw_d[:, :], [], [rc], "c4")
    dma("sp", col(CB, 4), convb_d[:, :], [], [rc], "c5")
    dma("sp", col(BR, 8), brg_d[:, :], [], [rc], "c6")
    dma("sp", col(BI, 8), big_d[:, :], [], [rc], "c7")
    dma("sp", col(CH, 8), lru_d[:, :], [], [rc], "c8")
    for i in range(4):
        dma("sp", lamt[:, i, :], lamv_d[i:i + 1, :].partition_broadcast(128), [], [rc], "c9_%d" % i)

    def C(eng, fn, r=None, w=None):
        P.op(eng, fn, r=[rc] if r is None else r, w=[rc] if w is None else w)

    C("pool", lambda e: e.iota(identf.bitcast(I32), pattern=[[1, 128]], base=0, channel_multiplier=-1))
    C("dve", lambda e: e.tensor_copy(out=Dt[:, 0:128], in_=identf.bitcast(I32)))
    C("dve", lambda e: e.tensor_single_scalar(out=identf, in_=Dt[:, 0:128], scalar=0.0, op=ALU.is_equal))
    C("dve", lambda e: e.tensor_copy(out=ident, in_=identf))
    C("pool", lambda e: e.iota(Dt.bitcast(I32), pattern=[[1, 896]], base=-384, channel_multiplier=-1))
    C("dve", lambda e: e.tensor_copy(out=Dt, in_=Dt.bitcast(I32)))
    C("act", lambda e: e.activation(out=Dt, in_=Dt, func=AF.Abs))
    C("pool", lambda e: e.iota(col(TMP).bitcast(I32), pattern=[[1, 1]], base=0, channel_multiplier=1))
    C("dve", lambda e: e.tensor_copy(out=col(KP), in_=col(TMP).bitcast(I32)))
    C("pool", lambda e: e.iota(dvals.bitcast(I32), pattern=[[1, 36]], base=0, channel_multiplier=0))
    C("dve", lambda e: e.tensor_copy(out=dvals, in_=dvals.bitcast(I32)))
    C("dve", lambda e: e.memset(col(NHALF), -0.5))
    C("dve", lambda e: e.memset(col(PHALF), 0.5))
    C("dve", lambda e: e.memset(col(EPSC), EPS))
    for h in range(NH):
        sl = SLOPES[h]
        C("dve", lambda e, h=h, sl=sl: e.tensor_scalar(out=col(KPS + h), in0=col(KP), scalar1=sl, scalar2=None, op0=ALU.mult))
        C("dve", lambda e, h=h, sl=sl: e.tensor_scalar(out=col(KPR + h), in0=col(KP), scalar1=-sl, scalar2=-sl, op0=ALU.mult, op1=ALU.add))
        C("dve", lambda e, h=h, sl=sl: e.tensor_scalar(out=biasL[:, h, :], in0=dvals, scalar1=-128.0 * sl, scalar2=col(KPS + h), op0=ALU.mult, op1=ALU.add))
        C("dve", lambda e, h=h, sl=sl: e.tensor_scalar(out=biasR[:, h, :], in0=dvals, scalar1=-128.0 * sl, scalar2=col(KPR + h), op0=ALU.mult, op1=ALU.add))
        for qs in range(4):
            C("act", lambda e, h=h, sl=sl, qs=qs: e.activation(out=col(FL + h * 4 + qs), in_=col(KP), func=AF.Exp, scale=-sl, bias=-sl * 128.0 * qs))
            C("act", lambda e, h=h, sl=sl, qs=qs: e.activation(out=col(FR + h * 4 + qs), in_=col(KP), func=AF.Exp, scale=sl, bias=-sl * (511.0 - 128.0 * qs)))
    for src_, dst_ in ((FL, F8L), (FR, F8R)):
        C("dve", lambda e, src_=src_, dst_=dst_: e.tensor_copy(
            out=col(dst_, 32).rearrange("p (h m q) -> p h m q", h=NH, m=2),
            in_=col(src_, 16).rearrange("p (h q) -> p h q", h=NH).unsqueeze(2).to_broadcast([128, NH, 2, 4])))
    C("dve", lambda e: e.tensor_scalar(out=gffn, in0=gffn, scalar1=32.0, scalar2=None, op0=ALU.mult))
    C("dve", lambda e: e.tensor_scalar(out=gfin, in0=gfin, scalar1=32.0, scalar2=None, op0=ALU.mult))
    C("dve", lambda e: e.tensor_scalar(out=gsub, in0=gsub, scalar1=(1.0 - LAM_INIT) * math.sqrt(128.0), scalar2=None, op0=ALU.mult))
    C("dve", lambda e: e.tensor_tensor(out=lamt[:, 0, :], in0=lamt[:, 0, :], in1=lamt[:, 1, :], op=ALU.mult))
    C("dve", lambda e: e.tensor_tensor(out=lamt[:, 2, :], in0=lamt[:, 2, :], in1=lamt[:, 3, :], op=ALU.mult))
    C("dve", lambda e: e.reduce_sum(out=col(TMP + 1), in_=lamt[:, 0, :], axis=mybir.AxisListType.X))
    C("dve", lambda e: e.reduce_sum(out=col(TMP + 2), in_=lamt[:, 2, :], axis=mybir.AxisListType.X))
    C("act", lambda e: e.activation(out=col(TMP + 1, 2), in_=col(TMP + 1, 2), func=AF.Exp))
    C("dve", lambda e: e.tensor_tensor(out=col(LAM), in0=col(TMP + 1), in1=col(TMP + 2), op=ALU.subtract))
    C("dve", lambda e: e.tensor_scalar(out=col(LAM), in0=col(LAM), scalar1=LAM_INIT, scalar2=None, op0=ALU.add))
    C("dve", lambda e: e.tensor_scalar(out=col(NLAM), in0=col(LAM), scalar1=-1.0, scalar2=None, op0=ALU.mult))
    C("dve", lambda e: e.tensor_scalar(out=col(BR, 16), in0=col(BR, 16), scalar1=0.5, scalar2=None, op0=ALU.mult))
    C("act", lambda e: e.activation(out=col(CH, 8), in_=col(CH, 8), func=AF.Exp, scale=-1.0))
    C("act", lambda e: e.activation(out=col(CH, 8), in_=col(CH, 8), func=AF.Ln, bias=1.0))
    C("dve", lambda e: e.tensor_scalar(out=col(CF, 8), in0=col(CH, 8), scalar1=-8.0, scalar2=None, op0=ALU.mult))
    C("dve", lambda e: e.tensor_scalar(out=col(CH, 8), in0=col(CH, 8), scalar1=-4.0, scalar2=None, op0=ALU.mult))
    C("pool", lambda e: e.memset(bd, 0.0))

    def bdload(e):
        out = []
        for d in range(2):
            for g, src in enumerate((wrg_d, wig_d)):
                for c in range(4):
                    idx = (d * 2 + g) * 4 + c
                    out.append(e.dma_start(out=bd[0:64, idx, 0:64], in_=src[d, 2 * c, :, :]))
                    out.append(e.dma_start(out=bd[64:128, idx, 64:128], in_=src[d, 2 * c + 1, :, :]))
        return out
    P.op("pool", bdload, r=[rc], w=[rc], key="bd", ndma=32)

    P.barrier()
    castw(w_out, w_out_b, D, "w_out_b")
    castgu(w_gate, w_gate_b, "w_gate_b")
    castgu(w_up, w_up_b, "w_up_b")
    castw(w_down, w_down_b, DFF, "w_down_b")

    def rstd_ops(ssq_ap, n, epsn, rr, rw):
        P.op("dve", lambda e: e.tensor_scalar(out=ssq_ap, in0=ssq_ap, scalar1=epsn, scalar2=None, op0=ALU.add), r=rr, w=rw)
        P.op("pool", lambda e: e.tensor_tensor(out=ssq_ap, in0=ssq_ap, in1=col(NHALF).to_broadcast([128, n]), op=ALU.pow), r=rr + [rc], w=rw)

    psB6 = ps[:, 6, :].bitcast(BF16)

    def prep_tile(x_d, t, xt_s, xn_s, junk, ssq_c, dst_ap, rs, gain):
        r_xt, r_xn, r_junk, r_ssq, r_dst = rs
        pB, bB = (psB, bank[7]) if t % 2 == 0 else (psB6, bank[6])
        P.op("sp", lambda e: [e.dma_start(out=xt_s, in_=x_d[t * 128:(t + 1) * 128, :])], w=[r_xt], key=r_xt.name)
        P.op("act", lambda e: e.activation(out=xn_s, in_=xt_s, func=AF.Square, accum_out=ssq_c), r=[r_xt], w=[r_xn, r_ssq])
        P.op("act", lambda e: e.activation(out=ssq_c, in_=ssq_c, func=AF.Sqrt, scale=1.0 / D, bias=col(EPSC)), r=[r_ssq, rc], w=[r_ssq])
        P.op("dve", lambda e: e.reciprocal(out=ssq_c, in_=ssq_c), r=[r_ssq], w=[r_ssq])
        P.op("dve", lambda e: e.scalar_tensor_tensor(out=xn_s, in0=xt_s, scalar=ssq_c, in1=gain, op0=ALU.mult, op1=ALU.mult),
             r=[r_xt, r_ssq, rc], w=[r_xn])

        def tr(e):
            i = None
            for kc in range(8):
                i = e.transpose(out=pB[:, kc * 128:(kc + 1) * 128], in_=xn_s[:, kc * 128:(kc + 1) * 128], identity=ident)
            return i

        def tpart():
            P.op("pe", tr, r=[r_xn, rc], w=[bB])

        def back():
            if t % 2 == 0:
                P.op("act", lambda e: e.activation(out=dst_ap, in_=pB.rearrange("p (a b) -> p a b", a=8), func=AF.Copy), r=[bB], w=[r_dst])
            else:
                P.op("dve", lambda e: e.tensor_copy(out=dst_ap, in_=pB.rearrange("p (a b) -> p a b", a=8)), r=[bB], w=[r_dst])
        return tpart, back

    def mm_group(out_ap, pairs, r, w):
        def fn(e):
            i = None
            n = len(pairs)
            for k, (l, rh) in enumerate(pairs):
                i = e.matmul(out_ap, lhsT=l, rhs=rh, start=(k == 0), stop=(k == n - 1))
            return i
        P.op("pe", fn, r=r, w=w)

    Y0 = CONST_END
    yT, Y1 = V(Y0, [128, 4, SM], BF16)

    def do_seq(si, S):
        x_d, y_d = xs[si], ys[si]
        NT = S // 128
        NB = S // 512

        o = Y1
        xnT, o = V(o, [128, 8, SM], BF16)
        wA, o = V(o, [128, 2, 8, 256], BF16)
        PREP_OFF = o
        xt, o = V(o, [128, 3, D], F32)
        xn, o = V(o, [128, 3, D], BF16)
        junk = None
        bufX, o = V(o, [128, SM + 4], F32)
        xc, o = V(o, [128, SM], F32)
        xcb, o = V(o, [128, SM], BF16)
        tmp, o = V(o, [128, 10, 512], F32)
        tag = "s%dA" % si
        r_xnT = [Res(tag + "xnT%d" % b) for b in range(NB)]
        r_xt = [Res(tag + "xt%d" % i) for i in range(3)]
        r_xn = [Res(tag + "xn%d" % i) for i in range(3)]
        r_junk = Res(tag + "junk")
        r_ssq = [Res(tag + "ssq%d" % i) for i in range(3)]
        r_wA = [Res(tag + "wA%d" % i) for i in range(2)]
        r_bufX, r_xc, r_xcb = Res(tag + "bufX"), Res(tag + "xc"), Res(tag + "xcb")
        r_tmp = [Res(tag + "tmp%d" % i) for i in range(10)]
        r_yT = [res("yT%d" % c) for c in range(4)]
        r_hcar = Res(tag + "hcar")

        pipe = []
        for t in range(NT):
            s = t % 3
            pipe.append(prep_tile(x_d, t, xt[:, s, :], xn[:, s, :], junk, col(TMP + 4 + s),
                                  xnT[:, :, t * 128:(t + 1) * 128], (r_xt[s], r_xn[s], r_junk, r_ssq[s], r_xnT[t // 4]), gmix))
            if t >= 1:
                pipe[t - 1][0]()
            if t >= 2:
                pipe[t - 2][1]()
        pipe[NT - 1][0]()
        if NT >= 2:
            pipe[NT - 2][1]()
        pipe[NT - 1][1]()
        P.op("dve", lambda e: e.memset(bufX[:, 0:2], 0.0), w=[r_bufX])
        set1 = tuple(V(PREP_OFF + k * 4096, [128, 1024], F32)[0] for k in range(3))
        fence = []
        for x_ in r_xt + r_xn + [r_junk]:
            fence += ([x_.w] if x_.w is not None else []) + list(x_.r)
        r_sets = [(Res(tag + "a0"), Res(tag + "m0"), Res(tag + "u0")), (Res(tag + "a1"), Res(tag + "m1"), Res(tag + "u1"))]
        for x_ in r_sets[1]:
            x_.r = list(fence)
        bcount = [0]
        for c in range(4):
            ws = c % 2
            def ldw(e, c=c, ws=ws):
                src = w_in_b.rearrange("(kc p) n -> p kc n", p=128)
                return [e.dma_start(out=wA[:, ws, :, 0:128], in_=src[:, :, 1536 + c * 128:1536 + (c + 1) * 128]),
                        e.dma_start(out=wA[:, ws, :, 128:256], in_=src[:, :, 2048 + c * 128:2048 + (c + 1) * 128])]
            P.op("sp", ldw, r=[res("w_in_b")], w=[r_wA[ws]], key=r_wA[ws].name, ndma=2)
            if c > 0:
                P.op("dve", lambda e: e.memset(bufX[:, 0:2], 0.0), w=[r_bufX])
            P.op("dve", lambda e, S=S: e.memset(bufX[:, S + 2:S + 4], 0.0), w=[r_bufX])
            for b in range(NB):
                pb = bank[b % 2]
                mm_group(ps[:, b % 2, :], [(wA[:, ws, kc, 0:128], xnT[:, kc, b * 512:(b + 1) * 512]) for kc in range(8)],
                         [r_wA[ws], r_xnT[b]], [pb])
                P.op("act", lambda e, b=b: e.activation(out=bufX[:, 2 + b * 512:2 + (b + 1) * 512], in_=ps[:, b % 2, :], func=AF.Copy),
                     r=[pb], w=[r_bufX])
            cw = lambda j, c=c: col(CW + c * 4 + j)
            P.op("dve", lambda e, c=c, S=S, cw=cw: e.tensor_scalar(out=xc[:, 0:S], in0=bufX[:, 0:S], scalar1=cw(0), scalar2=col(CB + c),
                                                               op0=ALU.mult, op1=ALU.add), r=[r_bufX, rc], w=[r_xc])
            for j in range(1, 4):
                P.op("dve", lambda e, j=j, S=S, cw=cw: e.scalar_tensor_tensor(out=xc[:, 0:S], in0=bufX[:, j:j + S], scalar=cw(j), in1=xc[:, 0:S],
                                                                            op0=ALU.mult, op1=ALU.add), r=[r_bufX, r_xc, rc], w=[r_xc])
            P.op("act", lambda e, S=S: e.activation(out=xcb[:, 0:S], in_=xc[:, 0:S], func=AF.Copy), r=[r_xc], w=[r_xcb])
            def gelu_front(b, c=c, ws=ws):
                pb = bank[2 + b % 2]
                pg = ps[:, 2 + b % 2, :]
                blk = slice(b * 512, (b + 1) * 512)
                mm_group(pg, [(wA[:, ws, kc, 128:256], xnT[:, kc, blk]) for kc in range(8)], [r_wA[ws], r_xnT[b]], [pb])
                if b % 2 == 0:
                    t0, t1, rt0, rt1 = tmp[:, 8, :], tmp[:, 9, :], r_tmp[2], r_tmp[3]
                else:
                    t0, t1, rt0, rt1 = tmp[:, 0, :], tmp[:, 2, :], r_sets[0][0], r_sets[0][1]
                P.op("act", lambda e: e.activation(out=t0, in_=pg, func=AF.Square), r=[pb], w=[rt0])
                P.op("dve", lambda e: e.tensor_scalar(out=t0, in0=t0, scalar1=0.044715, scalar2=1.0, op0=ALU.mult, op1=ALU.add),
                     r=[rt0], w=[rt0])
                P.op("dve", lambda e: e.tensor_tensor(out=t0, in0=t0, in1=pg, op=ALU.mult), r=[rt0, pb], w=[rt0])

                def back():
                    P.op("act", lambda e: e.activation(out=t1, in_=t0, func=AF.Tanh, scale=math.sqrt(2.0 / math.pi)),
                         r=[rt0], w=[rt1])
                    P.op("dve", lambda e: e.scalar_tensor_tensor(out=yT[:, c, blk], in0=t1, scalar=1.0, in1=pg,
                                                                op0=ALU.add, op1=ALU.mult), r=[rt1, pb], w=[r_yT[c]])
                return back
            pend = None
            for b in range(NB):
                bk_ = gelu_front(b)
                if pend is not None:
                    pend()
                pend = bk_
            pend()
            sets = [tuple(tmp[:, 2 * k:2 * k + 2, :].rearrange("p a b -> p (a b)") for k in range(3)), set1]
            hb_b = tmp[:, 6:8, :].rearrange("p a b -> p (a b)")
            tr_, ti_ = tmp[:, 8, :], tmp[:, 9, :]
            batches = [list(range(g, min(g + 2, NB))) for g in range(0, NB, 2)]
            items = [(0, n, blks) for n, blks in enumerate(batches)] + [(1, n, blks) for n, blks in enumerate(batches[::-1])]
            hcar = col(TMP + 8)

            def phase12(item, sx, c=c):
                d, n, blks = item
                a_b, m_b, u_b = sets[sx]
                r_a, r_m, r_u = r_sets[sx]
                ir, ii = (d * 2 + 0) * 4 + c, (d * 2 + 1) * 4 + c
                kb = d * 4 + c
                nb = len(blks)
                t0_, t1_ = blks[0] * 512, (blks[-1] + 1) * 512
                L = t1_ - t0_
                for bb, b in enumerate(blks):
                    blk = slice(b * 512, (b + 1) * 512)
                    mm_group(ps[:, 4 + bb, :], [(bd[:, ir, :], xcb[:, blk])], [rc, r_xcb], [bank[4 + bb]])
                    mm_group(ps[:, 2 + bb, :], [(bd[:, ii, :], xcb[:, blk])], [rc, r_xcb], [bank[2 + bb]])
                m3 = m_b[:, 0:L].rearrange("p (a b) -> p a b", a=nb)
                u3 = u_b[:, 0:L].rearrange("p (a b) -> p a b", a=nb)
                P.op("act", lambda e: e.activation(out=m3, in_=ps[:, 4:4 + nb, :], func=AF.Tanh, scale=0.5, bias=col(BR + kb)),
                     r=[bank[4 + bb] for bb in range(nb)] + [rc], w=[r_m])
                P.op("act", lambda e: e.activation(out=a_b[:, 0:L], in_=m_b[:, 0:L], func=AF.Exp, scale=col(CH + kb), bias=col(CH + kb)),
                     r=[r_m, rc], w=[r_a])
                P.op("act", lambda e: e.activation(out=m_b[:, 0:L], in_=m_b[:, 0:L], func=AF.Exp, scale=col(CF + kb), bias=col(CF + kb)),
                     r=[r_m, rc], w=[r_m])
                P.op("act", lambda e: e.activation(out=u3, in_=ps[:, 2:2 + nb, :], func=AF.Tanh, scale=0.5, bias=col(BI + kb)),
                     r=[bank[2 + bb] for bb in range(nb)] + [rc], w=[r_u])
                P.op("dve", lambda e: e.tensor_scalar(out=m_b[:, 0:L], in0=m_b[:, 0:L], scalar1=1.0, scalar2=-1e-12, op0=ALU.subtract, op1=ALU.min),
                     r=[r_m], w=[r_m])
                P.op("dve", lambda e: e.scalar_tensor_tensor(out=u_b[:, 0:L], in0=u_b[:, 0:L], scalar=1.0, in1=xc[:, t0_:t1_], op0=ALU.add, op1=ALU.mult),
                     r=[r_u, r_xc], w=[r_u])
                P.op("act", lambda e: e.activation(out=m_b[:, 0:L], in_=m_b[:, 0:L], func=AF.Sqrt, scale=-1.0), r=[r_m], w=[r_m])

            def phase3(item, sx, c=c):
                d, n, blks = item
                a_b, m_b, u_b = sets[sx]
                r_a, r_m, r_u = r_sets[sx]
                t0_, t1_ = blks[0] * 512, (blks[-1] + 1) * 512
                L = t1_ - t0_
                P.op("dve", lambda e: e.scalar_tensor_tensor(out=u_b[:, 0:L], in0=u_b[:, 0:L], scalar=0.5, in1=m_b[:, 0:L], op0=ALU.mult, op1=ALU.mult),
                     r=[r_u, r_m], w=[r_u])
                if d == 0:
                    init = 0.0 if n == 0 else bufX[:, t0_ - 1:t0_]
                    P.op("dve", lambda e: e.tensor_tensor_scan(out=bufX[:, t0_:t1_], data0=a_b[:, 0:L], data1=u_b[:, 0:L], initial=init,
                                                               op0=ALU.mult, op1=ALU.add), r=[r_a, r_u, r_bufX], w=[r_bufX])
                else:
                    init = 0.0 if n == 0 else hcar
                    hx = n % 2
                    hb_b = hbs[hx]
                    r_hb = r_hbs[hx]
                    P.op("dve", lambda e: e.tensor_tensor_scan(out=hb_b[:, 0:L][:, ::-1], data0=a_b[:, 0:L][:, ::-1], data1=u_b[:, 0:L][:, ::-1],
                                                               initial=init, op0=ALU.mult, op1=ALU.add), r=[r_a, r_u, r_hcar], w=[r_hb])
                    P.op("dve", lambda e: e.tensor_copy(out=hcar, in_=hb_b[:, 0:1]), r=[r_hb], w=[r_hcar])
                    P.op("pool", lambda e: e.tensor_tensor(out=hb_b[:, 0:L], in0=hb_b[:, 0:L], in1=bufX[:, t0_:t1_], op=ALU.add),
                         r=[r_hb, r_bufX, r_hcar], w=[r_hb])
                    for fn_ in pend_y:
                        fn_()
                    pend_y[:] = [lambda: P.op("dve", lambda e: e.scalar_tensor_tensor(out=yT[:, c, t0_:t1_], in0=hb_b[:, 0:L], scalar=0.5, in1=yT[:, c, t0_:t1_],
                                                                                     op0=ALU.mult, op1=ALU.mult), r=[r_hb, r_yT[c]], w=[r_yT[c]])]

            hbs = [tmp[:, 6:8, :].rearrange("p a b -> p (a b)"), tmp[:, 8:10, :].rearrange("p a b -> p (a b)")]
            r_hbs = [r_tmp[8], r_tmp[2]]
            pend_y = []
            sx0 = bcount[0]
            bcount[0] += len(items)
            phase12(items[0], sx0 % 2)
            for i_, item in enumerate(items):
                if i_ + 1 < len(items):
                    phase12(items[i_ + 1], (sx0 + i_ + 1) % 2)
                phase3(item, (sx0 + i_) % 2)
            for fn_ in pend_y:
                fn_()
        P.barrier()

        o = Y1
        qT, o = V(o, [128, NH, SM], BF16)
        kT, o = V(o, [128, NH, SM], BF16)
        Va, o = V(o, [128, NTM, NH, 130], BF16)
        XR = o
        wB, o = V(o, [128, 8, 1536], BF16)
        xt, o = V(o, [128, 2, D], F32)
        xn, o = V(o, [128, 2, D], BF16)
        junk, o = V(o, [128, D], BF16)
        xnb, o = V(o, [128, 2, 8, 512], BF16)
        tag = "s%dB" % si
        r_wB = Res(tag + "wB")
        r_xt = [Res(tag + "xt%d" % i) for i in range(2)]
        r_xn = [Res(tag + "xn%d" % i) for i in range(2)]
        r_junk = Res(tag + "junk")
        r_ssq = [Res(tag + "ssq%d" % i) for i in range(2)]
        r_xnb = [Res(tag + "xnb%d" % i) for i in range(2)]
        r_q, r_k, r_v = Res(tag + "q"), Res(tag + "k"), Res(tag + "v")

        def ldwB(e):
            src = w_in_b.rearrange("(kc p) n -> p kc n", p=128)
            return [e.dma_start(out=wB[:, :, i * 512:(i + 1) * 512], in_=src[:, :, i * 512:(i + 1) * 512]) for i in range(3)]
        P.op("sp", ldwB, r=[res("w_in_b")], w=[r_wB], key=r_wB.name, ndma=3)
        P.op("pool", lambda e, NT=NT: e.memset(Va[:, 0:NT, :, 128:129], 1.0), w=[r_v])
        def b1_A(b, tts):
            bs = b % 2
            out = []
            for tt in tts:
                t = b * 4 + tt
                s = t % 2
                out.append(prep_tile(x_d, t, xt[:, s, :], xn[:, s, :], junk, col(TMP + 4 + s),
                                     xnb[:, bs, :, tt * 128:(tt + 1) * 128], (r_xt[s], r_xn[s], r_junk, r_ssq[s], r_xnb[bs]), gmix))
            return out

        def b1_TC(parts):
            for tp_, _ in parts:
                tp_()
            for _, bk_ in parts:
                bk_()

        def b1_groups(b):
            bs = b % 2
            blk = slice(b * 512, (b + 1) * 512)
            gl = []
            k = 0
            for h in range(NH):
                for which, dst, rr, scale in ((0, qT, r_q, 0.125), (1, kT, r_k, 1.0)):
                    pbi = k % 4
                    k += 1

                    def g(pbi=pbi, which=which, dst=dst, rr=rr, scale=scale, h=h):
                        c0 = which * 512 + h * 128
                        mm_group(ps[:, pbi, :], [(wB[:, kc, c0:c0 + 128], xnb[:, bs, kc, :]) for kc in range(8)], [r_wB, r_xnb[bs]], [bank[pbi]])
                        if which == 0:
                            P.op("act", lambda e: e.activation(out=dst[:, h, blk], in_=ps[:, pbi, :], func=AF.Copy, scale=scale), r=[bank[pbi]], w=[rr])
                        else:
                            P.op("dve", lambda e: e.tensor_copy(out=dst[:, h, blk], in_=ps[:, pbi, :]), r=[bank[pbi]], w=[rr])
                    gl.append(g)
            for tt in range(4):
                def g(tt=tt):
                    t = b * 4 + tt
                    pbi = 4 + tt % 2
                    mm_group(ps[:, pbi, :], [(xnb[:, bs, kc, tt * 128:(tt + 1) * 128], wB[:, kc, 1024:1536]) for kc in range(8)], [r_wB, r_xnb[bs]], [bank[pbi]])
                    P.op("dve", lambda e: e.tensor_copy(out=Va[:, t, :, 0:128], in_=ps[:, pbi, :].rearrange("p (h d) -> p h d", h=NH)),
                         r=[bank[pbi]], w=[r_v])
                gl.append(g)
            return gl

        p01 = b1_A(0, [0, 1])
        b1_TC(p01)
        p23 = b1_A(0, [2, 3])
        b1_TC(p23)
        for b in range(NB):
            gl = b1_groups(b)
            nxt = b + 1 < NB
            if nxt:
                p01 = b1_A(b + 1, [0, 1])
            for g in gl[:6]:
                g()
            if nxt:
                b1_TC(p01)
                p23 = b1_A(b + 1, [2, 3])
            for g in gl[6:]:
                g()
            if nxt:
                b1_TC(p23)
        P.barrier()

        o = XR
        oT, o = V(o, [128, NH, SM], BF16)
        OT_END = o
        Pb, o = V(o, [128, 3, 2, 512], BF16)
        Ssb, o = V(o, [128, 2, 512], F32)
        acc, o = V(o, [128, 8, 130], F32)
        ot, o = V(o, [128, 2, 4, 128], F32)
        onb, o = V(o, [128, 2, 4, 128], BF16)
        rl, o = V(o, [128, 32], F32)
        junkf, o = V(o, [128, 128], F32)
        tag = "s%dC" % si
        r_P = [Res(tag + "P%d" % i) for i in range(3)]
        r_Ssb = [Res(tag + "S%d" % i) for i in range(2)]
        r_accs = [Res(tag + "acc%d" % i) for i in range(8)]
        r_junkb = Res(tag + "jb")
        r_ot = [Res(tag + "ot%d" % i) for i in range(2)]
        r_on = [Res(tag + "on%d" % i) for i in range(2)]
        r_rl = [Res(tag + "rl%d" % i) for i in range(2)]
        r_oT = [res("oT%d" % h) for h in range(NH)]
        psO = ps[:, 4:7, :].rearrange("p a b -> p (a b)")

        def slot_ap(sl):
            bk, j = divmod(sl, 3)
            return ps[:, 4 + bk, j * 130:j * 130 + 129]
        iters = []
        for qb in range(NB):
            jd0 = 4 * qb
            for h in range(NH):
                phases = [(pn, ch) for pn, ch in (("L", list(range(0, jd0))), ("D", list(range(jd0, jd0 + 4))), ("R", list(range(jd0 + 4, NT)))) if ch]
                for pi_, (pn, ch) in enumerate(phases):
                    for idx, j in enumerate(ch):
                        iters.append(dict(qb=qb, h=h, pn=pn, idx=idx, n=len(ch), j=j, first_phase=(pi_ == 0),
                                          last_phase=(pi_ == len(phases) - 1)))
        NI = len(iters)

        def emit_qk_exp(i):
            it = iters[i]
            qb, h, j, pn = it["qb"], it["h"], it["j"], it["pn"]
            sbi, pbi = i % 2, i % 3
            jd0 = 4 * qb
            qblk = slice(qb * 512, (qb + 1) * 512)
            kblk = slice(j * 128, (j + 1) * 128)

            def qk(e):
                e.matmul(ps[:, 2 * sbi, :], lhsT=kT[0:64, h, kblk], rhs=qT[0:64, h, qblk], start=True, stop=True)
                return e.matmul(ps[:, 2 * sbi + 1, :], lhsT=kT[64:128, h, kblk], rhs=qT[64:128, h, qblk], start=True, stop=True)
            P.op("pe", qk, r=[r_q, r_k], w=[bank[2 * sbi], bank[2 * sbi + 1]])
            src2 = ps[:, 2 * sbi:2 * sbi + 2, :]
            if pn == "D":
                off = 384 - 128 * (j - jd0)
                P.op("dve", lambda e, off=off: e.scalar_tensor_tensor(out=src2, in0=Dt[:, off:off + 512].unsqueeze(1).to_broadcast([128, 2, 512]), scalar=-SLOPES[h],
                                                                    in1=src2, op0=ALU.mult, op1=ALU.add),
                     r=[bank[2 * sbi], bank[2 * sbi + 1], rc], w=[bank[2 * sbi], bank[2 * sbi + 1]])
                P.op("act", lambda e: e.activation(out=Pb[:, pbi, :, :], in_=src2, func=AF.Exp), r=[bank[2 * sbi], bank[2 * sbi + 1]], w=[r_P[pbi]])
            else:
                bias = biasL[:, h, jd0 - j:jd0 - j + 1] if pn == "L" else biasR[:, h, j - jd0 - 4:j - jd0 - 3]
                P.op("act", lambda e, bias=bias: e.activation(out=Pb[:, pbi, :, :], in_=src2, func=AF.Exp, bias=bias),
                     r=[bank[2 * sbi], bank[2 * sbi + 1], rc], w=[r_P[pbi]])

        deferred = []

        def flush(parity=None, tick=False):
            keep = []
            for ent in deferred:
                if tick:
                    ent[0] -= 1
                if ent[0] <= 0 or (parity is not None and ent[1] == parity) or (parity == -1):
                    ent[2]()
                else:
                    keep.append(ent)
            deferred[:] = keep

        fin_count = [0]

        def emit_pv(i):
            it = iters[i]
            qb, h, j, pn, idx, n = it["qb"], it["h"], it["j"], it["pn"], it["idx"], it["n"]
            pbi = i % 3
            qblk = slice(qb * 512, (qb + 1) * 512)
            for bk in range(3):
                sls = [sl for sl in range(8) if sl // 3 == bk]

                def pv(e, sls=sls):
                    ins = None
                    for sl in sls:
                        m, qs = divmod(sl, 4)
                        ins = e.matmul(slot_ap(sl), lhsT=Pb[:, pbi, m, qs * 128:(qs + 1) * 128], rhs=Va[:, j, h, 0:129],
                                       start=(idx == 0 and sl % 3 == 0), stop=(idx == n - 1), skip_group_check=True)
                    return ins
                P.op("pe", pv, r=[r_P[pbi], r_v], w=[bank[4 + bk]])
                if idx == n - 1:
                    fp = it["first_phase"]
                    nsl = len(sls)
                    pview = ps[:, 4 + bk, 0:nsl * 130].rearrange("p (s c) -> p s c", c=130)[:, :, 0:129]
                    aview = acc[:, sls[0]:sls[0] + nsl, 0:129]
                    racc = [r_accs[sl] for sl in sls]
                    if pn == "D":
                        if fp:
                            P.op("dve", lambda e, pview=pview, aview=aview: e.tensor_copy(out=aview, in_=pview), r=[bank[4 + bk]], w=racc)
                        else:
                            P.op("dve", lambda e, pview=pview, aview=aview: e.tensor_tensor(out=aview, in0=aview, in1=pview, op=ALU.add),
                                 r=[bank[4 + bk]] + racc, w=racc)
                    elif fp:
                        ftab = col((F8L if pn == "L" else F8R) + h * 8 + sls[0], nsl).unsqueeze(2).to_broadcast([128, nsl, 129])
                        P.op("dve", lambda e, pview=pview, aview=aview, ftab=ftab: e.tensor_tensor(out=aview, in0=pview, in1=ftab, op=ALU.mult),
                             r=[bank[4 + bk], rc], w=racc)
                    else:
                        for sl in sls:
                            f = col((FL if pn == "L" else FR) + h * 4 + sl % 4)
                            P.op("dve", lambda e, sl=sl, f=f: e.scalar_tensor_tensor(out=acc[:, sl, 0:129], in0=slot_ap(sl), scalar=f, in1=acc[:, sl, 0:129],
                                                                                    op0=ALU.mult, op1=ALU.add), r=[bank[4 + bk], r_accs[sl], rc], w=[r_accs[sl]])
            if idx == n - 1 and it["last_phase"]:
                par = fin_count[0] % 2
                fin_count[0] += 1
                flush(parity=par)
                otp, onp, rlp = ot[:, par], onb[:, par], rl[:, par * 16:(par + 1) * 16]
                P.op("dve", lambda e: e.reciprocal(out=rlp[:, 0:8], in_=acc[:, :, 128]), r=r_accs, w=[r_rl[par]])
                P.op("dve", lambda e: e.tensor_scalar(out=rlp[:, 4:8], in0=rlp[:, 4:8], scalar1=col(NLAM), scalar2=None, op0=ALU.mult), r=[r_rl[par], rc], w=[r_rl[par]])
                for qs in range(4):
                    P.op("dve", lambda e, qs=qs: e.tensor_scalar(out=otp[:, qs, :], in0=acc[:, 4 + qs, 0:128], scalar1=rlp[:, 4 + qs:5 + qs], scalar2=None, op0=ALU.mult),
                         r=[r_accs[4 + qs], r_rl[par]], w=[r_ot[par]])
                    P.op("dve", lambda e, qs=qs: e.scalar_tensor_tensor(out=otp[:, qs, :], in0=acc[:, qs, 0:128], scalar=rlp[:, qs:qs + 1], in1=otp[:, qs, :],
                                                                       op0=ALU.mult, op1=ALU.add), r=[r_accs[qs], r_rl[par], r_ot[par]], w=[r_ot[par]])
                for qs in range(4):
                    P.op("dve", lambda e, qs=qs: e.scalar_tensor_tensor(out=junkf, in0=otp[:, qs, :], scalar=1.0, in1=otp[:, qs, :], op0=ALU.mult, op1=ALU.mult,
                                                                       accum_out=rlp[:, 8 + qs:9 + qs]), r=[r_ot[par]], w=[r_junkb, r_rl[par]])
                rstd_ops(rlp[:, 8:12], 4, 128.0 * EPS, [r_rl[par]], [r_rl[par]])
                for qs in range(4):
                    P.op("dve", lambda e, qs=qs: e.scalar_tensor_tensor(out=onp[:, qs, :], in0=otp[:, qs, :], scalar=rlp[:, 8 + qs:9 + qs], in1=gsub,
                                                                       op0=ALU.mult, op1=ALU.mult), r=[r_ot[par], r_rl[par], rc], w=[r_on[par]])

                def fin(par=par, h=h, qblk=qblk, onp=onp):
                    def tro(e):
                        ins = None
                        for qs in range(4):
                            ins = e.transpose(out=psB[:, qs * 128:(qs + 1) * 128], in_=onp[:, qs, :], identity=ident)
                        return ins
                    P.op("pe", tro, r=[r_on[par], rc], w=[bank[7]])
                    P.op("act", lambda e: e.activation(out=oT[:, h, qblk], in_=psB[:, 0:512], func=AF.Copy), r=[bank[7]], w=[r_oT[h]])
                deferred.append([10, par, fin])

        nq = [0]

        def ensure_qk(n):
            while nq[0] <= min(n, NI - 1):
                emit_qk_exp(nq[0])
                nq[0] += 1
        for i in range(NI):
            ensure_qk(i + (2 if (iters[i]["idx"] == 0 and i > 0) else 1))
            emit_pv(i)
            flush(tick=True)
        flush(parity=-1)
        P.barrier()

        o = Y1
        wo, o = V(o, [128, 8, D], BF16)
        wd, o = V(o, [128, 11, D], BF16)
        hT, o = V(o, [128, 2, 11, 512], BF16)
        x1, o = V(o, [128, 4, D], F32)
        h2T, o = V(o, [128, 8, 512], BF16)
        wgu, o = V(o, [128, 2, 2, 8, 128], BF16)
        assert o <= XR, (o, XR)
        o = OT_END
        xt, o = V(o, [128, 2, D], F32)
        h2n, o = V(o, [128, 2, D], BF16)
        tw, o = V(o, [128, 4, 512], F32)
        tag = "s%dD" % si
        r_wo, r_wd = Res(tag + "wo"), Res(tag + "wd")
        r_hT = [Res(tag + "hT%d" % i) for i in range(2)]
        r_x1 = [Res(tag + "x1_%d" % i) for i in range(4)]
        r_h2T = Res(tag + "h2T")
        r_wgu = [Res(tag + "wgu%d" % i) for i in range(2)]
        r_xt = [Res(tag + "xt%d" % i) for i in range(2)]
        r_h2n, r_junk = Res(tag + "h2n"), Res(tag + "junk")
        r_ssq = [Res(tag + "ssq%d" % i) for i in range(2)]
        r_tw = [Res(tag + "tw%d" % i) for i in range(4)]

        P.op("sp", lambda e: [e.dma_start(out=wo, in_=w_out_b.rearrange("(kc p) n -> p kc n", p=128))], r=[res("w_out_b")], w=[r_wo], key=r_wo.name)
        gcount = 0
        h2n2 = h2n
        sqo = tw[:, 0:2, :].rearrange("p a b -> p (a b)")
        r_h2ns = [Res(tag + "h2n%d" % i) for i in range(2)]
        r_ssq4 = [Res(tag + "ssq4_%d" % i) for i in range(4)]

        def c_front(b, tt):
            t = b * 4 + tt
            s = t % 2
            tok = slice(t * 128, (t + 1) * 128)
            P.op("sp", lambda e: [e.dma_start(out=xt[:, s, :], in_=x_d[tok, :])], w=[r_xt[s]], key=r_xt[s].name)
            for hf in range(2):
                cs = slice(hf * 512, (hf + 1) * 512)
                pairs = [((oT[:, kc, tok] if kc < 4 else yT[:, kc - 4, tok]), wo[:, kc, cs]) for kc in range(8)]
                mm_group(ps[:, hf, :], pairs, r_oT + r_yT + [r_wo], [bank[hf]])
                P.op("dve", lambda e, hf=hf, cs=cs: e.tensor_tensor(out=x1[:, tt, cs], in0=ps[:, hf, :], in1=xt[:, s, cs], op=ALU.add),
                     r=[bank[hf], r_xt[s]], w=[r_x1[tt]])
            ssq_c = col(TMP + 10 + tt)
            P.op("act", lambda e: e.activation(out=sqo, in_=x1[:, tt, :], func=AF.Square, accum_out=ssq_c), r=[r_x1[tt]], w=[r_tw[0], r_tw[1], r_ssq4[tt]])
            rstd_ops(ssq_c, 1, D * EPS, [r_ssq4[tt]], [r_ssq4[tt]])
            P.op("dve", lambda e: e.scalar_tensor_tensor(out=h2n2[:, tt % 2, :], in0=x1[:, tt, :], scalar=ssq_c, in1=gffn, op0=ALU.mult, op1=ALU.mult),
                 r=[r_x1[tt], r_ssq4[tt], rc], w=[r_h2ns[tt % 2]])

        def c_back(b, tt):
            def tr2(e):
                i = None
                for kc in range(8):
                    i = e.transpose(out=psB[:, kc * 128:(kc + 1) * 128], in_=h2n2[:, tt % 2, kc * 128:(kc + 1) * 128], identity=ident)
                return i
            P.op("pe", tr2, r=[r_h2ns[tt % 2], rc], w=[bank[7]])
            P.op("act", lambda e: e.activation(out=h2T[:, :, tt * 128:(tt + 1) * 128], in_=psB.rearrange("p (a b) -> p a b", a=8), func=AF.Copy),
                 r=[bank[7]], w=[r_h2T])

        def c_gu(fc):
            nonlocal_g = gstate
            gs = nonlocal_g[0] % 2
            nonlocal_g[0] += 1
            part, fi = divmod(fc, 11)

            def ldgu(e):
                return [e.dma_start(out=wgu[:, gs, 0, :, :], in_=w_gate_b[fc]),
                        e.dma_start(out=wgu[:, gs, 1, :, :], in_=w_up_b[fc])]
            P.op("sp", ldgu, r=[res("w_gate_b"), res("w_up_b")], w=[r_wgu[gs]], key=r_wgu[gs].name, ndma=2)
            bg, bu = 2 + 2 * gs, 3 + 2 * gs
            mm_group(ps[:, bg, :], [(wgu[:, gs, 0, kc, :], h2T[:, kc, :]) for kc in range(8)], [r_wgu[gs], r_h2T], [bank[bg]])
            mm_group(ps[:, bu, :], [(wgu[:, gs, 1, kc, :], h2T[:, kc, :]) for kc in range(8)], [r_wgu[gs], r_h2T], [bank[bu]])
            th, ww = tw[:, 2 * gs, :], tw[:, 2 * gs + 1, :]
            P.op("act", lambda e: e.activation(out=th, in_=ps[:, bg, :], func=AF.Tanh, scale=0.5), r=[bank[bg]], w=[r_tw[2 * gs]])
            P.op("dve", lambda e: e.scalar_tensor_tensor(out=ww, in0=th, scalar=1.0, in1=ps[:, bg, :], op0=ALU.add, op1=ALU.mult),
                 r=[r_tw[2 * gs], bank[bg]], w=[r_tw[2 * gs + 1]])
            P.op("dve", lambda e: e.scalar_tensor_tensor(out=hT[:, part, fi, :], in0=ww, scalar=0.5, in1=ps[:, bu, :],
                                                        op0=ALU.mult, op1=ALU.mult), r=[r_tw[2 * gs + 1], bank[bu]], w=[r_hT[part]])

        def c_wd(part):
            P.op("pool", lambda e: [e.dma_start(out=wd, in_=w_down_b[part * 1408:(part + 1) * 1408, :].rearrange("(fc p) n -> p fc n", p=128))],
                 r=[res("w_down_b")], w=[r_wd], key=r_wd.name)

        def c_down(part):
            for tt in range(4):
                for hf in range(2):
                    cs = slice(hf * 512, (hf + 1) * 512)
                    mm_group(ps[:, hf, :], [(hT[:, part, fi, tt * 128:(tt + 1) * 128], wd[:, fi, cs]) for fi in range(11)], [r_hT[part], r_wd], [bank[hf]])
                    P.op("dve", lambda e, hf=hf, tt=tt, cs=cs: e.tensor_tensor(out=x1[:, tt, cs], in0=x1[:, tt, cs], in1=ps[:, hf, :], op=ALU.add),
                         r=[bank[hf], r_x1[tt]], w=[r_x1[tt]])

        def c_final(b, tt):
            t = b * 4 + tt
            tok = slice(t * 128, (t + 1) * 128)
            ssq_c = col(TMP + 14 + tt)
            P.op("act", lambda e: e.activation(out=sqo, in_=x1[:, tt, :], func=AF.Square, accum_out=ssq_c), r=[r_x1[tt]], w=[r_tw[0], r_tw[1], r_ssq4[tt]])
            rstd_ops(ssq_c, 1, D * EPS, [r_ssq4[tt]], [r_ssq4[tt]])
            P.op("dve", lambda e: e.scalar_tensor_tensor(out=x1[:, tt, :], in0=x1[:, tt, :], scalar=ssq_c, in1=gfin, op0=ALU.mult, op1=ALU.mult),
                 r=[r_x1[tt], r_ssq4[tt], rc], w=[r_x1[tt]])
            P.op("pool", lambda e: [e.dma_start(out=y_d[tok, :], in_=x1[:, tt, :])], r=[r_x1[tt]], key="st" + r_x1[tt].name)

        gstate = [0]
        for b in range(NB):
            for tt in range(4):
                c_front(b, tt)
                if tt >= 1:
                    c_back(b, tt - 1)
            c_back(b, 3)
            c_wd(0)
            for fc in range(12):
                c_gu(fc)
            c_down(0)
            c_wd(1)
            for fc in range(12, 22):
                c_gu(fc)
            c_down(1)
            for tt in range(4):
                c_final(b, tt)
        P.barrier()

    for si_, S_ in enumerate(seqs):
        do_seq(si_, S_)
    P.emit(nc)
    es.close()
    return nc


_NC_CACHE = {}


def _common_inputs(inp):
    f = lambda a: np.ascontiguousarray(np.asarray(a, dtype=np.float32))
    pc = lambda a: f(np.asarray(a).reshape(4, 128).T)
    pdc = lambda a: f(np.asarray(a).reshape(2, 4, 128).transpose(2, 0, 1).reshape(128, 8))
    return {
        "w_in": f(inp["w_in"][0]), "w_out": f(inp["w_out"][0]), "w_gate": f(inp["w_gate"][0]),
        "w_up": f(inp["w_up"][0]), "w_down": f(inp["w_down"][0]),
        "vec1024": f(np.stack([inp["norm_mix"][0], inp["norm_ffn"][0], inp["norm_final"]])),
        "convw": f(np.asarray(inp["conv_w"][0]).T.reshape(4, 128, 4).transpose(1, 0, 2).reshape(128, 16)),
        "convb": pc(inp["conv_b"][0]),
        "brg": pdc(inp["b_rg"][0]), "big": pdc(inp["b_ig"][0]), "lrul": pdc(inp["lru_lambda"][0]),
        "wrg": f(inp["w_rg"][0]), "wig": f(inp["w_ig"][0]),
        "lamv": f(np.stack([inp["lambda_q1"][0], inp["lambda_k1"][0], inp["lambda_q2"][0], inp["lambda_k2"][0]])),
        "subg": f(np.asarray(inp["subln_g"][0]).reshape(1, 128)),
    }


def kernel(**inp):
    xp = np.asarray(inp["x_prompt"], dtype=np.float32)
    xsm = np.asarray(inp["x_sample"], dtype=np.float32)
    seqs = (xp.shape[1], xp.shape[1], xsm.shape[1])
    if seqs not in _NC_CACHE:
        _NC_CACHE[seqs] = _build(list(seqs))
    nc = _NC_CACHE[seqs]
    common = _common_inputs(inp)
    in_maps = []
    for c in range(8):
        m = dict(common)
        m["x0"] = np.ascontiguousarray(xp[2 * c])
        m["x1"] = np.ascontiguousarray(xp[2 * c + 1])
        m["x2"] = np.ascontiguousarray(xsm[c])
        in_maps.append(m)
    res = run_bass_kernel_spmd(nc, in_maps, core_ids=list(range(8)))
    yp = np.empty_like(xp)
    ysm = np.empty_like(xsm)
    for c in range(8):
        r = res.results[c]
        yp[2 * c] = r["y0"]
        yp[2 * c + 1] = r["y1"]
        ysm[c] = r["y2"]
    return yp, ysm
```

```python
import math
from contextlib import ExitStack

import numpy as np
import concourse.bass as bass
import concourse.mybir as mybir
from concourse.bass_utils import run_bass_kernel_spmd

F32 = mybir.dt.float32
BF16 = mybir.dt.bfloat16
U8 = mybir.dt.uint8
I32 = mybir.dt.int32
ALU = mybir.AluOpType
AF = mybir.ActivationFunctionType

D = 1024
DFF = 2816
NH = 4
INW = 2560
EPS = 1e-6
LAM_INIT = 0.8 - 0.6 * math.exp(0.0)
SLOPES = [2.0 ** (-8.0 * (h + 1) / NH) for h in range(NH)]
NFC = DFF // 128
ENGS = ("pe", "act", "dve", "pool", "sp")


class Res:
    __slots__ = ("name", "w", "r")

    def __init__(self, name):
        self.name = name
        self.w = None
        self.r = []


class Op:
    __slots__ = ("eng", "fn", "deps", "sig", "cnt", "key")


class Prog:
    def __init__(self):
        self.ops = {e: [] for e in ENGS}
        self.keycnt = {}
        self.dma_since_bar = []

    def op(self, eng, fn, r=(), w=(), key=None, ndma=1):
        o = Op()
        o.eng, o.fn, o.sig, o.cnt, o.key = eng, fn, False, 0, key
        deps = {}
        for x in r:
            if x.w is not None:
                deps[id(x.w)] = (x.w, True)
        for x in w:
            if x.w is not None and id(x.w) not in deps:
                deps[id(x.w)] = (x.w, True)
            for q in x.r:
                if id(q) not in deps:
                    deps[id(q)] = (q, False)
        o.deps = list(deps.values())
        for d, raw in o.deps:
            if d.key is None and (d.eng != eng or (raw and eng in ("act", "dve", "pool"))):
                d.sig = True
        for x in r:
            if key is None:
                x.r = [q for q in x.r if not (q.key is None and q.eng == eng)]
            x.r.append(o)
        for x in w:
            x.w = o
            x.r = []
        if key is not None:
            self.keycnt[key] = self.keycnt.get(key, 0) + 16 * ndma
            o.cnt = self.keycnt[key]
            self.dma_since_bar.append(o)
        self.ops[eng].append(o)
        return o

    def barrier(self):
        lasts = []
        for e in ENGS:
            for o in reversed(self.ops[e]):
                if o.key is None and o.fn is not None:
                    lasts.append(o)
                    o.sig = True
                    break
        lasts += self.dma_since_bar
        self.dma_since_bar = []
        for e in ENGS:
            o = Op()
            o.eng, o.fn, o.sig, o.cnt, o.key = e, None, False, 0, None
            o.deps = [(l, True) for l in lasts]
            self.ops[e].append(o)

    def emit(self, nc):
        with ExitStack() as es:
            sem = {e: es.enter_context(nc.semaphore("s_" + e)) for e in ("pe", "act", "dve", "pool")}
            keysem = {k: es.enter_context(nc.semaphore("k_%d" % i)) for i, k in enumerate(self.keycnt)}
            for e in ("pe", "act", "dve", "pool"):
                c = 0
                for o in self.ops[e]:
                    if o.key is None and o.sig and o.fn is not None:
                        c += 1
                        o.cnt = c
            block = es.enter_context(nc.Block())
            prog = self

            def run(engobj, E):
                waited = {}
                for o in prog.ops[E]:
                    for d, raw in o.deps:
                        if d.key is None:
                            if d.eng == E and (E == "pe" or not raw):
                                continue
                            sh, val = sem[d.eng], d.cnt
                        else:
                            sh, val = keysem[d.key], d.cnt
                        if waited.get(id(sh), 0) >= val:
                            continue
                        engobj.wait_ge(sh, val)
                        waited[id(sh)] = val
                    if o.fn is None:
                        continue
                    ins = o.fn(engobj)
                    if o.key is not None:
                        for i in ins:
                            i.then_inc(keysem[o.key], 16)
                    elif o.sig:
                        ins.then_inc(sem[E], 1)

            @block.tensor
            def _(e):
                run(e, "pe")

            @block.scalar
            def _(e):
                run(e, "act")

            @block.vector
            def _(e):
                run(e, "dve")

            @block.gpsimd
            def _(e):
                run(e, "pool")

            @block.sync
            def _(e):
                run(e, "sp")


def _build(seqs):
    SM = 4096
    assert max(seqs) <= SM
    NTM = SM // 128
    nc = bass.Bass("TRN2", target_bir_lowering=False)
    P = Prog()

    def din(name, shape, dt=F32):
        return nc.dram_tensor(name, list(shape), dt, kind="ExternalInput").ap()

    xs = [din("x%d" % i, [S, D]) for i, S in enumerate(seqs)]
    ys = [nc.dram_tensor("y%d" % i, [S, D], F32, kind="ExternalOutput").ap() for i, S in enumerate(seqs)]
    w_in = din("w_in", [D, INW])
    w_out = din("w_out", [D, D])
    w_gate = din("w_gate", [D, DFF])
    w_up = din("w_up", [D, DFF])
    w_down = din("w_down", [DFF, D])
    vec1024 = din("vec1024", [3, D])
    convw_d = din("convw", [128, 16])
    convb_d = din("convb", [128, 4])
    brg_d = din("brg", [128, 8])
    big_d = din("big", [128, 8])
    lru_d = din("lrul", [128, 8])
    wrg_d = din("wrg", [2, 8, 64, 64])
    wig_d = din("wig", [2, 8, 64, 64])
    lamv_d = din("lamv", [4, 64])
    subg_d = din("subg", [1, 128])

    def dint(name, shape):
        return nc.dram_tensor(name, list(shape), BF16, kind="Internal").ap()

    w_in_b = dint("w_in_b", [D, INW])
    w_out_b = dint("w_out_b", [D, D])
    w_gate_b = dint("w_gate_b", [NFC, 128, 8, 128])
    w_up_b = dint("w_up_b", [NFC, 128, 8, 128])
    w_down_b = dint("w_down_b", [DFF, D])

    es = ExitStack()
    sb = es.enter_context(nc.sbuf_tensor("sb", [128, 212000], U8))
    ps = es.enter_context(nc.psum_tensor("ps", [128, 8, 512], F32))
    psB = ps[:, 7, :].bitcast(BF16)

    def V(off, shape, dt):
        n = 1
        for s in shape[1:]:
            n *= s
        esz = 4 if dt in (F32, I32) else 2
        assert off % 4 == 0 and off + n * esz <= 212000, (off, shape)
        v = sb[:, off:off + n * esz].bitcast(dt)
        if len(shape) > 2:
            names = "abcde"[:len(shape) - 1]
            kw = {names[i]: shape[i + 1] for i in range(len(shape) - 2)}
            v = v.rearrange("p (%s) -> p %s" % (" ".join(names), " ".join(names)), **kw)
        return v, off + n * esz

    o = 0
    ident, o = V(o, [128, 128], BF16)
    gmix, o = V(o, [128, D], F32)
    gffn, o = V(o, [128, D], F32)
    gfin, o = V(o, [128, D], F32)
    gsub, o = V(o, [128, 128], F32)
    Dt, o = V(o, [128, 896], F32)
    sm, o = V(o, [128, 256], F32)
    biasL, o = V(o, [128, NH, 36], F32)
    biasR, o = V(o, [128, NH, 36], F32)
    dvals, o = V(o, [128, 36], F32)
    bd, o = V(o, [128, 16, 128], BF16)
    lamt, o = V(o, [128, 4, 64], F32)
    identf, o = V(o, [128, 128], F32)
    CONST_END = (o + 63) // 64 * 64
    CW, CB, BR, BI, CH, KP = 0, 16, 20, 28, 36, 44
    NHALF, PHALF, LAM, NLAM = 45, 46, 47, 48
    FL, FR = 52, 68
    KPS, KPR = 84, 88
    EPSC = 92
    F8L, F8R = 128, 160
    CF = 120
    TMP = 96

    def col(c, n=1):
        return sm[:, c:c + n]

    R = {}

    def res(name):
        if name not in R:
            R[name] = Res(name)
        return R[name]

    rc = res("const")
    bank = [res("bank%d" % i) for i in range(8)]

    def castw(src, dst, rows, name):
        def fn(e):
            out = []
            for r0 in range(0, rows, 128):
                out.append(e.dma_start(out=dst[r0:r0 + 128, :], in_=src[r0:r0 + 128, :]))
            return out
        P.op("pool", fn, w=[res(name)], key=name, ndma=rows // 128)

    castw(w_in, w_in_b, D, "w_in_b")
    def castgu(src, dst, name):
        def fn(e):
            out = []
            for kc in range(8):
                out.append(e.dma_start(out=dst[:, :, kc, :], in_=src[kc * 128:(kc + 1) * 128, :].rearrange("p (fc n) -> fc p n", n=128)))
            return out
        P.op("pool", fn, w=[res(name)], key=name, ndma=8)


    def dma(eng, out, in_, r, w, key, n=1):
        P.op(eng, lambda e: [e.dma_start(out=out, in_=in_)], r=r, w=w, key=key)

    dma("sp", gmix, vec1024[0:1, :].partition_broadcast(128), [], [rc], "c0")
    dma("sp", gffn, vec1024[1:2, :].partition_broadcast(128), [], [rc], "c1")
    dma("sp", gfin, vec1024[2:3, :].partition_broadcast(128), [], [rc], "c2")
    dma("sp", gsub, subg_d[0:1, :].partition_broadcast(128), [], [rc], "c3")
    dma("sp", col(CW, 16), convw_d[:, :], [], [rc], "c4")
    dma("sp", col(CB, 4), convb_d[:, :], [], [rc], "c5")
    dma("sp", col(BR, 8), brg_d[:, :], [], [rc], "c6")
    dma("sp", col(BI, 8), big_d[:, :], [], [rc], "c7")
    dma("sp", col(CH, 8), lru_d[:, :], [], [rc], "c8")
    for i in range(4):
        dma("sp", lamt[:, i, :], lamv_d[i:i + 1, :].partition_broadcast(128), [], [rc], "c9_%d" % i)

    def C(eng, fn, r=None, w=None):
        P.op(eng, fn, r=[rc] if r is None else r, w=[rc] if w is None else w)

    C("pool", lambda e: e.iota(identf.bitcast(I32), pattern=[[1, 128]], base=0, channel_multiplier=-1))
    C("dve", lambda e: e.tensor_copy(out=Dt[:, 0:128], in_=identf.bitcast(I32)))
    C("dve", lambda e: e.tensor_single_scalar(out=identf, in_=Dt[:, 0:128], scalar=0.0, op=ALU.is_equal))
    C("dve", lambda e: e.tensor_copy(out=ident, in_=identf))
    C("pool", lambda e: e.iota(Dt.bitcast(I32), pattern=[[1, 896]], base=-384, channel_multiplier=-1))
    C("dve", lambda e: e.tensor_copy(out=Dt, in_=Dt.bitcast(I32)))
    C("act", lambda e: e.activation(out=Dt, in_=Dt, func=AF.Abs))
    C("pool", lambda e: e.iota(col(TMP).bitcast(I32), pattern=[[1, 1]], base=0, channel_multiplier=1))
    C("dve", lambda e: e.tensor_copy(out=col(KP), in_=col(TMP).bitcast(I32)))
    C("pool", lambda e: e.iota(dvals.bitcast(I32), pattern=[[1, 36]], base=0, channel_multiplier=0))
    C("dve", lambda e: e.tensor_copy(out=dvals, in_=dvals.bitcast(I32)))
    C("dve", lambda e: e.memset(col(NHALF), -0.5))
    C("dve", lambda e: e.memset(col(PHALF), 0.5))
    C("dve", lambda e: e.memset(col(EPSC), EPS))
    for h in range(NH):
        sl = SLOPES[h]
        C("dve", lambda e, h=h, sl=sl: e.tensor_scalar(out=col(KPS + h), in0=col(KP), scalar1=sl, scalar2=None, op0=ALU.mult))
        C("dve", lambda e, h=h, sl=sl: e.tensor_scalar(out=col(KPR + h), in0=col(KP), scalar1=-sl, scalar2=-sl, op0=ALU.mult, op1=ALU.add))
        C("dve", lambda e, h=h, sl=sl: e.tensor_scalar(out=biasL[:, h, :], in0=dvals, scalar1=-128.0 * sl, scalar2=col(KPS + h), op0=ALU.mult, op1=ALU.add))
        C("dve", lambda e, h=h, sl=sl: e.tensor_scalar(out=biasR[:, h, :], in0=dvals, scalar1=-128.0 * sl, scalar2=col(KPR + h), op0=ALU.mult, op1=ALU.add))
        for qs in range(4):
            C("act", lambda e, h=h, sl=sl, qs=qs: e.activation(out=col(FL + h * 4 + qs), in_=col(KP), func=AF.Exp, scale=-sl, bias=-sl * 128.0 * qs))
            C("act", lambda e, h=h, sl=sl, qs=qs: e.activation(out=col(FR + h * 4 + qs), in_=col(KP), func=AF.Exp, scale=sl, bias=-sl * (511.0 - 128.0 * qs)))
    for src_, dst_ in ((FL, F8L), (FR, F8R)):
        C("dve", lambda e, src_=src_, dst_=dst_: e.tensor_copy(
            out=col(dst_, 32).rearrange("p (h m q) -> p h m q", h=NH, m=2),
            in_=col(src_, 16).rearrange("p (h q) -> p h q", h=NH).unsqueeze(2).to_broadcast([128, NH, 2, 4])))
    C("dve", lambda e: e.tensor_scalar(out=gffn, in0=gffn, scalar1=32.0, scalar2=None, op0=ALU.mult))
    C("dve", lambda e: e.tensor_scalar(out=gfin, in0=gfin, scalar1=32.0, scalar2=None, op0=ALU.mult))
    C("dve", lambda e: e.tensor_scalar(out=gsub, in0=gsub, scalar1=(1.0 - LAM_INIT) * math.sqrt(128.0), scalar2=None, op0=ALU.mult))
    C("dve", lambda e: e.tensor_tensor(out=lamt[:, 0, :], in0=lamt[:, 0, :], in1=lamt[:, 1, :], op=ALU.mult))
    C("dve", lambda e: e.tensor_tensor(out=lamt[:, 2, :], in0=lamt[:, 2, :], in1=lamt[:, 3, :], op=ALU.mult))
    C("dve", lambda e: e.reduce_sum(out=col(TMP + 1), in_=lamt[:, 0, :], axis=mybir.AxisListType.X))
    C("dve", lambda e: e.reduce_sum(out=col(TMP + 2), in_=lamt[:, 2, :], axis=mybir.AxisListType.X))
    C("act", lambda e: e.activation(out=col(TMP + 1, 2), in_=col(TMP + 1, 2), func=AF.Exp))
    C("dve", lambda e: e.tensor_tensor(out=col(LAM), in0=col(TMP + 1), in1=col(TMP + 2), op=ALU.subtract))
    C("dve", lambda e: e.tensor_scalar(out=col(LAM), in0=col(LAM), scalar1=LAM_INIT, scalar2=None, op0=ALU.add))
    C("dve", lambda e: e.tensor_scalar(out=col(NLAM), in0=col(LAM), scalar1=-1.0, scalar2=None, op0=ALU.mult))
    C("dve", lambda e: e.tensor_scalar(out=col(BR, 16), in0=col(BR, 16), scalar1=0.5, scalar2=None, op0=ALU.mult))
    C("act", lambda e: e.activation(out=col(CH, 8), in_=col(CH, 8), func=AF.Exp, scale=-1.0))
    C("act", lambda e: e.activation(out=col(CH, 8), in_=col(CH, 8), func=AF.Ln, bias=1.0))
    C("dve", lambda e: e.tensor_scalar(out=col(CF, 8), in0=col(CH, 8), scalar1=-8.0, scalar2=None, op0=ALU.mult))
    C("dve", lambda e: e.tensor_scalar(out=col(CH, 8), in0=col(CH, 8), scalar1=-4.0, scalar2=None, op0=ALU.mult))
    C("pool", lambda e: e.memset(bd, 0.0))

    def bdload(e):
        out = []
        for d in range(2):
            for g, src in enumerate((wrg_d, wig_d)):
                for c in range(4):
                    idx = (d * 2 + g) * 4 + c
                    out.append(e.dma_start(out=bd[0:64, idx, 0:64], in_=src[d, 2 * c, :, :]))
                    out.append(e.dma_start(out=bd[64:128, idx, 64:128], in_=src[d, 2 * c + 1, :, :]))
        return out
    P.op("pool", bdload, r=[rc], w=[rc], key="bd", ndma=32)

    P.barrier()
    castw(w_out, w_out_b, D, "w_out_b")
    castgu(w_gate, w_gate_b, "w_gate_b")
    castgu(w_up, w_up_b, "w_up_b")
    castw(w_down, w_down_b, DFF, "w_down_b")

    def rstd_ops(ssq_ap, n, epsn, rr, rw):
        P.op("dve", lambda e: e.tensor_scalar(out=ssq_ap, in0=ssq_ap, scalar1=epsn, scalar2=None, op0=ALU.add), r=rr, w=rw)
        P.op("pool", lambda e: e.tensor_tensor(out=ssq_ap, in0=ssq_ap, in1=col(NHALF).to_broadcast([128, n]), op=ALU.pow), r=rr + [rc], w=rw)

    psB6 = ps[:, 6, :].bitcast(BF16)

    def prep_tile(x_d, t, xt_s, xn_s, junk, ssq_c, dst_ap, rs, gain):
        r_xt, r_xn, r_junk, r_ssq, r_dst = rs
        pB, bB = (psB, bank[7]) if t % 2 == 0 else (psB6, bank[6])
        P.op("sp", lambda e: [e.dma_start(out=xt_s, in_=x_d[t * 128:(t + 1) * 128, :])], w=[r_xt], key=r_xt.name)
        P.op("act", lambda e: e.activation(out=xn_s, in_=xt_s, func=AF.Square, accum_out=ssq_c), r=[r_xt], w=[r_xn, r_ssq])
        P.op("act", lambda e: e.activation(out=ssq_c, in_=ssq_c, func=AF.Sqrt, scale=1.0 / D, bias=col(EPSC)), r=[r_ssq, rc], w=[r_ssq])
        P.op("dve", lambda e: e.reciprocal(out=ssq_c, in_=ssq_c), r=[r_ssq], w=[r_ssq])
        P.op("dve", lambda e: e.scalar_tensor_tensor(out=xn_s, in0=xt_s, scalar=ssq_c, in1=gain, op0=ALU.mult, op1=ALU.mult),
             r=[r_xt, r_ssq, rc], w=[r_xn])

        def tr(e):
            i = None
            for kc in range(8):
                i = e.transpose(out=pB[:, kc * 128:(kc + 1) * 128], in_=xn_s[:, kc * 128:(kc + 1) * 128], identity=ident)
            return i

        def tpart():
            P.op("pe", tr, r=[r_xn, rc], w=[bB])

        def back():
            if t % 2 == 0:
                P.op("act", lambda e: e.activation(out=dst_ap, in_=pB.rearrange("p (a b) -> p a b", a=8), func=AF.Copy), r=[bB], w=[r_dst])
            else:
                P.op("dve", lambda e: e.tensor_copy(out=dst_ap, in_=pB.rearrange("p (a b) -> p a b", a=8)), r=[bB], w=[r_dst])
        return tpart, back

    def mm_group(out_ap, pairs, r, w):
        def fn(e):
            i = None
            n = len(pairs)
            for k, (l, rh) in enumerate(pairs):
                i = e.matmul(out_ap, lhsT=l, rhs=rh, start=(k == 0), stop=(k == n - 1))
            return i
        P.op("pe", fn, r=r, w=w)

    Y0 = CONST_END
    yT, Y1 = V(Y0, [128, 4, SM], BF16)

    def do_seq(si, S):
        x_d, y_d = xs[si], ys[si]
        NT = S // 128
        NB = S // 512

        o = Y1
        xnT, o = V(o, [128, 8, SM], BF16)
        wA, o = V(o, [128, 2, 8, 256], BF16)
        PREP_OFF = o
        xt, o = V(o, [128, 3, D], F32)
        xn, o = V(o, [128, 3, D], BF16)
        junk = None
        bufX, o = V(o, [128, SM + 4], F32)
        xc, o = V(o, [128, SM], F32)
        xcb, o = V(o, [128, SM], BF16)
        tmp, o = V(o, [128, 10, 512], F32)
        tag = "s%dA" % si
        r_xnT = [Res(tag + "xnT%d" % b) for b in range(NB)]
        r_xt = [Res(tag + "xt%d" % i) for i in range(3)]
        r_xn = [Res(tag + "xn%d" % i) for i in range(3)]
        r_junk = Res(tag + "junk")
        r_ssq = [Res(tag + "ssq%d" % i) for i in range(3)]
        r_wA = [Res(tag + "wA%d" % i) for i in range(2)]
        r_bufX, r_xc, r_xcb = Res(tag + "bufX"), Res(tag + "xc"), Res(tag + "xcb")
        r_tmp = [Res(tag + "tmp%d" % i) for i in range(10)]
        r_yT = [res("yT%d" % c) for c in range(4)]
        r_hcar = Res(tag + "hcar")

        pipe = []
        for t in range(NT):
            s = t % 3
            pipe.append(prep_tile(x_d, t, xt[:, s, :], xn[:, s, :], junk, col(TMP + 4 + s),
                                  xnT[:, :, t * 128:(t + 1) * 128], (r_xt[s], r_xn[s], r_junk, r_ssq[s], r_xnT[t // 4]), gmix))
            if t >= 1:
                pipe[t - 1][0]()
            if t >= 2:
                pipe[t - 2][1]()
        pipe[NT - 1][0]()
        if NT >= 2:
            pipe[NT - 2][1]()
        pipe[NT - 1][1]()
        P.op("dve", lambda e: e.memset(bufX[:, 0:2], 0.0), w=[r_bufX])
        set1 = tuple(V(PREP_OFF + k * 4096, [128, 1024], F32)[0] for k in range(3))
        fence = []
        for x_ in r_xt + r_xn + [r_junk]:
            fence += ([x_.w] if x_.w is not None else []) + list(x_.r)
        r_sets = [(Res(tag + "a0"), Res(tag + "m0"), Res(tag + "u0")), (Res(tag + "a1"), Res(tag + "m1"), Res(tag + "u1"))]
        for x_ in r_sets[1]:
            x_.r = list(fence)
        bcount = [0]
        for c in range(4):
            ws = c % 2
            def ldw(e, c=c, ws=ws):
                src = w_in_b.rearrange("(kc p) n -> p kc n", p=128)
                return [e.dma_start(out=wA[:, ws, :, 0:128], in_=src[:, :, 1536 + c * 128:1536 + (c + 1) * 128]),
                        e.dma_start(out=wA[:, ws, :, 128:256], in_=src[:, :, 2048 + c * 128:2048 + (c + 1) * 128])]
            P.op("sp", ldw, r=[res("w_in_b")], w=[r_wA[ws]], key=r_wA[ws].name, ndma=2)
            if c > 0:
                P.op("dve", lambda e: e.memset(bufX[:, 0:2], 0.0), w=[r_bufX])
            P.op("dve", lambda e, S=S: e.memset(bufX[:, S + 2:S + 4], 0.0), w=[r_bufX])
            for b in range(NB):
                pb = bank[b % 2]
                mm_group(ps[:, b % 2, :], [(wA[:, ws, kc, 0:128], xnT[:, kc, b * 512:(b + 1) * 512]) for kc in range(8)],
                         [r_wA[ws], r_xnT[b]], [pb])
                P.op("act", lambda e, b=b: e.activation(out=bufX[:, 2 + b * 512:2 + (b + 1) * 512], in_=ps[:, b % 2, :], func=AF.Copy),
                     r=[pb], w=[r_bufX])
            cw = lambda j, c=c: col(CW + c * 4 + j)
            P.op("dve", lambda e, c=c, S=S, cw=cw: e.tensor_scalar(out=xc[:, 0:S], in0=bufX[:, 0:S], scalar1=cw(0), scalar2=col(CB + c),
                                                               op0=ALU.mult, op1=ALU.add), r=[r_bufX, rc], w=[r_xc])
            for j in range(1, 4):
                P.op("dve", lambda e, j=j, S=S, cw=cw: e.scalar_tensor_tensor(out=xc[:, 0:S], in0=bufX[:, j:j + S], scalar=cw(j), in1=xc[:, 0:S],
                                                                            op0=ALU.mult, op1=ALU.add), r=[r_bufX, r_xc, rc], w=[r_xc])
            P.op("act", lambda e, S=S: e.activation(out=xcb[:, 0:S], in_=xc[:, 0:S], func=AF.Copy), r=[r_xc], w=[r_xcb])
            def gelu_front(b, c=c, ws=ws):
                pb = bank[2 + b % 2]
                pg = ps[:, 2 + b % 2, :]
                blk = slice(b * 512, (b + 1) * 512)
                mm_group(pg, [(wA[:, ws, kc, 128:256], xnT[:, kc, blk]) for kc in range(8)], [r_wA[ws], r_xnT[b]], [pb])
                if b % 2 == 0:
                    t0, t1, rt0, rt1 = tmp[:, 8, :], tmp[:, 9, :], r_tmp[2], r_tmp[3]
                else:
                    t0, t1, rt0, rt1 = tmp[:, 0, :], tmp[:, 2, :], r_sets[0][0], r_sets[0][1]
                P.op("act", lambda e: e.activation(out=t0, in_=pg, func=AF.Square), r=[pb], w=[rt0])
                P.op("dve", lambda e: e.tensor_scalar(out=t0, in0=t0, scalar1=0.044715, scalar2=1.0, op0=ALU.mult, op1=ALU.add),
                     r=[rt0], w=[rt0])
                P.op("dve", lambda e: e.tensor_tensor(out=t0, in0=t0, in1=pg, op=ALU.mult), r=[rt0, pb], w=[rt0])

                def back():
                    P.op("act", lambda e: e.activation(out=t1, in_=t0, func=AF.Tanh, scale=math.sqrt(2.0 / math.pi)),
                         r=[rt0], w=[rt1])
                    P.op("dve", lambda e: e.scalar_tensor_tensor(out=yT[:, c, blk], in0=t1, scalar=1.0, in1=pg,
                                                                op0=ALU.add, op1=ALU.mult), r=[rt1, pb], w=[r_yT[c]])
                return back
            pend = None
            for b in range(NB):
                bk_ = gelu_front(b)
                if pend is not None:
                    pend()
                pend = bk_
            pend()
            sets = [tuple(tmp[:, 2 * k:2 * k + 2, :].rearrange("p a b -> p (a b)") for k in range(3)), set1]
            hb_b = tmp[:, 6:8, :].rearrange("p a b -> p (a b)")
            tr_, ti_ = tmp[:, 8, :], tmp[:, 9, :]
            batches = [list(range(g, min(g + 2, NB))) for g in range(0, NB, 2)]
            items = [(0, n, blks) for n, blks in enumerate(batches)] + [(1, n, blks) for n, blks in enumerate(batches[::-1])]
            hcar = col(TMP + 8)

            def phase12(item, sx, c=c):
                d, n, blks = item
                a_b, m_b, u_b = sets[sx]
                r_a, r_m, r_u = r_sets[sx]
                ir, ii = (d * 2 + 0) * 4 + c, (d * 2 + 1) * 4 + c
                kb = d * 4 + c
                nb = len(blks)
                t0_, t1_ = blks[0] * 512, (blks[-1] + 1) * 512
                L = t1_ - t0_
                for bb, b in enumerate(blks):
                    blk = slice(b * 512, (b + 1) * 512)
                    mm_group(ps[:, 4 + bb, :], [(bd[:, ir, :], xcb[:, blk])], [rc, r_xcb], [bank[4 + bb]])
                    mm_group(ps[:, 2 + bb, :], [(bd[:, ii, :], xcb[:, blk])], [rc, r_xcb], [bank[2 + bb]])
                m3 = m_b[:, 0:L].rearrange("p (a b) -> p a b", a=nb)
                u3 = u_b[:, 0:L].rearrange("p (a b) -> p a b", a=nb)
                P.op("act", lambda e: e.activation(out=m3, in_=ps[:, 4:4 + nb, :], func=AF.Tanh, scale=0.5, bias=col(BR + kb)),
                     r=[bank[4 + bb] for bb in range(nb)] + [rc], w=[r_m])
                P.op("act", lambda e: e.activation(out=a_b[:, 0:L], in_=m_b[:, 0:L], func=AF.Exp, scale=col(CH + kb), bias=col(CH + kb)),
                     r=[r_m, rc], w=[r_a])
                P.op("act", lambda e: e.activation(out=m_b[:, 0:L], in_=m_b[:, 0:L], func=AF.Exp, scale=col(CF + kb), bias=col(CF + kb)),
                     r=[r_m, rc], w=[r_m])
                P.op("act", lambda e: e.activation(out=u3, in_=ps[:, 2:2 + nb, :], func=AF.Tanh, scale=0.5, bias=col(BI + kb)),
                     r=[bank[2 + bb] for bb in range(nb)] + [rc], w=[r_u])
                P.op("dve", lambda e: e.tensor_scalar(out=m_b[:, 0:L], in0=m_b[:, 0:L], scalar1=1.0, scalar2=-1e-12, op0=ALU.subtract, op1=ALU.min),
                     r=[r_m], w=[r_m])
                P.op("dve", lambda e: e.scalar_tensor_tensor(out=u_b[:, 0:L], in0=u_b[:, 0:L], scalar=1.0, in1=xc[:, t0_:t1_], op0=ALU.add, op1=ALU.mult),
                     r=[r_u, r_xc], w=[r_u])
                P.op("act", lambda e: e.activation(out=m_b[:, 0:L], in_=m_b[:, 0:L], func=AF.Sqrt, scale=-1.0), r=[r_m], w=[r_m])

            def phase3(item, sx, c=c):
                d, n, blks = item
                a_b, m_b, u_b = sets[sx]
                r_a, r_m, r_u = r_sets[sx]
                t0_, t1_ = blks[0] * 512, (blks[-1] + 1) * 512
                L = t1_ - t0_
                P.op("dve", lambda e: e.scalar_tensor_tensor(out=u_b[:, 0:L], in0=u_b[:, 0:L], scalar=0.5, in1=m_b[:, 0:L], op0=ALU.mult, op1=ALU.mult),
                     r=[r_u, r_m], w=[r_u])
                if d == 0:
                    init = 0.0 if n == 0 else bufX[:, t0_ - 1:t0_]
                    P.op("dve", lambda e: e.tensor_tensor_scan(out=bufX[:, t0_:t1_], data0=a_b[:, 0:L], data1=u_b[:, 0:L], initial=init,
                                                               op0=ALU.mult, op1=ALU.add), r=[r_a, r_u, r_bufX], w=[r_bufX])
                else:
                    init = 0.0 if n == 0 else hcar
                    hx = n % 2
                    hb_b = hbs[hx]
                    r_hb = r_hbs[hx]
                    P.op("dve", lambda e: e.tensor_tensor_scan(out=hb_b[:, 0:L][:, ::-1], data0=a_b[:, 0:L][:, ::-1], data1=u_b[:, 0:L][:, ::-1],
                                                               initial=init, op0=ALU.mult, op1=ALU.add), r=[r_a, r_u, r_hcar], w=[r_hb])
                    P.op("dve", lambda e: e.tensor_copy(out=hcar, in_=hb_b[:, 0:1]), r=[r_hb], w=[r_hcar])
                    P.op("pool", lambda e: e.tensor_tensor(out=hb_b[:, 0:L], in0=hb_b[:, 0:L], in1=bufX[:, t0_:t1_], op=ALU.add),
                         r=[r_hb, r_bufX, r_hcar], w=[r_hb])
                    for fn_ in pend_y:
                        fn_()
                    pend_y[:] = [lambda: P.op("dve", lambda e: e.scalar_tensor_tensor(out=yT[:, c, t0_:t1_], in0=hb_b[:, 0:L], scalar=0.5, in1=yT[:, c, t0_:t1_],
                                                                                     op0=ALU.mult, op1=ALU.mult), r=[r_hb, r_yT[c]], w=[r_yT[c]])]

            hbs = [tmp[:, 6:8, :].rearrange("p a b -> p (a b)"), tmp[:, 8:10, :].rearrange("p a b -> p (a b)")]
            r_hbs = [r_tmp[8], r_tmp[2]]
            pend_y = []
            sx0 = bcount[0]
            bcount[0] += len(items)
            phase12(items[0], sx0 % 2)
            for i_, item in enumerate(items):
                if i_ + 1 < len(items):
                    phase12(items[i_ + 1], (sx0 + i_ + 1) % 2)
                phase3(item, (sx0 + i_) % 2)
            for fn_ in pend_y:
                fn_()
        P.barrier()

        o = Y1
        qT, o = V(o, [128, NH, SM], BF16)
        kT, o = V(o, [128, NH, SM], BF16)
        Va, o = V(o, [128, NTM, NH, 130], BF16)
        XR = o
        wB, o = V(o, [128, 8, 1536], BF16)
        xt, o = V(o, [128, 2, D], F32)
        xn, o = V(o, [128, 2, D], BF16)
        junk, o = V(o, [128, D], BF16)
        xnb, o = V(o, [128, 2, 8, 512], BF16)
        tag = "s%dB" % si
        r_wB = Res(tag + "wB")
        r_xt = [Res(tag + "xt%d" % i) for i in range(2)]
        r_xn = [Res(tag + "xn%d" % i) for i in range(2)]
        r_junk = Res(tag + "junk")
        r_ssq = [Res(tag + "ssq%d" % i) for i in range(2)]
        r_xnb = [Res(tag + "xnb%d" % i) for i in range(2)]
        r_q, r_k, r_v = Res(tag + "q"), Res(tag + "k"), Res(tag + "v")

        def ldwB(e):
            src = w_in_b.rearrange("(kc p) n -> p kc n", p=128)
            return [e.dma_start(out=wB[:, :, i * 512:(i + 1) * 512], in_=src[:, :, i * 512:(i + 1) * 512]) for i in range(3)]
        P.op("sp", ldwB, r=[res("w_in_b")], w=[r_wB], key=r_wB.name, ndma=3)
        P.op("pool", lambda e, NT=NT: e.memset(Va[:, 0:NT, :, 128:129], 1.0), w=[r_v])
        def b1_A(b, tts):
            bs = b % 2
            out = []
            for tt in tts:
                t = b * 4 + tt
                s = t % 2
                out.append(prep_tile(x_d, t, xt[:, s, :], xn[:, s, :], junk, col(TMP + 4 + s),
                                     xnb[:, bs, :, tt * 128:(tt + 1) * 128], (r_xt[s], r_xn[s], r_junk, r_ssq[s], r_xnb[bs]), gmix))
            return out

        def b1_TC(parts):
            for tp_, _ in parts:
                tp_()
            for _, bk_ in parts:
                bk_()

        def b1_groups(b):
            bs = b % 2
            blk = slice(b * 512, (b + 1) * 512)
            gl = []
            k = 0
            for h in range(NH):
                for which, dst, rr, scale in ((0, qT, r_q, 0.125), (1, kT, r_k, 1.0)):
                    pbi = k % 4
                    k += 1

                    def g(pbi=pbi, which=which, dst=dst, rr=rr, scale=scale, h=h):
                        c0 = which * 512 + h * 128
                        mm_group(ps[:, pbi, :], [(wB[:, kc, c0:c0 + 128], xnb[:, bs, kc, :]) for kc in range(8)], [r_wB, r_xnb[bs]], [bank[pbi]])
                        if which == 0:
                            P.op("act", lambda e: e.activation(out=dst[:, h, blk], in_=ps[:, pbi, :], func=AF.Copy, scale=scale), r=[bank[pbi]], w=[rr])
                        else:
                            P.op("dve", lambda e: e.tensor_copy(out=dst[:, h, blk], in_=ps[:, pbi, :]), r=[bank[pbi]], w=[rr])
                    gl.append(g)
            for tt in range(4):
                def g(tt=tt):
                    t = b * 4 + tt
                    pbi = 4 + tt % 2
                    mm_group(ps[:, pbi, :], [(xnb[:, bs, kc, tt * 128:(tt + 1) * 128], wB[:, kc, 1024:1536]) for kc in range(8)], [r_wB, r_xnb[bs]], [bank[pbi]])
                    P.op("dve", lambda e: e.tensor_copy(out=Va[:, t, :, 0:128], in_=ps[:, pbi, :].rearrange("p (h d) -> p h d", h=NH)),
                         r=[bank[pbi]], w=[r_v])
                gl.append(g)
            return gl

        p01 = b1_A(0, [0, 1])
        b1_TC(p01)
        p23 = b1_A(0, [2, 3])
        b1_TC(p23)
        for b in range(NB):
            gl = b1_groups(b)
            nxt = b + 1 < NB
            if nxt:
                p01 = b1_A(b + 1, [0, 1])
            for g in gl[:6]:
                g()
            if nxt:
                b1_TC(p01)
                p23 = b1_A(b + 1, [2, 3])
            for g in gl[6:]:
                g()
            if nxt:
                b1_TC(p23)
        P.barrier()

        o = XR
        oT, o = V(o, [128, NH, SM], BF16)
        OT_END = o
        Pb, o = V(o, [128, 3, 2, 512], BF16)
        Ssb, o = V(o, [128, 2, 512], F32)
        acc, o = V(o, [128, 8, 130], F32)
        ot, o = V(o, [128, 2, 4, 128], F32)
        onb, o = V(o, [128, 2, 4, 128], BF16)
        rl, o = V(o, [128, 32], F32)
        junkf, o = V(o, [128, 128], F32)
        tag = "s%dC" % si
        r_P = [Res(tag + "P%d" % i) for i in range(3)]
        r_Ssb = [Res(tag + "S%d" % i) for i in range(2)]
        r_accs = [Res(tag + "acc%d" % i) for i in range(8)]
        r_junkb = Res(tag + "jb")
        r_ot = [Res(tag + "ot%d" % i) for i in range(2)]
        r_on = [Res(tag + "on%d" % i) for i in range(2)]
        r_rl = [Res(tag + "rl%d" % i) for i in range(2)]
        r_oT = [res("oT%d" % h) for h in range(NH)]
        psO = ps[:, 4:7, :].rearrange("p a b -> p (a b)")

        def slot_ap(sl):
            bk, j = divmod(sl, 3)
            return ps[:, 4 + bk, j * 130:j * 130 + 129]
        iters = []
        for qb in range(NB):
            jd0 = 4 * qb
            for h in range(NH):
                phases = [(pn, ch) for pn, ch in (("L", list(range(0, jd0))), ("D", list(range(jd0, jd0 + 4))), ("R", list(range(jd0 + 4, NT)))) if ch]
                for pi_, (pn, ch) in enumerate(phases):
                    for idx, j in enumerate(ch):
                        iters.append(dict(qb=qb, h=h, pn=pn, idx=idx, n=len(ch), j=j, first_phase=(pi_ == 0),
                                          last_phase=(pi_ == len(phases) - 1)))
        NI = len(iters)

        def emit_qk_exp(i):
            it = iters[i]
            qb, h, j, pn = it["qb"], it["h"], it["j"], it["pn"]
            sbi, pbi = i % 2, i % 3
            jd0 = 4 * qb
            qblk = slice(qb * 512, (qb + 1) * 512)
            kblk = slice(j * 128, (j + 1) * 128)

            def qk(e):
                e.matmul(ps[:, 2 * sbi, :], lhsT=kT[0:64, h, kblk], rhs=qT[0:64, h, qblk], start=True, stop=True)
                return e.matmul(ps[:, 2 * sbi + 1, :], lhsT=kT[64:128, h, kblk], rhs=qT[64:128, h, qblk], start=True, stop=True)
            P.op("pe", qk, r=[r_q, r_k], w=[bank[2 * sbi], bank[2 * sbi + 1]])
            src2 = ps[:, 2 * sbi:2 * sbi + 2, :]
            if pn == "D":
                off = 384 - 128 * (j - jd0)
                P.op("dve", lambda e, off=off: e.scalar_tensor_tensor(out=src2, in0=Dt[:, off:off + 512].unsqueeze(1).to_broadcast([128, 2, 512]), scalar=-SLOPES[h],
                                                                    in1=src2, op0=ALU.mult, op1=ALU.add),
                     r=[bank[2 * sbi], bank[2 * sbi + 1], rc], w=[bank[2 * sbi], bank[2 * sbi + 1]])
                P.op("act", lambda e: e.activation(out=Pb[:, pbi, :, :], in_=src2, func=AF.Exp), r=[bank[2 * sbi], bank[2 * sbi + 1]], w=[r_P[pbi]])
            else:
                bias = biasL[:, h, jd0 - j:jd0 - j + 1] if pn == "L" else biasR[:, h, j - jd0 - 4:j - jd0 - 3]
                P.op("act", lambda e, bias=bias: e.activation(out=Pb[:, pbi, :, :], in_=src2, func=AF.Exp, bias=bias),
                     r=[bank[2 * sbi], bank[2 * sbi + 1], rc], w=[r_P[pbi]])

        deferred = []

        def flush(parity=None, tick=False):
            keep = []
            for ent in deferred:
                if tick:
                    ent[0] -= 1
                if ent[0] <= 0 or (parity is not None and ent[1] == parity) or (parity == -1):
                    ent[2]()
                else:
                    keep.append(ent)
            deferred[:] = keep

        fin_count = [0]

        def emit_pv(i):
            it = iters[i]
            qb, h, j, pn, idx, n = it["qb"], it["h"], it["j"], it["pn"], it["idx"], it["n"]
            pbi = i % 3
            qblk = slice(qb * 512, (qb + 1) * 512)
            for bk in range(3):
                sls = [sl for sl in range(8) if sl // 3 == bk]

                def pv(e, sls=sls):
                    ins = None
                    for sl in sls:
                        m, qs = divmod(sl, 4)
                        ins = e.matmul(slot_ap(sl), lhsT=Pb[:, pbi, m, qs * 128:(qs + 1) * 128], rhs=Va[:, j, h, 0:129],
                                       start=(idx == 0 and sl % 3 == 0), stop=(idx == n - 1), skip_group_check=True)
                    return ins
                P.op("pe", pv, r=[r_P[pbi], r_v], w=[bank[4 + bk]])
                if idx == n - 1:
                    fp = it["first_phase"]
                    nsl = len(sls)
                    pview = ps[:, 4 + bk, 0:nsl * 130].rearrange("p (s c) -> p s c", c=130)[:, :, 0:129]
                    aview = acc[:, sls[0]:sls[0] + nsl, 0:129]
                    racc = [r_accs[sl] for sl in sls]
                    if pn == "D":
                        if fp:
                            P.op("dve", lambda e, pview=pview, aview=aview: e.tensor_copy(out=aview, in_=pview), r=[bank[4 + bk]], w=racc)
                        else:
                            P.op("dve", lambda e, pview=pview, aview=aview: e.tensor_tensor(out=aview, in0=aview, in1=pview, op=ALU.add),
                                 r=[bank[4 + bk]] + racc, w=racc)
                    elif fp:
                        ftab = col((F8L if pn == "L" else F8R) + h * 8 + sls[0], nsl).unsqueeze(2).to_broadcast([128, nsl, 129])
                        P.op("dve", lambda e, pview=pview, aview=aview, ftab=ftab: e.tensor_tensor(out=aview, in0=pview, in1=ftab, op=ALU.mult),
                             r=[bank[4 + bk], rc], w=racc)
                    else:
                        for sl in sls:
                            f = col((FL if pn == "L" else FR) + h * 4 + sl % 4)
                            P.op("dve", lambda e, sl=sl, f=f: e.scalar_tensor_tensor(out=acc[:, sl, 0:129], in0=slot_ap(sl), scalar=f, in1=acc[:, sl, 0:129],
                                                                                    op0=ALU.mult, op1=ALU.add), r=[bank[4 + bk], r_accs[sl], rc], w=[r_accs[sl]])
            if idx == n - 1 and it["last_phase"]:
                par = fin_count[0] % 2
                fin_count[0] += 1
                flush(parity=par)
                otp, onp, rlp = ot[:, par], onb[:, par], rl[:, par * 16:(par + 1) * 16]
                P.op("dve", lambda e: e.reciprocal(out=rlp[:, 0:8], in_=acc[:, :, 128]), r=r_accs, w=[r_rl[par]])
                P.op("dve", lambda e: e.tensor_scalar(out=rlp[:, 4:8], in0=rlp[:, 4:8], scalar1=col(NLAM), scalar2=None, op0=ALU.mult), r=[r_rl[par], rc], w=[r_rl[par]])
                for qs in range(4):
                    P.op("dve", lambda e, qs=qs: e.tensor_scalar(out=otp[:, qs, :], in0=acc[:, 4 + qs, 0:128], scalar1=rlp[:, 4 + qs:5 + qs], scalar2=None, op0=ALU.mult),
                         r=[r_accs[4 + qs], r_rl[par]], w=[r_ot[par]])
                    P.op("dve", lambda e, qs=qs: e.scalar_tensor_tensor(out=otp[:, qs, :], in0=acc[:, qs, 0:128], scalar=rlp[:, qs:qs + 1], in1=otp[:, qs, :],
                                                                       op0=ALU.mult, op1=ALU.add), r=[r_accs[qs], r_rl[par], r_ot[par]], w=[r_ot[par]])
                for qs in range(4):
                    P.op("dve", lambda e, qs=qs: e.scalar_tensor_tensor(out=junkf, in0=otp[:, qs, :], scalar=1.0, in1=otp[:, qs, :], op0=ALU.mult, op1=ALU.mult,
                                                                       accum_out=rlp[:, 8 + qs:9 + qs]), r=[r_ot[par]], w=[r_junkb, r_rl[par]])
                rstd_ops(rlp[:, 8:12], 4, 128.0 * EPS, [r_rl[par]], [r_rl[par]])
                for qs in range(4):
                    P.op("dve", lambda e, qs=qs: e.scalar_tensor_tensor(out=onp[:, qs, :], in0=otp[:, qs, :], scalar=rlp[:, 8 + qs:9 + qs], in1=gsub,
                                                                       op0=ALU.mult, op1=ALU.mult), r=[r_ot[par], r_rl[par], rc], w=[r_on[par]])

                def fin(par=par, h=h, qblk=qblk, onp=onp):
                    def tro(e):
                        ins = None
                        for qs in range(4):
                            ins = e.transpose(out=psB[:, qs * 128:(qs + 1) * 128], in_=onp[:, qs, :], identity=ident)
                        return ins
                    P.op("pe", tro, r=[r_on[par], rc], w=[bank[7]])
                    P.op("act", lambda e: e.activation(out=oT[:, h, qblk], in_=psB[:, 0:512], func=AF.Copy), r=[bank[7]], w=[r_oT[h]])
                deferred.append([10, par, fin])

        emit_qk_exp(0)
        for i in range(NI):
            if i + 1 < NI:
                emit_qk_exp(i + 1)
            emit_pv(i)
            flush(tick=True)
        flush(parity=-1)
        P.barrier()

        o = Y1
        wo, o = V(o, [128, 8, D], BF16)
        wd, o = V(o, [128, 11, D], BF16)
        hT, o = V(o, [128, 2, 11, 512], BF16)
        x1, o = V(o, [128, 4, D], F32)
        h2T, o = V(o, [128, 8, 512], BF16)
        wgu, o = V(o, [128, 2, 2, 8, 128], BF16)
        assert o <= XR, (o, XR)
        o = OT_END
        xt, o = V(o, [128, 2, D], F32)
        h2n, o = V(o, [128, 2, D], BF16)
        tw, o = V(o, [128, 4, 512], F32)
        tag = "s%dD" % si
        r_wo, r_wd = Res(tag + "wo"), Res(tag + "wd")
        r_hT = [Res(tag + "hT%d" % i) for i in range(2)]
        r_x1 = [Res(tag + "x1_%d" % i) for i in range(4)]
        r_h2T = Res(tag + "h2T")
        r_wgu = [Res(tag + "wgu%d" % i) for i in range(2)]
        r_xt = [Res(tag + "xt%d" % i) for i in range(2)]
        r_h2n, r_junk = Res(tag + "h2n"), Res(tag + "junk")
        r_ssq = [Res(tag + "ssq%d" % i) for i in range(2)]
        r_tw = [Res(tag + "tw%d" % i) for i in range(4)]

        P.op("sp", lambda e: [e.dma_start(out=wo, in_=w_out_b.rearrange("(kc p) n -> p kc n", p=128))], r=[res("w_out_b")], w=[r_wo], key=r_wo.name)
        gcount = 0
        h2n2 = h2n
        sqo = tw[:, 0:2, :].rearrange("p a b -> p (a b)")
        r_h2ns = [Res(tag + "h2n%d" % i) for i in range(2)]
        r_ssq4 = [Res(tag + "ssq4_%d" % i) for i in range(4)]

        def c_front(b, tt):
            t = b * 4 + tt
            s = t % 2
            tok = slice(t * 128, (t + 1) * 128)
            P.op("sp", lambda e: [e.dma_start(out=xt[:, s, :], in_=x_d[tok, :])], w=[r_xt[s]], key=r_xt[s].name)
            for hf in range(2):
                cs = slice(hf * 512, (hf + 1) * 512)
                pairs = [((oT[:, kc, tok] if kc < 4 else yT[:, kc - 4, tok]), wo[:, kc, cs]) for kc in range(8)]
                mm_group(ps[:, hf, :], pairs, r_oT + r_yT + [r_wo], [bank[hf]])
                P.op("dve", lambda e, hf=hf, cs=cs: e.tensor_tensor(out=x1[:, tt, cs], in0=ps[:, hf, :], in1=xt[:, s, cs], op=ALU.add),
                     r=[bank[hf], r_xt[s]], w=[r_x1[tt]])
            ssq_c = col(TMP + 10 + tt)
            P.op("act", lambda e: e.activation(out=sqo, in_=x1[:, tt, :], func=AF.Square, accum_out=ssq_c), r=[r_x1[tt]], w=[r_tw[0], r_tw[1], r_ssq4[tt]])
            rstd_ops(ssq_c, 1, D * EPS, [r_ssq4[tt]], [r_ssq4[tt]])
            P.op("dve", lambda e: e.scalar_tensor_tensor(out=h2n2[:, tt % 2, :], in0=x1[:, tt, :], scalar=ssq_c, in1=gffn, op0=ALU.mult, op1=ALU.mult),
                 r=[r_x1[tt], r_ssq4[tt], rc], w=[r_h2ns[tt % 2]])

        def c_back(b, tt):
            def tr2(e):
                i = None
                for kc in range(8):
                    i = e.transpose(out=psB[:, kc * 128:(kc + 1) * 128], in_=h2n2[:, tt % 2, kc * 128:(kc + 1) * 128], identity=ident)
                return i
            P.op("pe", tr2, r=[r_h2ns[tt % 2], rc], w=[bank[7]])
            P.op("act", lambda e: e.activation(out=h2T[:, :, tt * 128:(tt + 1) * 128], in_=psB.rearrange("p (a b) -> p a b", a=8), func=AF.Copy),
                 r=[bank[7]], w=[r_h2T])

        def c_gu(fc):
            nonlocal_g = gstate
            gs = nonlocal_g[0] % 2
            nonlocal_g[0] += 1
            part, fi = divmod(fc, 11)

            def ldgu(e):
                return [e.dma_start(out=wgu[:, gs, 0, :, :], in_=w_gate_b[fc]),
                        e.dma_start(out=wgu[:, gs, 1, :, :], in_=w_up_b[fc])]
            P.op("sp", ldgu, r=[res("w_gate_b"), res("w_up_b")], w=[r_wgu[gs]], key=r_wgu[gs].name, ndma=2)
            bg, bu = 2 + 2 * gs, 3 + 2 * gs
            mm_group(ps[:, bg, :], [(wgu[:, gs, 0, kc, :], h2T[:, kc, :]) for kc in range(8)], [r_wgu[gs], r_h2T], [bank[bg]])
            mm_group(ps[:, bu, :], [(wgu[:, gs, 1, kc, :], h2T[:, kc, :]) for kc in range(8)], [r_wgu[gs], r_h2T], [bank[bu]])
            th, ww = tw[:, 2 * gs, :], tw[:, 2 * gs + 1, :]
            P.op("act", lambda e: e.activation(out=th, in_=ps[:, bg, :], func=AF.Tanh, scale=0.5), r=[bank[bg]], w=[r_tw[2 * gs]])
            P.op("dve", lambda e: e.scalar_tensor_tensor(out=ww, in0=th, scalar=1.0, in1=ps[:, bg, :], op0=ALU.add, op1=ALU.mult),
                 r=[r_tw[2 * gs], bank[bg]], w=[r_tw[2 * gs + 1]])
            P.op("dve", lambda e: e.scalar_tensor_tensor(out=hT[:, part, fi, :], in0=ww, scalar=0.5, in1=ps[:, bu, :],
                                                        op0=ALU.mult, op1=ALU.mult), r=[r_tw[2 * gs + 1], bank[bu]], w=[r_hT[part]])

        def c_wd(part):
            P.op("pool", lambda e: [e.dma_start(out=wd, in_=w_down_b[part * 1408:(part + 1) * 1408, :].rearrange("(fc p) n -> p fc n", p=128))],
                 r=[res("w_down_b")], w=[r_wd], key=r_wd.name)

        def c_down(part, after_tile=None):
            for tt in range(4):
                if after_tile is not None and tt >= 1:
                    after_tile(tt - 1)
                for hf in range(2):
                    cs = slice(hf * 512, (hf + 1) * 512)
                    mm_group(ps[:, hf, :], [(hT[:, part, fi, tt * 128:(tt + 1) * 128], wd[:, fi, cs]) for fi in range(11)], [r_hT[part], r_wd], [bank[hf]])
                    P.op("dve", lambda e, hf=hf, tt=tt, cs=cs: e.tensor_tensor(out=x1[:, tt, cs], in0=x1[:, tt, cs], in1=ps[:, hf, :], op=ALU.add),
                         r=[bank[hf], r_x1[tt]], w=[r_x1[tt]])

        def c_final(b, tt):
            t = b * 4 + tt
            tok = slice(t * 128, (t + 1) * 128)
            ssq_c = col(TMP + 14 + tt)
            P.op("act", lambda e: e.activation(out=sqo, in_=x1[:, tt, :], func=AF.Square, accum_out=ssq_c), r=[r_x1[tt]], w=[r_tw[0], r_tw[1], r_ssq4[tt]])
            rstd_ops(ssq_c, 1, D * EPS, [r_ssq4[tt]], [r_ssq4[tt]])
            P.op("dve", lambda e: e.scalar_tensor_tensor(out=x1[:, tt, :], in0=x1[:, tt, :], scalar=ssq_c, in1=gfin, op0=ALU.mult, op1=ALU.mult),
                 r=[r_x1[tt], r_ssq4[tt], rc], w=[r_x1[tt]])
            P.op("pool", lambda e: [e.dma_start(out=y_d[tok, :], in_=x1[:, tt, :])], r=[r_x1[tt]], key="st" + r_x1[tt].name)

        gstate = [0]
        for b in range(NB):
            for tt in range(4):
                c_front(b, tt)
                if tt >= 1:
                    c_back(b, tt - 1)
            c_back(b, 3)
            c_wd(0)
            for fc in range(12):
                c_gu(fc)
            c_down(0)
            c_wd(1)
            for fc in range(12, 22):
                c_gu(fc)
            c_down(1, after_tile=lambda tt_, b=b: c_final(b, tt_))
            c_final(b, 3)
        P.barrier()

    for si_, S_ in enumerate(seqs):
        do_seq(si_, S_)
    P.emit(nc)
    es.close()
    return nc


_NC_CACHE = {}


def _common_inputs(inp):
    f = lambda a: np.ascontiguousarray(np.asarray(a, dtype=np.float32))
    pc = lambda a: f(np.asarray(a).reshape(4, 128).T)
    pdc = lambda a: f(np.asarray(a).reshape(2, 4, 128).transpose(2, 0, 1).reshape(128, 8))
    return {
        "w_in": f(inp["w_in"][0]), "w_out": f(inp["w_out"][0]), "w_gate": f(inp["w_gate"][0]),
        "w_up": f(inp["w_up"][0]), "w_down": f(inp["w_down"][0]),
        "vec1024": f(np.stack([inp["norm_mix"][0], inp["norm_ffn"][0], inp["norm_final"]])),
        "convw": f(np.asarray(inp["conv_w"][0]).T.reshape(4, 128, 4).transpose(1, 0, 2).reshape(128, 16)),
        "convb": pc(inp["conv_b"][0]),
        "brg": pdc(inp["b_rg"][0]), "big": pdc(inp["b_ig"][0]), "lrul": pdc(inp["lru_lambda"][0]),
        "wrg": f(inp["w_rg"][0]), "wig": f(inp["w_ig"][0]),
        "lamv": f(np.stack([inp["lambda_q1"][0], inp["lambda_k1"][0], inp["lambda_q2"][0], inp["lambda_k2"][0]])),
        "subg": f(np.asarray(inp["subln_g"][0]).reshape(1, 128)),
    }


def kernel(**inp):
    xp = np.asarray(inp["x_prompt"], dtype=np.float32)
    xsm = np.asarray(inp["x_sample"], dtype=np.float32)
    seqs = (xp.shape[1], xp.shape[1], xsm.shape[1])
    if seqs not in _NC_CACHE:
        _NC_CACHE[seqs] = _build(list(seqs))
    nc = _NC_CACHE[seqs]
    common = _common_inputs(inp)
    in_maps = []
    for c in range(8):
        m = dict(common)
        m["x0"] = np.ascontiguousarray(xp[2 * c])
        m["x1"] = np.ascontiguousarray(xp[2 * c + 1])
        m["x2"] = np.ascontiguousarray(xsm[c])
        in_maps.append(m)
    res = run_bass_kernel_spmd(nc, in_maps, core_ids=list(range(8)))
    yp = np.empty_like(xp)
    ysm = np.empty_like(xsm)
    for c in range(8):
        r = res.results[c]
        yp[2 * c] = r["y0"]
        yp[2 * c + 1] = r["y1"]
        ysm[c] = r["y2"]
    return yp, ysm
```
